# Optimizing a Trainium2 kernel written in Bass

```python
import math
import jax, jax.numpy as jnp
from jax import lax
import numpy as np

D_MODEL = 1024
BATCH = 8
SEQ = 2048
DEPTH = 4
DEC_BATCH = 128
DEC_SEQ = 8
PAST_LEN = 16384
PAGE_SIZE = 128

N_MIXERS = 4
ALPHA = (2 * DEPTH) ** 0.25
BETA = (8 * DEPTH) ** -0.25
LN_EPS = 1e-5
RMS_EPS = 1e-5

S5_WIDTH = D_MODEL
S5_GROUP = 16
S5_GROUPS = S5_WIDTH // S5_GROUP
S5_STATE = 64

POOL_WIDTH = D_MODEL
POOL_WINDOWS = (2, 4, 8, 16)
POOL_GROUP = POOL_WIDTH // len(POOL_WINDOWS)
POOL_BUF = max(POOL_WINDOWS) - 1

CMLP_WIDTH = D_MODEL
CMLP_CHUNK = 128
CMLP_HEADS = 4
CMLP_HEAD_DIM = CMLP_WIDTH // CMLP_HEADS

SSD_INNER = 2 * D_MODEL
SSD_HEAD_DIM = 64
SSD_HEADS = SSD_INNER // SSD_HEAD_DIM
SSD_STATE = 128
SSD_GROUPS = 4
SSD_CONV = 4
SSD_CHUNK = 128
SSD_CONV_DIM = SSD_INNER + 2 * SSD_GROUPS * SSD_STATE
SSD_PROJ = SSD_INNER + SSD_CONV_DIM + SSD_HEADS

PEER_HEADS = 8
PEER_NKEYS = 128
PEER_EXPERTS = PEER_NKEYS * PEER_NKEYS
PEER_TOPK = 16
PEER_QDIM = 256
PEER_HALF = PEER_QDIM // 2
PEER_BLOCK = 128

kernel_name = 'hybrid_s5_pool_gmlp_ssd_peer_step'


def layer_norm(x, g, b):
    xf = x.astype(jnp.float32)
    mu = jnp.mean(xf, axis=-1, keepdims=True)
    var = jnp.mean(jnp.square(xf - mu), axis=-1, keepdims=True)
    y = (xf - mu) * lax.rsqrt(var + LN_EPS) * g.astype(jnp.float32) + b.astype(jnp.float32)
    return y.astype(x.dtype)


def _cmul(ar, ai, br, bi):
    return ar * br - ai * bi, ar * bi + ai * br


def _s5_combine(e1, e2):
    a1r, a1i, b1r, b1i = e1
    a2r, a2i, b2r, b2i = e2
    ar, ai = _cmul(a2r, a2i, a1r, a1i)
    br, bi = _cmul(a2r, a2i, b1r, b1i)
    return ar, ai, br + b2r, bi + b2i


def s5_mixer(x, h0_re, h0_im, w_in, a_re, a_im, log_dt, b_re, b_im, c_re, c_im, d_skip,
             w_glu, b_glu, w_out):
    f32 = jnp.float32
    bsz, L, _ = x.shape
    u = (x @ w_in).astype(f32)
    ug = u.reshape(bsz, L, S5_GROUPS, S5_GROUP)
    dt = jnp.exp(log_dt.astype(f32))[:, None]
    lam_r, lam_i = a_re.astype(f32), a_im.astype(f32)
    mag = jnp.exp(lam_r * dt)
    lb_r, lb_i = mag * jnp.cos(lam_i * dt), mag * jnp.sin(lam_i * dt)
    den = lam_r * lam_r + lam_i * lam_i
    f_r = ((lb_r - 1.0) * lam_r + lb_i * lam_i) / den
    f_i = (lb_i * lam_r - (lb_r - 1.0) * lam_i) / den
    bb_r, bb_i = _cmul(f_r[..., None], f_i[..., None], b_re.astype(f32), b_im.astype(f32))
    bu_r = jnp.einsum('blgi,gpi->blgp', ug, bb_r)
    bu_i = jnp.einsum('blgi,gpi->blgp', ug, bb_i)
    h0r, h0i = h0_re.astype(f32), h0_im.astype(f32)
    bu_r = bu_r.at[:, 0].add(lb_r * h0r - lb_i * h0i)
    bu_i = bu_i.at[:, 0].add(lb_r * h0i + lb_i * h0r)
    a_r = jnp.broadcast_to(lb_r, bu_r.shape)
    a_i = jnp.broadcast_to(lb_i, bu_i.shape)
    _, _, h_r, h_i = lax.associative_scan(_s5_combine, (a_r, a_i, bu_r, bu_i), axis=1)
    y = (jnp.einsum('blgp,gip->blgi', h_r, c_re.astype(f32))
         - jnp.einsum('blgp,gip->blgi', h_i, c_im.astype(f32)))
    y = y.reshape(bsz, L, S5_WIDTH) + d_skip.astype(f32) * u
    g = jax.nn.gelu(y).astype(x.dtype)
    out = g * jax.nn.sigmoid(g @ w_glu + b_glu)
    return out @ w_out, h_r[:, -1], h_i[:, -1]


def pool_mixer(x, buf, pos0, w_in, w_grp, scale, w_out):
    f32 = jnp.float32
    bsz, L, _ = x.shape
    u = x @ w_in
    ctx = jnp.concatenate([buf.astype(u.dtype), u], axis=1)
    cs = jnp.pad(jnp.cumsum(ctx.astype(f32), axis=1), ((0, 0), (1, 0), (0, 0)))
    pos = pos0 + jnp.arange(L)
    means = []
    for gi, w in enumerate(POOL_WINDOWS):
        c0, c1 = gi * POOL_GROUP, (gi + 1) * POOL_GROUP
        win_sum = (cs[:, POOL_BUF + 1:POOL_BUF + 1 + L, c0:c1]
                   - cs[:, POOL_BUF + 1 - w:POOL_BUF + 1 - w + L, c0:c1])
        cnt = jnp.minimum(pos + 1, w).astype(f32)[None, :, None]
        means.append(win_sum / cnt)
    pooled = (jnp.concatenate(means, axis=-1) - u.astype(f32)).reshape(
        bsz, L, len(POOL_WINDOWS), POOL_GROUP)
    mixed = jnp.einsum('blgc,gcd->blgd', pooled, w_grp.astype(f32)).reshape(bsz, L, POOL_WIDTH)
    out = (mixed * scale.astype(f32)).astype(x.dtype)
    return out @ w_out, ctx[:, -POOL_BUF:]


def chunk_mlp_mixer(x, w_in, b_in, ln_g, ln_b, w_s, b_s, w_out):
    bsz, L, _ = x.shape
    z = jax.nn.gelu(x @ w_in + b_in)
    u, v = z[..., :CMLP_WIDTH], z[..., CMLP_WIDTH:]
    v = layer_norm(v, ln_g, ln_b)
    q = min(L, CMLP_CHUNK)
    n_chunks = L // q
    causal = jnp.tril(jnp.ones((q, q), dtype=bool))
    ws = jnp.where(causal[None], w_s[:, :q, :q], 0.0).astype(v.dtype)
    vc = v.reshape(bsz, n_chunks, q, CMLP_HEADS, CMLP_HEAD_DIM)
    mixed = (jnp.einsum('hts,bcshd->bcthd', ws, vc)
             + jnp.transpose(b_s[:, :q])[None, None, :, :, None])
    out = u * mixed.reshape(bsz, L, CMLP_WIDTH)
    return out @ w_out, v


def ssd_scan(xs, dt, a, bm, cm, h0, q):
    f32 = jnp.float32
    bsz, L, _, _ = xs.shape
    nc = L // q
    hpg = SSD_HEADS // SSD_GROUPS
    x = xs.astype(f32).reshape(bsz, nc, q, SSD_GROUPS, hpg, SSD_HEAD_DIM)
    dtc = dt.reshape(bsz, nc, q, SSD_GROUPS, hpg)
    bc = bm.astype(f32).reshape(bsz, nc, q, SSD_GROUPS, SSD_STATE)
    cc = cm.astype(f32).reshape(bsz, nc, q, SSD_GROUPS, SSD_STATE)
    a_cs = jnp.cumsum(dtc * a.reshape(SSD_GROUPS, hpg), axis=2)
    causal = jnp.tril(jnp.ones((q, q), dtype=bool))[:, :, None, None]
    seg = a_cs[:, :, :, None] - a_cs[:, :, None, :]
    decay = jnp.exp(jnp.where(causal, seg, -jnp.inf))
    xdt = x * dtc[..., None]
    cb = jnp.einsum('bctgn,bcsgn->bctsg', cc, bc)
    y_diag = jnp.einsum('bctsg,bctsgj,bcsgjp->bctgjp', cb, decay, xdt)
    decay_end = jnp.exp(a_cs[:, :, -1:] - a_cs)
    states = jnp.einsum('bcsgn,bcsgj,bcsgjp->bcgjpn', bc, decay_end, xdt)
    chunk_decay = jnp.exp(a_cs[:, :, -1])

    def step(h, inp):
        st, dec = inp
        return h * dec[..., None, None] + st, h

    h_init = h0.astype(f32).reshape(bsz, SSD_GROUPS, hpg, SSD_HEAD_DIM, SSD_STATE)
    h_last, h_prev = lax.scan(step, h_init,
                              (jnp.moveaxis(states, 1, 0), jnp.moveaxis(chunk_decay, 1, 0)))
    h_prev = jnp.moveaxis(h_prev, 0, 1)
    y_off = jnp.einsum('bctgn,bcgjpn,bctgj->bctgjp', cc, h_prev, jnp.exp(a_cs))
    y = (y_diag + y_off).reshape(bsz, L, SSD_HEADS, SSD_HEAD_DIM)
    return y, h_last.reshape(bsz, SSD_HEADS, SSD_HEAD_DIM, SSD_STATE)


def ssd_mixer(x, conv_buf, h0, w_in, conv_w, conv_b, dt_bias, a_log, d_skip, norm_g, w_out):
    f32 = jnp.float32
    bsz, L, _ = x.shape
    proj = x @ w_in
    z = proj[..., :SSD_INNER]
    xbc = proj[..., SSD_INNER:SSD_INNER + SSD_CONV_DIM]
    dt_raw = proj[..., SSD_INNER + SSD_CONV_DIM:]
    ctx = jnp.concatenate([conv_buf.astype(xbc.dtype), xbc], axis=1)
    conv = conv_b
    for k in range(SSD_CONV):
        conv = conv + ctx[:, k:k + L] * conv_w[k]
    xbc = jax.nn.silu(conv)
    gn = SSD_GROUPS * SSD_STATE
    xs = xbc[..., :SSD_INNER].reshape(bsz, L, SSD_HEADS, SSD_HEAD_DIM)
    bm = xbc[..., SSD_INNER:SSD_INNER + gn].reshape(bsz, L, SSD_GROUPS, SSD_STATE)
    cm = xbc[..., SSD_INNER + gn:].reshape(bsz, L, SSD_GROUPS, SSD_STATE)
    dt = jax.nn.softplus(dt_raw.astype(f32) + dt_bias.astype(f32))
    a = -jnp.exp(a_log.astype(f32))
    q = SSD_CHUNK if L % SSD_CHUNK == 0 else L
    y, h_last = ssd_scan(xs, dt, a, bm, cm, h0, q)
    y = y + d_skip.astype(f32)[:, None] * xs.astype(f32)
    yg = (y.reshape(bsz, L, SSD_INNER) * jax.nn.silu(z.astype(f32))).reshape(
        bsz, L, SSD_GROUPS, SSD_INNER // SSD_GROUPS)
    yg = yg * lax.rsqrt(jnp.mean(jnp.square(yg), axis=-1, keepdims=True) + RMS_EPS)
    y = (yg.reshape(bsz, L, SSD_INNER) * norm_g.astype(f32)).astype(x.dtype)
    return y @ w_out, ctx[:, -(SSD_CONV - 1):], h_last


def peer_ffn(x, w_q, sub_keys, expert_u, expert_v):
    f32 = jnp.float32
    shp = x.shape
    xt = x.reshape(-1, D_MODEL)
    T = xt.shape[0]
    nb = -(-T // PEER_BLOCK)
    xt = jnp.pad(xt, ((0, nb * PEER_BLOCK - T), (0, 0))).reshape(nb, PEER_BLOCK, D_MODEL)
    keys = sub_keys.astype(f32)

    def block(xb):
        q = (xb @ w_q).astype(f32).reshape(PEER_BLOCK, PEER_HEADS, 2, PEER_HALF)
        s = jnp.einsum('thid,hind->thin', q, keys)
        top_s, top_i = lax.top_k(s, PEER_TOPK)
        cand_s = top_s[:, :, 0, :, None] + top_s[:, :, 1, None, :]
        cand_i = top_i[:, :, 0, :, None] * PEER_NKEYS + top_i[:, :, 1, None, :]
        best_s, best_j = lax.top_k(cand_s.reshape(PEER_BLOCK, PEER_HEADS, -1), PEER_TOPK)
        idx = jnp.take_along_axis(cand_i.reshape(PEER_BLOCK, PEER_HEADS, -1), best_j, axis=-1)
        gate = jax.nn.softmax(best_s, axis=-1)
        u = expert_u[idx]
        v = expert_v[idx]
        act = jax.nn.gelu(jnp.einsum('td,thkd->thk', xb, u).astype(f32))
        return jnp.einsum('thk,thkd->td', (gate * act).astype(xb.dtype), v)

    out = lax.map(block, xt)
    return out.reshape(-1, D_MODEL)[:T].reshape(shp)


def trunk(h, pos0, s5_h_re, s5_h_im, pool_buf, ssd_conv_buf, ssd_h,
          s5_w_in, s5_a_re, s5_a_im, s5_log_dt, s5_b_re, s5_b_im, s5_c_re, s5_c_im, s5_d,
          s5_w_glu, s5_b_glu, s5_w_out,
          pool_w_in, pool_w_grp, pool_scale, pool_w_out,
          cmlp_w_in, cmlp_b_in, cmlp_ln_g, cmlp_ln_b, cmlp_w_s, cmlp_b_s, cmlp_w_out,
          ssd_w_in, ssd_conv_w, ssd_conv_b, ssd_dt_bias, ssd_a_log, ssd_d, ssd_norm_g, ssd_w_out,
          ln1_g, ln1_b, ln2_g, ln2_b, peer_w_q, peer_keys, peer_u, peer_v):
    cmlp_v = None
    for i in range(DEPTH):
        kind = i % N_MIXERS
        if kind == 0:
            mix, s5_h_re, s5_h_im = s5_mixer(h, s5_h_re, s5_h_im, s5_w_in, s5_a_re, s5_a_im,
                                             s5_log_dt, s5_b_re, s5_b_im, s5_c_re, s5_c_im,
                                             s5_d, s5_w_glu, s5_b_glu, s5_w_out)
        elif kind == 1:
            mix, pool_buf = pool_mixer(h, pool_buf, pos0, pool_w_in, pool_w_grp, pool_scale,
                                       pool_w_out)
        elif kind == 2:
            mix, cmlp_v = chunk_mlp_mixer(h, cmlp_w_in, cmlp_b_in, cmlp_ln_g, cmlp_ln_b,
                                          cmlp_w_s, cmlp_b_s, cmlp_w_out)
        else:
            mix, ssd_conv_buf, ssd_h = ssd_mixer(h, ssd_conv_buf, ssd_h, ssd_w_in, ssd_conv_w,
                                                 ssd_conv_b, ssd_dt_bias, ssd_a_log, ssd_d,
                                                 ssd_norm_g, ssd_w_out)
        h = layer_norm(ALPHA * h + mix, ln1_g[i], ln1_b[i])
        ffn = peer_ffn(h, peer_w_q[i], peer_keys[i], peer_u[i], peer_v[i])
        h = layer_norm(ALPHA * h + ffn, ln2_g[i], ln2_b[i])
    return h, s5_h_re, s5_h_im, pool_buf, cmlp_v, ssd_conv_buf, ssd_h


def setup_inputs(seed: int = 0) -> dict:
    key = jax.random.key(seed)
    ks = iter(jax.random.split(key, 64))

    def nrm(shape, scale=1.0):
        return jax.random.normal(next(ks), shape, jnp.float32) * scale

    def unif(shape, lo, hi):
        return jax.random.uniform(next(ks), shape, jnp.float32, lo, hi)

    p = {}
    p['x_prompt'] = nrm((BATCH, SEQ, D_MODEL))
    p['x_sample'] = nrm((DEC_BATCH, DEC_SEQ, D_MODEL))
    p['state_s5_re'] = nrm((DEC_BATCH, S5_GROUPS, S5_STATE), 0.2)
    p['state_s5_im'] = nrm((DEC_BATCH, S5_GROUPS, S5_STATE), 0.2)
    p['state_pool'] = nrm((DEC_BATCH, POOL_BUF, POOL_WIDTH))
    p['state_ssd_conv'] = nrm((DEC_BATCH, SSD_CONV - 1, SSD_CONV_DIM))
    p['state_ssd'] = nrm((DEC_BATCH, SSD_HEADS, SSD_HEAD_DIM, SSD_STATE), 0.5)
    n_idx = jnp.arange(S5_STATE, dtype=jnp.float32)[None, :]
    p['s5_w_in'] = nrm((D_MODEL, S5_WIDTH), D_MODEL ** -0.5)
    p['s5_a_re'] = -0.5 + nrm((S5_GROUPS, S5_STATE), 0.01)
    p['s5_a_im'] = math.pi * n_idx + nrm((S5_GROUPS, S5_STATE), 0.01)
    p['s5_log_dt'] = unif((S5_GROUPS,), math.log(1e-3), math.log(1e-1))
    p['s5_b_re'] = nrm((S5_GROUPS, S5_STATE, S5_GROUP), (2 * S5_GROUP) ** -0.5)
    p['s5_b_im'] = nrm((S5_GROUPS, S5_STATE, S5_GROUP), (2 * S5_GROUP) ** -0.5)
    p['s5_c_re'] = nrm((S5_GROUPS, S5_GROUP, S5_STATE), S5_STATE ** -0.5)
    p['s5_c_im'] = nrm((S5_GROUPS, S5_GROUP, S5_STATE), S5_STATE ** -0.5)
    p['s5_d'] = nrm((S5_WIDTH,))
    p['s5_w_glu'] = nrm((S5_WIDTH, S5_WIDTH), S5_WIDTH ** -0.5)
    p['s5_b_glu'] = nrm((S5_WIDTH,), 0.01)
    p['s5_w_out'] = nrm((S5_WIDTH, D_MODEL), BETA * S5_WIDTH ** -0.5)
    p['pool_w_in'] = nrm((D_MODEL, POOL_WIDTH), D_MODEL ** -0.5)
    p['pool_w_grp'] = nrm((len(POOL_WINDOWS), POOL_GROUP, POOL_GROUP), POOL_GROUP ** -0.5)
    p['pool_scale'] = 1.0 + nrm((POOL_WIDTH,), 0.01)
    p['pool_w_out'] = nrm((POOL_WIDTH, D_MODEL), BETA * POOL_WIDTH ** -0.5)
    p['cmlp_w_in'] = nrm((D_MODEL, 2 * CMLP_WIDTH), D_MODEL ** -0.5)
    p['cmlp_b_in'] = nrm((2 * CMLP_WIDTH,), 0.01)
    p['cmlp_ln_g'] = 1.0 + nrm((CMLP_WIDTH,), 0.01)
    p['cmlp_ln_b'] = nrm((CMLP_WIDTH,), 0.01)
    p['cmlp_w_s'] = nrm((CMLP_HEADS, CMLP_CHUNK, CMLP_CHUNK), 0.5 * CMLP_CHUNK ** -0.5)
    p['cmlp_b_s'] = 1.0 + nrm((CMLP_HEADS, CMLP_CHUNK), 0.1)
    p['cmlp_w_out'] = nrm((CMLP_WIDTH, D_MODEL), BETA * CMLP_WIDTH ** -0.5)
    dt0 = jnp.exp(unif((SSD_HEADS,), math.log(1e-3), math.log(1e-1)))
    p['ssd_w_in'] = nrm((D_MODEL, SSD_PROJ), D_MODEL ** -0.5)
    p['ssd_conv_w'] = nrm((SSD_CONV, SSD_CONV_DIM), SSD_CONV ** -0.5)
    p['ssd_conv_b'] = nrm((SSD_CONV_DIM,), 0.01)
    p['ssd_dt_bias'] = dt0 + jnp.log(-jnp.expm1(-dt0))
    p['ssd_a_log'] = jnp.log(unif((SSD_HEADS,), 1.0, 16.0))
    p['ssd_d'] = 1.0 + nrm((SSD_HEADS,), 0.01)
    p['ssd_norm_g'] = 1.0 + nrm((SSD_INNER,), 0.01)
    p['ssd_w_out'] = nrm((SSD_INNER, D_MODEL), BETA * SSD_INNER ** -0.5)
    p['ln1_g'] = 1.0 + nrm((DEPTH, D_MODEL), 0.01)
    p['ln1_b'] = nrm((DEPTH, D_MODEL), 0.01)
    p['ln2_g'] = 1.0 + nrm((DEPTH, D_MODEL), 0.01)
    p['ln2_b'] = nrm((DEPTH, D_MODEL), 0.01)
    p['peer_w_q'] = nrm((DEPTH, D_MODEL, PEER_HEADS * PEER_QDIM), D_MODEL ** -0.5)
    p['peer_keys'] = nrm((DEPTH, PEER_HEADS, 2, PEER_NKEYS, PEER_HALF), PEER_HALF ** -0.5)
    p['peer_u'] = nrm((DEPTH, PEER_EXPERTS, D_MODEL), D_MODEL ** -0.5)
    p['peer_v'] = nrm((DEPTH, PEER_EXPERTS, D_MODEL), BETA * PEER_HEADS ** -0.5)
    return p


def reference(x_prompt, x_sample, state_s5_re, state_s5_im, state_pool, state_ssd_conv, state_ssd,
              s5_w_in, s5_a_re, s5_a_im, s5_log_dt, s5_b_re, s5_b_im, s5_c_re, s5_c_im, s5_d,
              s5_w_glu, s5_b_glu, s5_w_out,
              pool_w_in, pool_w_grp, pool_scale, pool_w_out,
              cmlp_w_in, cmlp_b_in, cmlp_ln_g, cmlp_ln_b, cmlp_w_s, cmlp_b_s, cmlp_w_out,
              ssd_w_in, ssd_conv_w, ssd_conv_b, ssd_dt_bias, ssd_a_log, ssd_d, ssd_norm_g, ssd_w_out,
              ln1_g, ln1_b, ln2_g, ln2_b, peer_w_q, peer_keys, peer_u, peer_v):
    weights = (s5_w_in, s5_a_re, s5_a_im, s5_log_dt, s5_b_re, s5_b_im, s5_c_re, s5_c_im, s5_d,
               s5_w_glu, s5_b_glu, s5_w_out,
               pool_w_in, pool_w_grp, pool_scale, pool_w_out,
               cmlp_w_in, cmlp_b_in, cmlp_ln_g, cmlp_ln_b, cmlp_w_s, cmlp_b_s, cmlp_w_out,
               ssd_w_in, ssd_conv_w, ssd_conv_b, ssd_dt_bias, ssd_a_log, ssd_d, ssd_norm_g, ssd_w_out,
               ln1_g, ln1_b, ln2_g, ln2_b, peer_w_q, peer_keys, peer_u, peer_v)
    f32 = jnp.float32
    bp = x_prompt.shape[0]
    (y_prompt, s5_re_p, s5_im_p, pool_p, _, conv_p, ssd_p) = trunk(
        x_prompt, 0,
        jnp.zeros((bp, S5_GROUPS, S5_STATE), f32),
        jnp.zeros((bp, S5_GROUPS, S5_STATE), f32),
        jnp.zeros((bp, POOL_BUF, POOL_WIDTH), x_prompt.dtype),
        jnp.zeros((bp, SSD_CONV - 1, SSD_CONV_DIM), x_prompt.dtype),
        jnp.zeros((bp, SSD_HEADS, SSD_HEAD_DIM, SSD_STATE), f32),
        *weights)
    (y_sample, s5_re_s, s5_im_s, pool_s, cmlp_v_s, conv_s, ssd_s) = trunk(
        x_sample, PAST_LEN, state_s5_re, state_s5_im, state_pool, state_ssd_conv, state_ssd,
        *weights)
    return (y_prompt, y_sample, s5_re_p, s5_im_p, pool_p, conv_p, ssd_p,
            s5_re_s, s5_im_s, pool_s, cmlp_v_s, conv_s, ssd_s)
```

```python
import numpy as np
from contextlib import ExitStack
import concourse.bass as bass
import concourse.mybir as mybir
from concourse.bass_utils import run_bass_kernel_spmd

F32 = mybir.dt.float32
BF16 = mybir.dt.bfloat16
I32 = mybir.dt.int32
U32 = mybir.dt.uint32
ALU = mybir.AluOpType
AF = mybir.ActivationFunctionType
AX = mybir.AxisListType

COMPUTE = ("pe", "dve", "act", "pool")
NDSEM = {"sp": 12, "pool": 12, "act": 4}
SAME_ENGINE_SYNC = {"pe": False, "dve": True, "act": True, "pool": True, "sp": True}

D = 1024
ALPHA = 8.0 ** 0.25
LN_EPS = 1e-5
NSEQ = 16
DSEQ = 8
NEXP = 16384


class Op:
    __slots__ = ("eng", "fn", "waits", "kind", "dsem", "dval", "cidx")


class Prog:
    def __init__(self, nc):
        self.nc = nc
        self.es = ExitStack()
        self.ops = {e: [] for e in ("pe", "dve", "act", "pool", "sp")}
        self.ncomp = {e: 0 for e in COMPUTE}
        self.last_w = {}
        self.readers = {}
        self.dcount = {}
        self.dlast = {}
        self.drr = {e: 0 for e in NDSEM}
        self.nbuf = 0
        self.bar = []
        self.scopes = []

    def sb(self, shape, dt=F32, name=None):
        self.nbuf += 1
        name = f"{name or 'sb'}_{self.nbuf}"
        es = self.scopes[-1] if self.scopes else self.es
        return es.enter_context(self.nc.sbuf_tensor(name, list(shape), dt))

    def ps(self, shape, dt=F32, name=None):
        self.nbuf += 1
        name = name or f"ps{self.nbuf}"
        return self.es.enter_context(self.nc.psum_tensor(name, list(shape), dt))

    def push_scope(self):
        self.scopes.append(ExitStack())

    def pop_scope(self):
        self.barrier()
        self.scopes.pop().close()

    def barrier(self):
        bar = []
        for e in COMPUTE:
            if self.ncomp[e]:
                bar.append(("c", e, self.ncomp[e] - 1))
        bar.extend(self.dlast.values())
        self.bar = bar

    def _deps(self, reads, writes):
        deps = list(self.bar)
        for r in reads:
            w = self.last_w.get(r)
            if w is not None:
                deps.append(w)
        for w_ in writes:
            w = self.last_w.get(w_)
            if w is not None:
                deps.append(w)
            deps.extend(self.readers.get(w_, ()))
        return deps

    def _commit(self, token, reads, writes):
        for w_ in writes:
            self.last_w[w_] = token
            self.readers[w_] = []
        for r in reads:
            if r not in writes:
                self.readers.setdefault(r, []).append(token)

    @staticmethod
    def _excl(reads, writes):
        ex = [r for r in reads if isinstance(r, str) and r.startswith("PS")]
        if ex:
            reads = [r for r in reads if r not in ex]
            writes = list(writes) + [r for r in ex if r not in writes]
        return reads, writes

    def defer_begin(self, name="f"):
        if not hasattr(self, "dq"):
            self.dq = {}
        self.dq[name] = []
        self.deferring = name

    def defer_end(self):
        n = len(self.dq[self.deferring])
        self.deferring = None
        return n

    def flush(self, n, name="f"):
        q = getattr(self, "dq", {}).get(name)
        if not q:
            return
        k = len(q) if n is None else min(n, len(q))
        for _ in range(k):
            kind, a = q.pop(0)
            (self.op if kind == "c" else self.dma)(*a, _now=True)

    def op(self, eng, fn, reads=(), writes=(), _now=False):
        if getattr(self, "deferring", None) and not _now:
            self.dq[self.deferring].append(("c", (eng, fn, reads, writes)))
            return None
        reads, writes = self._excl(list(reads), list(writes))
        o = Op()
        o.eng, o.fn, o.kind = eng, fn, "c"
        o.waits = self._deps(reads, writes)
        o.cidx = self.ncomp[eng]
        self.ncomp[eng] += 1
        self.ops[eng].append(o)
        self._commit(("c", eng, o.cidx), reads, writes)
        return o

    def I(self, eng, meth, *args, reads=(), writes=(), **kw):
        return self.op(eng, lambda h: getattr(h, meth)(*args, **kw), reads=reads, writes=writes)

    def Dm(self, q, meth, *args, reads=(), writes=(), **kw):
        return self.dma(q, lambda h: getattr(h, meth)(*args, **kw), reads=reads, writes=writes)

    def dma(self, q, fn, reads=(), writes=(), _now=False):
        if getattr(self, "deferring", None) and not _now:
            self.dq[self.deferring].append(("d", (q, fn, reads, writes)))
            return None
        o = Op()
        o.eng, o.fn, o.kind = q, fn, "d"
        deps = self._deps(reads, writes)
        k = self.drr[q]
        self.drr[q] = (k + 1) % NDSEM[q]
        key = (q, k)
        prev = self.dlast.get(key)
        if prev is not None:
            deps.append(prev)
        cnt = self.dcount.get(key, 0) + 1
        self.dcount[key] = cnt
        o.dsem, o.dval = key, 16 * cnt
        tok = ("d", key, o.dval)
        self.dlast[key] = tok
        o.waits = deps
        self.ops[q].append(o)
        self._commit(tok, reads, writes)
        return o

    def emit(self):
        nc = self.nc
        es = self.es
        csem = {e: es.enter_context(nc.semaphore(f"c_{e}")) for e in COMPUTE}
        dsem = {}
        for q, n in NDSEM.items():
            for k in range(n):
                dsem[(q, k)] = es.enter_context(nc.semaphore(f"d_{q}{k}"))
        ops = self.ops
        final = dict(self.dcount)

        def run(ename, h):
            waited = {}
            for o in ops[ename]:
                need = {}
                for d in o.waits:
                    if d[0] == "c":
                        _, e2, idx = d
                        if e2 == ename and not SAME_ENGINE_SYNC[ename]:
                            continue
                        key, val = ("c", e2), idx + 1
                    else:
                        _, dk, val = d
                        key = ("d", dk)
                    if need.get(key, 0) < val:
                        need[key] = val
                for key, val in need.items():
                    if waited.get(key, 0) >= val:
                        continue
                    waited[key] = val
                    s = csem[key[1]] if key[0] == "c" else dsem[key[1]]
                    h.wait_ge(s, val)
                inst = o.fn(h)
                if o.kind == "c":
                    inst.then_inc(csem[ename], 1)
                else:
                    inst.then_inc(dsem[o.dsem], 16)
            if ename == "sp":
                for key, cnt in final.items():
                    h.wait_ge(dsem[key], 16 * cnt)

        with nc.Block() as block:
            @block.sync
            def _(h):
                run("sp", h)

            @block.tensor
            def _(h):
                run("pe", h)

            @block.vector
            def _(h):
                run("dve", h)

            @block.scalar
            def _(h):
                run("act", h)

            @block.gpsimd
            def _(h):
                run("pool", h)
        es.close()


class K:
    def __init__(self, TP=16, layers=(0, 1, 2, 3), mixers=True, peer=True, dbg=()):
        self.TP = TP
        self.NT = TP + 1
        self.layers = layers
        self.do_mix = mixers
        self.do_peer = peer
        self.dbg = set(dbg)
        nc = bass.Bass("TRN2", target_bir_lowering=False)
        self.nc = nc
        self.P = Prog(nc)
        self.din = {}
        self.dout = {}
        self.evk = 0
        self.NG = 8

    def inp(self, name, shape, dt=F32):
        t = self.nc.dram_tensor(name, list(shape), dt, kind="ExternalInput").ap()
        self.din[name] = t
        return t

    def outp(self, name, shape, dt=F32):
        t = self.nc.dram_tensor(name, list(shape), dt, kind="ExternalOutput").ap()
        self.dout[name] = t
        return t

    def ev(self, out, in_, reads, writes, eng=None):
        if eng is None:
            eng = ("act", "dve")[self.evk % 2]
            self.evk += 1
        if eng == "act":
            self.P.I("act", "copy", out, in_, reads=reads, writes=writes)
        elif eng == "dve":
            self.P.I("dve", "tensor_copy", out, in_, reads=reads, writes=writes)
        else:
            self.P.I("pool", "tensor_copy", out, in_, reads=reads, writes=writes)

    def setup(self):
        P = self.P
        TP, NT = self.TP, self.NT
        L = TP * 128
        self.x_p = self.inp("x_p", [L, D])
        self.x_s = self.inp("x_s", [128, D])
        self.y_p = self.outp("y_p", [L, D])
        self.y_s = self.outp("y_s", [128, D])
        self.ln1_g = self.inp("ln1_g", [4, D]); self.ln1_b = self.inp("ln1_b", [4, D])
        self.ln2_g = self.inp("ln2_g", [4, D]); self.ln2_b = self.inp("ln2_b", [4, D])
        if self.do_peer:
            self.d_wq = self.inp("peer_w_q", [4, D, 2048])
            self.d_keysT = self.inp("peer_keysT", [4, 128, 2048])
            self.d_u = self.inp("peer_u", [4 * NEXP, D])
            self.d_v = self.inp("peer_v", [4 * NEXP, D])
            self.uvb = self.nc.dram_tensor("uvb", [4 * NEXP, 2048], BF16, kind="Internal").ap()
        if self.do_mix:
            if 0 in self.layers:
                for n in ("s5_w_in", "s5_w_glu", "s5_w_out"):
                    self.inp(n, [D, D])
                self.inp("s5_aT", [128, 3, 32]); self.inp("s5_bT", [128, 2, 32, 16]); self.inp("s5_cT", [128, 2, 32, 16])
                self.inp("s5_dT", [128, 8]); self.inp("s5_b_gluT", [128, 8])
                self.inp("state_s5_re", [NSEQ, 4096]); self.inp("state_s5_im", [NSEQ, 4096])
                self.outp("s5_re_p", [32, 128]); self.outp("s5_im_p", [32, 128])
                self.outp("s5_re_s", [NSEQ, 4096]); self.outp("s5_im_s", [NSEQ, 4096])
            if 3 in self.layers:
                self.inp("ssd_w_in", [D, 5152]); self.inp("ssd_w_out", [2048, D]); self.inp("ssd_hd", [1, 96])
                self.inp("ssd_conv_wT", [128, 24, 4]); self.inp("ssd_conv_bT", [128, 24]); self.inp("ssd_norm_gT", [128, 16])
                self.inp("state_ssd_conv", [NSEQ * 3, 3072]); self.inp("state_ssd", [NSEQ, 2048, 128])
                self.outp("conv_p", [3, 3072]); self.outp("ssd_p", [2048, 128])
                self.outp("conv_s", [NSEQ * 3, 3072]); self.outp("ssd_s", [NSEQ * 2048, 128])
            if 1 in self.layers:
                self.inp("pool_w_in", [D, D]); self.inp("pool_w_out", [D, D]); self.inp("pool_w_grp", [4, 256, 256])
                self.inp("pool_scaleT", [128, 8]); self.inp("state_pool", [NSEQ * 15, D])
                self.outp("pool_p", [15, D]); self.outp("pool_s", [NSEQ * 15, D])
            if 2 in self.layers:
                self.inp("cmlp_w_in", [D, 2048]); self.inp("cmlp_w_out", [D, D]); self.inp("cmlp_b_in", [1, 2048])
                self.inp("cmlp_ln_g", [1, D]); self.inp("cmlp_ln_b", [1, D])
                self.inp("cmlp_w_sT", [4, 128, 128]); self.inp("cmlp_b_sT", [128, 4])
                self.outp("cmlp_v_s", [128, D])
        self.H = P.sb([128, NT, D], F32, "H")
        self.ident = P.sb([128, 128], F32, "ident")
        self.identb = P.sb([128, 128], BF16, "identb")
        self.iota16 = P.sb([128, 16], F32, "iota16")
        self.lng = P.sb([128, D], F32, "lng")
        self.lnb = P.sb([128, D], F32, "lnb")
        self.tmp = P.sb([128, D], F32, "tmp")
        self.small = P.sb([128, 32], F32, "small")
        self.psum = P.ps([128, 8, 512], F32, "psum")
        self.bank = [self.psum[:, i, :] for i in range(8)]
        iot = P.sb([128, 128], F32, "iot")
        P.I("pool", "iota", iot[:], pattern=[[1, 128]], base=0, channel_multiplier=-1,
                                      allow_small_or_imprecise_dtypes=True, writes=["iot"])
        P.I("dve", "tensor_single_scalar", self.ident[:], iot[:], 0.0, ALU.is_equal,
             reads=["iot"], writes=["ident"])
        P.I("dve", "tensor_copy", self.identb[:], self.ident[:], reads=["ident"], writes=["identb"])
        P.I("pool", "iota", self.iota16[:], pattern=[[1, 16]], base=0, channel_multiplier=0,
                                      allow_small_or_imprecise_dtypes=True, writes=["iota16"])
        self.iot = iot
        for i in range(TP):
            P.Dm("sp", "dma_start", out=self.H[:, i, :], in_=self.x_p[i * 128:(i + 1) * 128, :],
                  writes=[("H", i)])
        P.Dm("sp", "dma_start", out=self.H[:, TP, :], in_=self.x_s, writes=[("H", TP)])

    def finish(self):
        P = self.P
        TP = self.TP
        for i in range(TP):
            P.Dm("sp", "dma_start", out=self.y_p[i * 128:(i + 1) * 128, :], in_=self.H[:, i, :],
                  reads=[("H", i)])
        P.Dm("sp", "dma_start", out=self.y_s, in_=self.H[:, TP, :], reads=[("H", TP)])
        P.emit()

    def transpose_f32(self, src, nch, dst, src_reg, dst_reg, banks=(0, 1)):
        P = self.P
        for g in range(0, nch, 4):
            b = banks[(g // 4) % len(banks)]
            n = min(4, nch - g)
            for c in range(n):
                P.I("pe", "transpose", self.bank[b][:, c * 128:(c + 1) * 128],
                                                                src[:, (g + c) * 128:(g + c + 1) * 128], self.ident[:],
                     reads=[src_reg, "ident"], writes=[f"PS{b}"])
            self.ev(dst[:, g:g + n, :], self.bank[b][:, 0:n * 128].rearrange("p (c n) -> p c n", c=n),
                    reads=[f"PS{b}"], writes=[dst_reg])

    def load_ln(self, g_ap, b_ap, l):
        P = self.P
        P.Dm("sp", "dma_start", out=self.lng[:], in_=g_ap[l:l + 1, :].to_broadcast([128, D]), writes=["lng"])
        P.Dm("sp", "dma_start", out=self.lnb[:], in_=b_ap[l:l + 1, :].to_broadcast([128, D]), writes=["lnb"])

    def resid_ln(self, i, mix_ap, mix_regs):
        P = self.P
        Hi = self.H[:, i, :]
        tmp, sm = self.tmp, self.small
        P.I("dve", "scalar_tensor_tensor", tmp[:], Hi, ALPHA, mix_ap, ALU.mult, ALU.add,
             reads=[("H", i)] + list(mix_regs), writes=["tmp"])
        self.ln_inplace(tmp, "tmp", Hi, ("H", i), self.lng, self.lnb)

    def ln_inplace(self, src, src_reg, dst_ap, dst_reg, g_t, b_t, n=D):
        P = self.P
        sm = self.small
        nchk = n // 512
        for j in range(nchk):
            P.I("dve", "bn_stats", sm[:, j * 6:(j + 1) * 6], src[:, j * 512:(j + 1) * 512],
                 reads=[src_reg], writes=["small"])
        P.I("dve", "bn_aggr", sm[:, 12:14], sm[:, 0:6 * nchk], reads=["small"], writes=["small"])
        P.I("act", "activation", sm[:, 14:15], sm[:, 13:14], AF.Sqrt, bias=self.eps_t[:, 0:1],
             reads=["small", "eps"], writes=["small"])
        P.I("dve", "reciprocal", sm[:, 15:16], sm[:, 14:15], reads=["small"], writes=["small"])
        P.I("dve", "tensor_scalar", src[:, 0:n], src[:, 0:n], sm[:, 12:13], sm[:, 15:16], ALU.subtract, ALU.mult,
             reads=["small", src_reg], writes=[src_reg])
        P.I("dve", "tensor_tensor", src[:, 0:n], src[:, 0:n], g_t[:, 0:n], ALU.mult,
             reads=[src_reg, "lng"], writes=[src_reg])
        P.I("dve", "tensor_tensor", dst_ap, src[:, 0:n], b_t[:, 0:n], ALU.add,
             reads=[src_reg, "lnb"], writes=[dst_reg])

    def convert_tables(self, l, stg, cb):
        P = self.P
        n = len(stg)
        for ch in range(128):
            r0 = l * NEXP + ch * 128
            for hf, src in enumerate((self.d_u, self.d_v)):
                k = (ch * 2 + hf) % n
                P.Dm("sp", "dma_start", out=stg[k][:], in_=src[r0:r0 + 128, :], writes=[f"cv_s{k}"])
                P.I("act", "copy", cb[k][:], stg[k][:], reads=[f"cv_s{k}"], writes=[f"cv_b{k}"])
                P.Dm("sp", "dma_start", out=self.uvb[r0:r0 + 128, hf * 1024:(hf + 1) * 1024], in_=cb[k][:],
                     reads=[f"cv_b{k}"], writes=[("uvb", l)])

    def peer(self, l):
        P = self.P
        NT = self.NT
        P.flush(None, "cv0")
        P.push_scope()
        wq = P.sb([128, 8, 2048], BF16, "wq")
        keysT = P.sb([128, 16, 128], F32, "keysT")
        hT = P.sb([128, 8, 128], BF16, "hT")
        qT = P.sb([128, 16, 128], F32, "qT")
        sbig = P.sb([128, 16, 128], F32, "sbig")
        oh = qT[:].rearrange("p c n -> p (c n)").rearrange("p (h a b) -> p h a b", h=8, a=16)
        QALL = [("qT", g) for g in range(4)]
        tv = P.sb([128, 16, 16], F32, "tv")
        ti = P.sb([128, 16, 16], U32, "ti")
        tif = P.sb([128, 16, 16], F32, "tif")
        wk2 = [P.sb([128, 256], F32, f"wk{j}") for j in range(2)]
        bs = P.sb([128, 8, 16], F32, "bs")
        bj = P.sb([128, 8, 16], U32, "bj")
        ja = P.sb([128, 8, 16], U32, "ja")
        jaf = P.sb([128, 8, 16], F32, "jaf")
        jbf = P.sb([128, 8, 16], F32, "jbf")
        i0 = P.sb([128, 8, 16], F32, "i0")
        i1 = P.sb([128, 8, 16], F32, "i1")
        idx2 = [P.sb([128, 128], I32, f"idx{j}") for j in range(2)]
        gate2 = [P.sb([128, 8, 16], F32, f"gate{j}") for j in range(2)]
        gsum = P.sb([128, 8], F32, "gsum")
        act = P.sb([128, 128], F32, "actv")
        wgt = P.sb([128, 128], F32, "wgt")
        NG = self.NG
        gb = [P.sb([128, 2048], BF16, f"gb{j}") for j in range(NG)]
        junk2 = [P.sb([128, D], BF16, f"junk{j}") for j in range(2)]
        tmpv = [P.sb([128, D], BF16, f"tmpv{j}") for j in range(2)]
        cstg = [self.stg0[0], P.sb([128, D], F32, "cv_s1")]
        ccb = [self.cb0[0], P.sb([128, D], BF16, "cv_b1")]
        bk = self.bank
        for k in range(8):
            P.Dm("pool", "dma_start", out=wq[:, k, :], in_=self.d_wq[l, k * 128:(k + 1) * 128, :], writes=["wq"])
        P.Dm("sp", "dma_start", out=keysT[:].rearrange("p c n -> p (c n)"), in_=self.d_keysT[l], writes=["keysT"])
        self.load_ln(self.ln2_g, self.ln2_b, l)
        tv4 = tv[:].rearrange("p (h i) k -> p h i k", i=2)
        tif4 = tif[:].rearrange("p (h i) k -> p h i k", i=2)
        cand = sbig[:].rearrange("p c n -> p (c n)").rearrange("p (h a b) -> p h a b", h=8, a=16)
        cand3 = sbig[:].rearrange("p c n -> p (c n)").rearrange("p (h x) -> p h x", h=8)
        sball = [("sbig", g) for g in range(4)]
        io4 = self.iota16[:].unsqueeze(1).unsqueeze(1).to_broadcast([128, 8, 16, 16])

        def front(i):
            sfx = i % 2
            idx, gate = idx2[sfx], gate2[sfx]
            ireg, greg = f"idx{sfx}", f"gate{sfx}"
            Hi = self.H[:, i, :]
            self.transpose_f32(Hi, 8, hT, ("H", i), "hT", banks=(6, 7))
            for c in range(16):
                b = 4 + (c // 4) % 2
                for k in range(8):
                    P.I("pe", "matmul", bk[b][:, (c % 4) * 128:(c % 4 + 1) * 128], lhsT=wq[:, k, c * 128:(c + 1) * 128],
                        rhs=hT[:, k, :], start=(k == 0), stop=(k == 7), reads=["wq", "hT"], writes=[f"PS{b}"])
                if c % 4 == 3:
                    self.ev(qT[:, c - 3:c + 1, :], bk[b][:, :].rearrange("p (c n) -> p c n", c=4),
                            reads=[f"PS{b}"], writes=[("qT", c // 4)], eng="act")
            for c in range(16):
                b = 6 + (c // 4) % 2
                P.I("pe", "matmul", bk[b][:, (c % 4) * 128:(c % 4 + 1) * 128], lhsT=qT[:, c, :], rhs=keysT[:, c, :],
                    start=True, stop=True, reads=[("qT", c // 4), "keysT"], writes=[f"PS{b}"])
                if c % 4 == 3:
                    self.ev(sbig[:, c - 3:c + 1, :], bk[b][:, :].rearrange("p (c n) -> p c n", c=4),
                            reads=[f"PS{b}"], writes=[("sbig", c // 4)], eng="act")
            TVA = [("tv", c) for c in range(16)]
            TIA = [("ti", c) for c in range(16)]
            for c in range(16):
                sr = ("sbig", c // 4)
                w_, wr_ = wk2[c % 2], f"wk{c % 2}"
                P.I("dve", "max", tv[:, c, 0:8], sbig[:, c, :], reads=[sr], writes=[("tv", c)])
                P.I("dve", "max_index", ti[:, c, 0:8], tv[:, c, 0:8], sbig[:, c, :], reads=[sr, ("tv", c)], writes=[("ti", c)])
                P.I("dve", "match_replace", w_[:, 0:128], tv[:, c, 0:8], sbig[:, c, :], -1e30, reads=[sr, ("tv", c)], writes=[wr_])
                P.I("dve", "max", tv[:, c, 8:16], w_[:, 0:128], reads=[wr_], writes=[("tv", c)])
                P.I("dve", "max_index", ti[:, c, 8:16], tv[:, c, 8:16], w_[:, 0:128], reads=[wr_, ("tv", c)], writes=[("ti", c)])
            P.I("dve", "tensor_copy", tif[:], ti[:], reads=TIA, writes=["tif"])
            P.I("dve", "tensor_tensor", cand, tv4[:, :, 0, :].unsqueeze(3).to_broadcast([128, 8, 16, 16]),
                tv4[:, :, 1, :].unsqueeze(2).to_broadcast([128, 8, 16, 16]), ALU.add, reads=TVA + sball, writes=sball)
            BSA = [("bs", hh) for hh in range(8)]
            BJA = [("bj", hh) for hh in range(8)]
            for hh in range(8):
                w_, wr_ = wk2[hh % 2], f"wk{hh % 2}"
                P.I("dve", "max", bs[:, hh, 0:8], cand3[:, hh, :], reads=sball, writes=[("bs", hh)])
                P.I("dve", "max_index", bj[:, hh, 0:8], bs[:, hh, 0:8], cand3[:, hh, :], reads=sball + [("bs", hh)], writes=[("bj", hh)])
                P.I("dve", "match_replace", w_[:], bs[:, hh, 0:8], cand3[:, hh, :], -1e30, reads=sball + [("bs", hh)], writes=[wr_])
                P.I("dve", "max", bs[:, hh, 8:16], w_[:], reads=[wr_], writes=[("bs", hh)])
                P.I("dve", "max_index", bj[:, hh, 8:16], bs[:, hh, 8:16], w_[:], reads=[wr_, ("bs", hh)], writes=[("bj", hh)])
            P.I("dve", "tensor_single_scalar", ja[:], bj[:], 4, ALU.logical_shift_right, reads=BJA, writes=["ja"])
            P.I("dve", "tensor_copy", jaf[:], ja[:], reads=["ja"], writes=["jaf"])
            P.I("dve", "tensor_single_scalar", ja[:], bj[:], 15, ALU.bitwise_and, reads=BJA + ["jaf"], writes=["ja"])
            P.I("dve", "tensor_copy", jbf[:], ja[:], reads=["ja"], writes=["jbf"])
            for (jf, half, dst, nm) in ((jaf, 0, i0, "i0"), (jbf, 1, i1, "i1")):
                P.I("dve", "tensor_tensor", oh, io4, jf[:].unsqueeze(3).to_broadcast([128, 8, 16, 16]), ALU.is_equal,
                    reads=["iota16", "jaf", "jbf"], writes=QALL)
                P.I("dve", "tensor_tensor", oh, oh, tif4[:, :, half, :].unsqueeze(2).to_broadcast([128, 8, 16, 16]), ALU.mult,
                    reads=QALL + ["tif"], writes=QALL)
                P.I("dve", "tensor_reduce", dst[:], oh, AX.X, ALU.add, reads=QALL, writes=[nm])
            P.I("dve", "scalar_tensor_tensor", i0[:], i0[:], 128.0, i1[:], ALU.mult, ALU.add, reads=["i0", "i1"], writes=["i0"])
            P.I("dve", "tensor_scalar", idx[:], i0[:].rearrange("p h k -> p (h k)"), float(l * NEXP), None, ALU.add,
                reads=["i0"], writes=[ireg])
            P.I("dve", "tensor_tensor", gate[:], bs[:], bs[:, :, 0:1].to_broadcast([128, 8, 16]), ALU.subtract, reads=BSA, writes=[greg])
            P.I("act", "activation", gate[:], gate[:], AF.Exp, reads=[greg], writes=[greg])
            P.I("dve", "tensor_reduce", gsum[:], gate[:], AX.X, ALU.add, reads=[greg], writes=["gsum"])
            P.I("dve", "reciprocal", gsum[:], gsum[:], reads=["gsum"], writes=["gsum"])
            P.I("dve", "tensor_tensor", gate[:], gate[:], gsum[:].unsqueeze(2).to_broadcast([128, 8, 16]), ALU.mult,
                reads=[greg, "gsum"], writes=[greg])

        def back(i, nflush, ncv):
            sfx = i % 2
            idx, gate = idx2[sfx], gate2[sfx]
            ireg, greg = f"idx{sfx}", f"gate{sfx}"
            Hi = self.H[:, i, :]
            gflat = gate[:].rearrange("p h k -> p (h k)")
            GS = 1
            for gi in range(128 // GS):
                js = list(range(gi * GS, (gi + 1) * GS))
                wr = ("wgt", gi % 16)
                ars = [("actv", j % 16) for j in js]
                for j in js:
                    g, gr = gb[j % NG], f"gb{j % NG}"
                    P.Dm("pool", "indirect_dma_start", out=g[:], out_offset=None, in_=self.uvb,
                         in_offset=bass.IndirectOffsetOnAxis(ap=idx[:, j:j + 1], axis=0), reads=[ireg, ("uvb", l)], writes=[gr])
                    P.I("dve", "scalar_tensor_tensor", junk2[j % 2][:], g[:, 0:1024], 1.0, Hi, ALU.mult, ALU.mult, accum_out=act[:, j:j + 1],
                        reads=[gr, ("H", i)], writes=[f"junk{j % 2}", ("actv", j % 16)])
                    P.flush(nflush)
                    if j % 3 == 2:
                        P.flush(ncv, "cv")
                P.I("act", "activation", wgt[:, js[0]:js[-1] + 1], act[:, js[0]:js[-1] + 1], AF.Gelu_apprx_tanh, reads=ars, writes=[wr])
                P.I("dve", "tensor_tensor", wgt[:, js[0]:js[-1] + 1], wgt[:, js[0]:js[-1] + 1], gflat[:, js[0]:js[-1] + 1], ALU.mult,
                    reads=[wr, greg], writes=[wr])
                for j in js:
                    g, gr = gb[j % NG], f"gb{j % NG}"
                    tb, tr = tmpv[j % 2], f"tmpv{j % 2}"
                    P.I("act", "activation", tb[:], g[:, 1024:2048], AF.Copy, scale=wgt[:, j:j + 1], reads=[gr, wr], writes=[tr])
                    for n in range(2):
                        P.I("pe", "matmul", bk[n], lhsT=self.identb[:], rhs=tb[:, n * 512:(n + 1) * 512], start=(j == 0), stop=(j == 127),
                            reads=[tr, "identb"], writes=[f"PS{n}"])
            P.flush(None)
            self.resid_ln(i, self.bank2(0), ["PS0", "PS1"])

        if l + 1 < 4 and (l + 1) in self.layers:
            P.defer_begin("cv")
            self.convert_tables(l + 1, cstg, ccb)
            ncv_total = P.defer_end()
        else:
            ncv_total = 0
        ncv = (ncv_total + (NT - 1) * 128 - 1) // max(1, (NT - 1) * 128) if ncv_total else 0
        front(0)
        for i in range(NT):
            if i + 1 < NT:
                P.defer_begin()
                front(i + 1)
                n = P.defer_end()
                back(i, (n + 99) // 100, ncv)
            else:
                back(i, 0, ncv)
        P.flush(None, "cv")
        P.pop_scope()

    def load_w(self, dst, src, reg):
        nk = src.shape[0] // 128
        for k in range(nk):
            self.P.Dm("pool", "dma_start", out=dst[:, k, :], in_=src[k * 128:(k + 1) * 128, :],
                       writes=[reg])

    def bank2(self, b):
        return self.psum[:, b:b + 2, :].rearrange("p a n -> p (a n)")

    def final_proj_ln(self, i, XT, xreg, w, wreg, nk, tok0=0):
        P = self.P
        for n in range(2):
            b = 6 + n
            for k in range(nk):
                P.I("pe", "matmul",
                    self.bank[b], lhsT=XT[:, k, tok0:tok0 + 128], rhs=w[:, k, n * 512:(n + 1) * 512],
                    start=(k == 0), stop=(k == nk - 1), reads=[xreg, wreg], writes=[f"PS{b}"])
        self.resid_ln(i, self.bank2(6), ["PS6", "PS7"])

    def make_hT(self, tiles, dst, reg):
        for n, i in enumerate(tiles):
            self.transpose_f32(self.H[:, i, :], 8, dst[:, :, n * 128:(n + 1) * 128], ("H", i), reg, banks=(0, 1))

    def cmlp(self, l):
        P = self.P
        TP, NT = self.TP, self.NT
        bk = self.bank
        d = self.din
        P.push_scope()
        w_in = P.sb([128, 8, 2048], BF16, "cw_in")
        w_out = P.sb([128, 8, 1024], BF16, "cw_out")
        b_in = P.sb([128, 2048], F32, "cb_in")
        cg = P.sb([128, D], F32, "c_lng")
        cb = P.sb([128, D], F32, "c_lnb")
        wsf = P.sb([128, 4, 128], F32, "wsf")
        wsp = P.sb([128, 4, 128], BF16, "wsp")
        wss = P.sb([128, 4, 128], BF16, "wss")
        bsp = P.sb([128, 4], F32, "bsp")
        bss = P.sb([128, 4], F32, "bss")
        hT = P.sb([128, 8, 128], BF16, "c_hT")
        zs = P.sb([128, 2048], F32, "zs")
        vb = P.sb([128, D], BF16, "vb")
        o = P.sb([128, D], F32, "c_o")
        oT = P.sb([128, 8, 128], BF16, "c_oT")
        self.load_w(w_in, d["cmlp_w_in"], "cw_in")
        self.load_w(w_out, d["cmlp_w_out"], "cw_out")
        P.Dm("sp", "dma_start", out=b_in[:], in_=d["cmlp_b_in"][0:1, :].to_broadcast([128, 2048]), writes=["cb_in"])
        P.Dm("sp", "dma_start", out=cg[:], in_=d["cmlp_ln_g"][0:1, :].to_broadcast([128, D]), writes=["c_lng"])
        P.Dm("sp", "dma_start", out=cb[:], in_=d["cmlp_ln_b"][0:1, :].to_broadcast([128, D]), writes=["c_lnb"])
        P.Dm("sp", "dma_start", out=bsp[:], in_=d["cmlp_b_sT"], writes=["bsp"])
        for j in range(NSEQ):
            P.Dm("sp", "dma_start", out=bss[j * 8:(j + 1) * 8, :], in_=d["cmlp_b_sT"][0:8, :], writes=["bss"])
        P.Dm("sp", "dma_start", out=wsf[:], in_=d["cmlp_w_sT"].rearrange("h s t -> s h t"), writes=["wsf"])
        for hh in range(4):
            P.I("pool", "affine_select", wsf[:, hh, :], wsf[:, hh, :], pattern=[[1, 128]],
                                                       compare_op=ALU.is_ge, fill=0.0, base=0, channel_multiplier=-1,
                 reads=["wsf"], writes=["wsf"])
        P.I("dve", "tensor_copy", wsp[:], wsf[:], reads=["wsf"], writes=["wsp"])
        wsf2 = P.sb([128, 4, 128], F32, "wsf2")
        P.I("dve", "memset", wsf2[:], 0.0, writes=["wsf2"])
        for j in range(NSEQ):
            P.Dm("sp", "dma_start", out=wsf2[j * 8:(j + 1) * 8, :, j * 8:(j + 1) * 8],
                                                  in_=d["cmlp_w_sT"][:, 0:8, 0:8].rearrange("h s t -> s h t"),
                  reads=[], writes=["wsf2"])
        for hh in range(4):
            P.I("pool", "affine_select", wsf2[:, hh, :], wsf2[:, hh, :], pattern=[[1, 128]],
                                                       compare_op=ALU.is_ge, fill=0.0, base=0, channel_multiplier=-1,
                 reads=["wsf2"], writes=["wsf2"])
        P.I("dve", "tensor_copy", wss[:], wsf2[:], reads=["wsf2"], writes=["wss"])
        self.load_ln(self.ln1_g, self.ln1_b, l)
        for i in range(NT):
            samp = (i == TP)
            ws_t, bs_t = (wss, bss) if samp else (wsp, bsp)
            wreg, breg = ("wss", "bss") if samp else ("wsp", "bsp")
            self.make_hT([i], hT, "c_hT")
            for n in range(4):
                b = 2 + n
                for k in range(8):
                    P.I("pe", "matmul", bk[b], lhsT=hT[:, k, :], rhs=w_in[:, k, n * 512:(n + 1) * 512],
                                                              start=(k == 0), stop=(k == 7),
                         reads=["c_hT", "cw_in"], writes=[f"PS{b}"])
                P.I("dve", "tensor_tensor", zs[:, n * 512:(n + 1) * 512], bk[b], b_in[:, n * 512:(n + 1) * 512], ALU.add,
                     reads=[f"PS{b}", "cb_in"], writes=[("zs", n)])
                P.I("act", "activation", zs[:, n * 512:(n + 1) * 512], zs[:, n * 512:(n + 1) * 512], AF.Gelu_apprx_tanh,
                     reads=[("zs", n)], writes=[("zs", n)])
            vv = zs[:, 1024:2048]
            sm = self.small
            for j in range(2):
                P.I("dve", "bn_stats", sm[:, j * 6:(j + 1) * 6], zs[:, 1024 + j * 512:1024 + (j + 1) * 512],
                     reads=[("zs", 2 + j)], writes=["small"])
            P.I("dve", "bn_aggr", sm[:, 12:14], sm[:, 0:12], reads=["small"], writes=["small"])
            P.I("act", "activation", sm[:, 14:15], sm[:, 13:14], AF.Sqrt, bias=self.eps_t[:, 0:1],
                 reads=["small", "eps"], writes=["small"])
            P.I("dve", "reciprocal", sm[:, 15:16], sm[:, 14:15], reads=["small"], writes=["small"])
            vregs = [("zs", 2), ("zs", 3)]
            P.I("dve", "tensor_scalar", vv, vv, sm[:, 12:13], sm[:, 15:16], ALU.subtract, ALU.mult,
                 reads=["small"] + vregs, writes=vregs)
            P.I("dve", "tensor_tensor", vv, vv, cg[:], ALU.mult, reads=vregs + ["c_lng"], writes=vregs)
            P.I("dve", "tensor_tensor", vv, vv, cb[:], ALU.add, reads=vregs + ["c_lnb"], writes=vregs)
            if samp:
                P.Dm("sp", "dma_start", out=self.dout["cmlp_v_s"], in_=vv, reads=vregs)
            P.I("act", "copy", vb[:], vv, reads=vregs, writes=["vb"])
            for hh in range(4):
                b = hh // 2
                P.I("pe", "matmul", bk[b][:, (hh % 2) * 256:(hh % 2 + 1) * 256], lhsT=ws_t[:, hh, :],
                                                          rhs=vb[:, hh * 256:(hh + 1) * 256], start=True, stop=True,
                     reads=[wreg, "vb"], writes=[f"PS{b}"])
            for hh in range(4):
                b = hh // 2
                P.I("dve", "scalar_tensor_tensor",
                    o[:, hh * 256:(hh + 1) * 256], bk[b][:, (hh % 2) * 256:(hh % 2 + 1) * 256], bs_t[:, hh:hh + 1],
                    zs[:, hh * 256:(hh + 1) * 256], ALU.add, ALU.mult,
                    reads=[f"PS{b}", breg, ("zs", hh // 2)], writes=["c_o"])
            self.transpose_f32(o, 8, oT, "c_o", "c_oT", banks=(2, 3))
            self.final_proj_ln(i, oT, "c_oT", w_out, "cw_out", 8)
        P.pop_scope()

    def pool(self, l):
        P = self.P
        TP, NT = self.TP, self.NT
        bk = self.bank
        d = self.din
        P.push_scope()
        w_in = P.sb([128, 8, 1024], BF16, "pw_in")
        w_out = P.sb([128, 8, 1024], BF16, "pw_out")
        w_grp = P.sb([128, 8, 256], BF16, "pw_grp")
        scl = P.sb([128, 8], F32, "p_scl")
        rc = P.sb([128, 4, 16], F32, "p_rc")
        BLK = min(4, TP)
        NB = BLK * 128
        HTb = P.sb([128, 8, NB], BF16, "p_HT")
        UT = P.sb([128, 8, 16 + NB], F32, "p_UT")
        SA = P.sb([128, max(16 + NB, 384)], F32, "p_SA")
        SB = P.sb([128, max(16 + NB, 384)], F32, "p_SB")
        ufm = P.sb([128, D], F32, "p_ufm")
        PT = P.sb([128, 8, NB], BF16, "p_PT")
        MT = P.sb([128, 8, NB], BF16, "p_MT")
        US = P.sb([128, 8, NSEQ, 24], F32, "p_US")
        stt = P.sb([120, 2, D], F32, "p_stt")
        utok = P.sb([128, D], F32, "p_utok")
        self.load_w(w_in, d["pool_w_in"], "pw_in")
        self.load_w(w_out, d["pool_w_out"], "pw_out")
        self.load_w(w_grp, d["pool_w_grp"].rearrange("g k n -> (g k) n"), "pw_grp")
        P.Dm("sp", "dma_start", out=scl[:], in_=d["pool_scaleT"], writes=["p_scl"])
        self.load_ln(self.ln1_g, self.ln1_b, l)
        for g in range(4):
            w = 2 ** (g + 1)
            P.I("dve", "tensor_scalar", rc[:, g, :], self.iota16[:], 1.0, float(w), ALU.add, ALU.min,
                 reads=["iota16"], writes=["p_rc"])
        P.I("dve", "reciprocal", rc[:].rearrange("p g t -> p (g t)"), rc[:].rearrange("p g t -> p (g t)"),
             reads=["p_rc"], writes=["p_rc"])
        P.I("dve", "memset", UT[:, :, 0:16], 0.0, writes=["p_UT"])

        def window(c, src3, lo_hi_views, first_block, is_sample):
            raise NotImplementedError

        def pooled_chunk(c, U, SAv, SBv, n0, PTout, first_block, ureg):
            g = c // 2
            lv = g + 1
            cur, curreg = U, ureg
            bufs = [(SAv, "p_SA"), (SBv, "p_SB")]
            for k in range(lv):
                sh = 2 ** k
                lo = 2 ** (k + 1)
                dst, dreg = bufs[k % 2]
                eng = "pool" if (k % 2 == 1 and cur is not U) else "dve"
                P.I(eng, "tensor_tensor",
                    dst(lo, None), cur(lo, None), cur(lo - sh, -sh), ALU.add,
                    reads=[curreg], writes=[dreg])
                cur, curreg = dst, dreg
            w = 2 ** lv
            P.I("dve", "scalar_tensor_tensor", PTout, cur(n0, None), 1.0 / w, U(n0, None), ALU.mult, ALU.subtract,
                 reads=[curreg, ureg], writes=["p_PT"])
            return cur, curreg

        nblk = (TP + BLK - 1) // BLK
        for blk in range(nblk):
            tiles = list(range(blk * BLK, min(TP, (blk + 1) * BLK)))
            nb = len(tiles) * 128
            self.make_hT(tiles, HTb, "p_HT")
            for c in range(8):
                b = 2 + c % 4
                for k in range(8):
                    P.I("pe", "matmul", bk[b][:, 0:nb], lhsT=w_in[:, k, c * 128:(c + 1) * 128], rhs=HTb[:, k, 0:nb],
                                                              start=(k == 0), stop=(k == 7),
                         reads=["pw_in", "p_HT"], writes=[f"PS{b}"])
                self.ev(UT[:, c, 16:16 + nb], bk[b][:, 0:nb], reads=[f"PS{b}"], writes=[("p_UT", c)])
            for c in range(8):
                g = c // 2
                U = lambda a, e, c=c: UT[:, c, a:(16 + nb + e) if e else 16 + nb]
                SAv = lambda a, e: SA[:, a:(16 + nb + e) if e else 16 + nb]
                SBv = lambda a, e: SB[:, a:(16 + nb + e) if e else 16 + nb]
                cur, curreg = pooled_chunk(c, U, SAv, SBv, 16, PT[:, c, 0:nb], blk == 0, ("p_UT", c))
                if blk == 0:
                    P.I("dve", "tensor_tensor", cur(16, None)[:, 0:16], cur(16, None)[:, 0:16], rc[:, g, :], ALU.mult,
                         reads=[curreg, "p_rc"], writes=[curreg])
                    P.I("dve", "tensor_tensor", PT[:, c, 0:16], cur(16, None)[:, 0:16], UT[:, c, 16:32], ALU.subtract,
                         reads=[curreg, ("p_UT", c)], writes=["p_PT"])
            if blk == nblk - 1:
                for c in range(8):
                    P.I("pe", "transpose", bk[0][0:16, c % 4 * 128:(c % 4 + 1) * 128], UT[:, c, nb:nb + 16], self.ident[:],
                         reads=[("p_UT", c), "ident"], writes=["PS0"])
                    if c % 4 == 3:
                        self.ev(utok[0:16, (c - 3) * 128:(c + 1) * 128], bk[0][0:16, :], reads=["PS0"], writes=["p_utok"])
                P.Dm("sp", "dma_start", out=self.dout["pool_p"], in_=utok[1:16, :], reads=["p_utok"])
            else:
                for c in range(8):
                    self.ev(UT[:, c, 1:16], UT[:, c, nb + 1:nb + 16], reads=[("p_UT", c)], writes=[("p_UT", c)], eng="pool")
            self.pool_tail(tiles, nb, PT, MT, w_grp, scl, w_out)
        for half in range(2):
            P.Dm("sp", "dma_start", out=stt[:, half, :], in_=d["state_pool"][half * 120:(half + 1) * 120, :],
                  writes=["p_stt"])
        P.Dm("sp", "dma_start", out=self.dout["pool_s"].rearrange("(s j) d -> s j d", j=15)[:, 0:7, :],
                                          in_=d["state_pool"].rearrange("(s j) d -> s j d", j=15)[:, 8:15, :], reads=[])
        for c in range(8):
            b = c % 2
            for half in range(2):
                P.I("pe", "transpose", bk[b][:, half * 120:(half + 1) * 120],
                                                                 stt[:, half, c * 128:(c + 1) * 128], self.ident[0:120, 0:120],
                     reads=["p_stt", "ident"], writes=[f"PS{b}"])
            self.ev(US[:, c, :, 1:16], bk[b][:, 0:240].rearrange("p (s j) -> p s j", j=15), reads=[f"PS{b}"], writes=[("p_US", c)])
        self.make_hT([TP], HTb, "p_HT")
        for c in range(8):
            b = 2 + c % 4
            for k in range(8):
                P.I("pe", "matmul", bk[b][:, 0:128], lhsT=w_in[:, k, c * 128:(c + 1) * 128], rhs=HTb[:, k, 0:128],
                                                          start=(k == 0), stop=(k == 7),
                     reads=["pw_in", "p_HT"], writes=[f"PS{b}"])
            self.ev(US[:, c, :, 16:24], bk[b][:, 0:128].rearrange("p (s t) -> p s t", t=8), reads=[f"PS{b}"], writes=[("p_US", c)])
        SA3 = SA[:, 0:NSEQ * 24].rearrange("p (s t) -> p s t", t=24)
        SB3 = SB[:, 0:NSEQ * 24].rearrange("p (s t) -> p s t", t=24)
        for c in range(8):
            U = lambda a, e, c=c: US[:, c, :, a:(24 + e) if e else 24]
            SAv = lambda a, e: SA3[:, :, a:(24 + e) if e else 24]
            SBv = lambda a, e: SB3[:, :, a:(24 + e) if e else 24]
            pooled_chunk(c, U, SAv, SBv, 16, PT[:, c, 0:128].rearrange("p (s t) -> p s t", t=8), False, ("p_US", c))
        for c in range(8):
            P.I("act", "copy", ufm[:, c * 128:(c + 1) * 128].rearrange("p (s t) -> p s t", t=8), US[:, c, :, 16:24],
                 reads=[("p_US", c)], writes=["p_ufm"])
        self.transpose_f32_fm(ufm, 8, utok, "p_ufm", "p_utok")
        for j in range(NSEQ):
            P.Dm("sp", "dma_start", out=self.dout["pool_s"][j * 15 + 7:j * 15 + 15, :], in_=utok[j * 8:(j + 1) * 8, :],
                  reads=["p_utok"])
        self.pool_tail([TP], 128, PT, MT, w_grp, scl, w_out)
        P.pop_scope()

    def transpose_f32_fm(self, src, nch, dst, src_reg, dst_reg, banks=(0, 1)):
        P = self.P
        for g in range(0, nch, 4):
            b = banks[(g // 4) % len(banks)]
            for c in range(4):
                P.I("pe", "transpose", self.bank[b][:, c * 128:(c + 1) * 128],
                                                                src[:, (g + c) * 128:(g + c + 1) * 128], self.ident[:],
                     reads=[src_reg, "ident"], writes=[f"PS{b}"])
            self.ev(dst[:, g * 128:(g + 4) * 128], self.bank[b], reads=[f"PS{b}"], writes=[dst_reg])

    def pool_tail(self, tiles, nb, PT, MT, w_grp, scl, w_out):
        P = self.P
        bk = self.bank
        for c in range(8):
            g, oc = c // 2, c % 2
            b = 2 + c % 4
            for kc in range(2):
                P.I("pe", "matmul", bk[b][:, 0:nb], lhsT=w_grp[:, g * 2 + kc, oc * 128:(oc + 1) * 128],
                                                                   rhs=PT[:, g * 2 + kc, 0:nb], start=(kc == 0), stop=(kc == 1),
                     reads=["pw_grp", "p_PT"], writes=[f"PS{b}"])
            P.I("dve", "tensor_scalar", MT[:, c, 0:nb], bk[b][:, 0:nb], scl[:, c:c + 1], None, ALU.mult,
                 reads=[f"PS{b}", "p_scl"], writes=["p_MT"])
        for n, i in enumerate(tiles):
            self.final_proj_ln(i, MT, "p_MT", w_out, "pw_out", 8, tok0=n * 128)


    def sin_turns(self, out, x, n, xreg, oreg, wi, wf, shift=0.0, eng="dve"):
        P = self.P
        e = eng
        if shift:
            P.I(e, "tensor_scalar", wf, x, shift, None, ALU.add, reads=[xreg], writes=["s_wf"])
            src, sreg = wf, "s_wf"
        else:
            src, sreg = x, xreg
        P.I(e, "tensor_copy", wi, src, reads=[sreg], writes=["s_wi"])
        P.I(e, "tensor_copy", out, wi, reads=["s_wi"], writes=[oreg])
        P.I(e, "tensor_tensor", out, src, out, ALU.subtract, reads=[sreg, oreg], writes=[oreg])
        P.I("dve", "scalar_tensor_tensor", out, out, 0.5, out, ALU.is_gt, ALU.subtract, reads=[oreg], writes=[oreg])
        P.I("dve", "scalar_tensor_tensor", out, out, 0.5, out, ALU.is_gt, ALU.subtract, reads=[oreg], writes=[oreg])
        P.I("act", "activation", out, out, AF.Sin, scale=2.0 * np.pi * (1.0 - 1e-6), reads=[oreg], writes=[oreg])

    def s5(self, l):
        P = self.P
        TP, NT = self.TP, self.NT
        bk = self.bank
        d = self.din
        BLK = min(4, TP)
        NB = BLK * 128
        P.push_scope()
        dcol = P.sb([128, 8], F32, "s_d")
        bglu = P.sb([128, 8], F32, "s_bglu")
        cst = P.sb([128, 12, 32], F32, "s_cst")
        RHO, RT, LBR, LBI, NLBI, FR, FI, T0, T1_, T2_, T3_, T4_ = [cst[:, j, :] for j in range(12)]
        wi32 = P.sb([128, 512], I32, "s_wi")
        wf32 = P.sb([128, 512], F32, "s_wf")
        BbT = P.sb([128, 32, 2, 128], BF16, "s_BbT")
        CT = P.sb([128, 32, 2, 128], BF16, "s_CT")
        Dg = P.sb([128, 8, 128], BF16, "s_Dg")
        HE = P.sb([128, 2, 32, 16], F32, "s_HE")
        H0 = P.sb([128, 2, 32, 16], F32, "s_H0")
        iotaS = P.sb([128, 512], F32, "s_iota")
        mask01 = P.sb([128, 128], F32, "s_m01")
        ones = P.sb([128, 512], F32, "s_ones")
        UTb = P.sb([128, 8, NB], BF16, "s_UT")
        GTb = P.sb([128, 8, NB], BF16, "s_GT")
        P.push_scope()
        aT = P.sb([128, 3, 32], F32, "s_aT")
        bT = P.sb([128, 2, 32, 16], F32, "s_bT")
        cT = P.sb([128, 2, 32, 16], F32, "s_cT")
        bb = P.sb([128, 2, 32, 16], F32, "s_bb")
        P.Dm("sp", "dma_start", out=aT[:], in_=d["s5_aT"], writes=["s_aT"])
        P.Dm("sp", "dma_start", out=bT[:], in_=d["s5_bT"], writes=["s_bT"])
        P.Dm("sp", "dma_start", out=cT[:], in_=d["s5_cT"], writes=["s_cT"])
        P.Dm("sp", "dma_start", out=dcol[:], in_=d["s5_dT"], writes=["s_d"])
        P.Dm("sp", "dma_start", out=bglu[:], in_=d["s5_b_gluT"], writes=["s_bglu"])
        self.load_ln(self.ln1_g, self.ln1_b, l)
        P.I("pool", "iota", iotaS[:], pattern=[[1, 512]], base=0, channel_multiplier=0,
                                      allow_small_or_imprecise_dtypes=True, writes=["s_iota"])
        P.I("dve", "memset", ones[:], 1.0, writes=["s_ones"])
        P.I("dve", "memset", mask01[:], 1.0, writes=["s_m01"])
        P.I("dve", "memset", mask01[:].rearrange("p (s t) -> p s t", t=8)[:, :, 0:1], 0.0, writes=["s_m01"])
        P.I("dve", "memset", HE[:], 0.0, writes=["s_HE"])
        A_RE, A_IM, LDT = aT[:, 0, :], aT[:, 1, :], aT[:, 2, :]
        C = "s_cst"

        def tt(out, a, b, op, eng="dve", extra=()):
            P.I(eng, "tensor_tensor", out, a, b, op, reads=[C, "s_aT"] + list(extra), writes=[C])
        P.I("act", "activation", T0, LDT, AF.Exp, reads=["s_aT"], writes=[C])
        tt(T1_, A_RE, T0, ALU.mult)
        P.I("act", "activation", RHO, T1_, AF.Exp, reads=[C], writes=[C])
        tt(T2_, A_IM, T0, ALU.mult)
        P.I("dve", "tensor_scalar", RT, T2_, 1.0 / (2.0 * np.pi), None, ALU.mult, reads=[C], writes=[C])
        self.sin_turns(T3_, RT, 32, C, C, wi32[:, 0:32], wf32[:, 0:32])
        self.sin_turns(T4_, RT, 32, C, C, wi32[:, 0:32], wf32[:, 0:32], shift=0.25)
        tt(LBR, RHO, T4_, ALU.mult)
        tt(LBI, RHO, T3_, ALU.mult)
        P.I("dve", "tensor_scalar", NLBI, LBI, -1.0, None, ALU.mult, reads=[C], writes=[C])
        tt(T0, A_RE, A_RE, ALU.mult)
        tt(T1_, A_IM, A_IM, ALU.mult)
        tt(T0, T0, T1_, ALU.add)
        P.I("dve", "reciprocal", T0, T0, reads=[C], writes=[C])
        P.I("dve", "tensor_scalar", T1_, LBR, -1.0, None, ALU.add, reads=[C], writes=[C])
        tt(T2_, T1_, A_RE, ALU.mult)
        tt(T3_, LBI, A_IM, ALU.mult)
        tt(T2_, T2_, T3_, ALU.add)
        tt(FR, T2_, T0, ALU.mult)
        tt(T2_, LBI, A_RE, ALU.mult)
        tt(T3_, T1_, A_IM, ALU.mult)
        tt(T2_, T2_, T3_, ALU.subtract)
        tt(FI, T2_, T0, ALU.mult)
        frb = FR.unsqueeze(2).to_broadcast([128, 32, 16])
        fib = FI.unsqueeze(2).to_broadcast([128, 32, 16])
        tmpb = P.sb([128, 32, 16], F32, "s_tmpb")
        P.I("dve", "tensor_tensor", bb[:, 0], frb, bT[:, 0], ALU.mult, reads=[C, "s_bT"], writes=["s_bb"])
        P.I("dve", "tensor_tensor", tmpb[:], fib, bT[:, 1], ALU.mult, reads=[C, "s_bT"], writes=["s_tmpb"])
        P.I("dve", "tensor_tensor", bb[:, 0], bb[:, 0], tmpb[:], ALU.subtract, reads=["s_bb", "s_tmpb"], writes=["s_bb"])
        P.I("dve", "tensor_tensor", bb[:, 1], frb, bT[:, 1], ALU.mult, reads=[C, "s_bT", "s_bb"], writes=["s_bb"])
        P.I("dve", "tensor_tensor", tmpb[:], fib, bT[:, 0], ALU.mult, reads=[C, "s_bT", "s_bb"], writes=["s_tmpb"])
        P.I("dve", "tensor_tensor", bb[:, 1], bb[:, 1], tmpb[:], ALU.add, reads=["s_bb", "s_tmpb"], writes=["s_bb"])
        pads = P.sb([128, 2, 4, 128], F32, "s_pads")
        P.I("dve", "memset", pads[:], 0.0, writes=["s_pads"])
        for gp in range(32):
            q = gp % 4
            for ri in range(2):
                for g2 in range(2):
                    P.I("pool", "tensor_copy",
                        pads[g2 * 64:(g2 + 1) * 64, ri, q, q * 32 + g2 * 16:q * 32 + g2 * 16 + 16],
                        bb[g2 * 64:(g2 + 1) * 64, ri, gp, :], reads=["s_bb", "s_pads"], writes=["s_pads"])
            b = gp % 2
            for ri in range(2):
                P.I("pe", "transpose", bk[b][:, ri * 128:(ri + 1) * 128], pads[:, ri, q, :], self.ident[:],
                     reads=["s_pads", "ident"], writes=[f"PS{b}"])
            self.ev(BbT[:, gp, :, :], bk[b][:, 0:256].rearrange("p (r n) -> p r n", r=2), reads=[f"PS{b}"], writes=["s_BbT"])
        P.I("dve", "memset", CT[:], 0.0, writes=["s_CT"])
        P.I("dve", "tensor_scalar", cT[:, 1], cT[:, 1], -1.0, None, ALU.mult, reads=["s_cT"], writes=["s_cT"])
        for q in range(4):
            for ri in range(2):
                for g2 in range(2):
                    dst = CT[g2 * 64:(g2 + 1) * 64, :, ri, :].rearrange("p (a q) r -> p a q r", q=4)[:, :, q, q * 32 + g2 * 16:q * 32 + g2 * 16 + 16]
                    src = cT[g2 * 64:(g2 + 1) * 64, ri].rearrange("p (a q) i -> p a q i", q=4)[:, :, q, :]
                    P.I("dve", "tensor_copy", dst, src, reads=["s_cT"], writes=["s_CT"])
        for c in range(8):
            P.I("dve", "tensor_scalar", Dg[:, c, :], self.ident[:], dcol[:, c:c + 1], None, ALU.mult,
                 reads=["ident", "s_d"], writes=["s_Dg"])
        st = P.sb([16, 2, 4096], F32, "s_st")
        P.Dm("sp", "dma_start", out=st[:, 0, :], in_=d["state_s5_re"], writes=["s_st"])
        P.Dm("sp", "dma_start", out=st[:, 1, :], in_=d["state_s5_im"], writes=["s_st"])
        for ri in range(2):
            for g8 in range(4):
                b = g8 % 2
                for j in range(8):
                    gp = g8 * 8 + j
                    P.I("pe", "transpose", bk[b][:, j * 16:(j + 1) * 16], st[:, ri, gp * 128:(gp + 1) * 128],
                                                                      self.ident[0:16, 0:16],
                         reads=["s_st", "ident"], writes=[f"PS{b}"])
                self.ev(H0[:, ri, g8 * 8:(g8 + 1) * 8, :], bk[b][:, 0:128].rearrange("p (g s) -> p g s", s=16),
                        reads=[f"PS{b}"], writes=["s_H0"])

        P.pop_scope()
        nblk = (TP + BLK - 1) // BLK
        blocks = [(list(range(bi * BLK, min(TP, (bi + 1) * BLK))), False) for bi in range(nblk)] + [([TP], True)]
        for bidx, (tiles, samp) in enumerate(blocks):
            N = len(tiles) * 128
            P.push_scope()
            w_in = P.sb([128, 8, 1024], BF16, "sw_in")
            HTb = P.sb([128, 8, NB], BF16, "s_HT")
            self.load_w(w_in, d["s5_w_in"], "sw_in")
            self.make_hT(tiles, HTb, "s_HT")
            for c in range(8):
                b = 2 + c % 4
                for k in range(8):
                    P.I("pe", "matmul", bk[b][:, 0:N], lhsT=w_in[:, k, c * 128:(c + 1) * 128], rhs=HTb[:, k, 0:N],
                                                              start=(k == 0), stop=(k == 7), reads=["sw_in", "s_HT"], writes=[f"PS{b}"])
                self.ev(UTb[:, c, 0:N], bk[b][:, 0:N], reads=[f"PS{b}"], writes=[("s_UT", c)])
            P.pop_scope()
            P.push_scope()
            tab = [P.sb([128, 3, 512], F32, f"s_tab{j}") for j in range(2)]
            wk4_2 = [[P.sb([128, 512], F32, f"s_w{j}_{z}") for j in range(4)] for z in range(2)]
            BR_2 = [P.sb([128, 512], F32, f"s_BR{z}") for z in range(2)]; BI_2 = [P.sb([128, 512], F32, f"s_BI{z}") for z in range(2)]
            GR_2 = [P.sb([128, 512], F32, f"s_GR{z}") for z in range(2)]; GI_2 = [P.sb([128, 512], F32, f"s_GI{z}") for z in range(2)]
            HRb = [P.sb([128, 512], BF16, f"s_HR{j}") for j in range(2)]
            HIb = [P.sb([128, 512], BF16, f"s_HI{j}") for j in range(2)]
            inj = P.sb([128, 4, 16], F32, "s_inj")
            ns = NSEQ if samp else 1

            def V(t, n=N):
                return t[:, 0:n].rearrange("p (s t) -> p s t", t=8) if samp else t[:, 0:n]

            def TV(t):
                return t[:, 0:8].unsqueeze(1).to_broadcast([128, NSEQ, 8]) if samp else t[:, 0:N]

            def starts(t):
                return t[:, 0:N].rearrange("p (s t) -> p s t", t=8)[:, :, 0] if samp else t[:, 0:1]

            def ends(t):
                return t[:, 0:N].rearrange("p (s t) -> p s t", t=8)[:, :, 7] if samp else t[:, N - 1:N]
            nloc = 8 if samp else N

            def X(eng, meth, reads, writes, *args, **kw):
                if eng == "sp_dma":
                    P.dma("sp", lambda h: getattr(h, meth)(*args, **kw), reads=reads, writes=writes)
                else:
                    P.op(eng, lambda h: getattr(h, meth)(*args, **kw), reads=reads, writes=writes)
            for c in range(8):
                by = c % 2
                X("pe", "matmul", ["s_Dg", ("s_UT", c)], [f"PS{by}"], bk[by][:, 0:N], lhsT=Dg[:, c, :], rhs=UTb[:, c, 0:N], start=True, stop=False)
                for q in range(4):
                    gp = 4 * c + q
                    sl = gp % 2
                    P.flush(6, "cv0")
                    wk4, BR, BI, GR, GI = wk4_2[sl], BR_2[sl], BI_2[sl], GR_2[sl], GI_2[sl]
                    Z = f"_{sl}"
                    tb, treg = tab[sl], f"s_tab{sl}"
                    cosT, sinT, rhoT = tb[:, 0, :], tb[:, 1, :], tb[:, 2, :]
                    X("dve", "tensor_scalar", ["s_iota", C], [treg], rhoT[:, 0:nloc], iotaS[:, 0:nloc], RT[:, gp:gp + 1], None, ALU.mult)
                    self.sin_turns(sinT[:, 0:nloc], rhoT[:, 0:nloc], nloc, treg, treg, wi32[:, 0:nloc], wf32[:, 0:nloc], eng="dve")
                    self.sin_turns(cosT[:, 0:nloc], rhoT[:, 0:nloc], nloc, treg, treg, wi32[:, 0:nloc], wf32[:, 0:nloc], shift=0.25, eng="dve")
                    if samp:
                        X("dve", "tensor_scalar", ["s_m01", C, treg], [treg], rhoT[:, 0:N], mask01[:, 0:N], RHO[:, gp:gp + 1], None, ALU.mult)
                    else:
                        X("dve", "tensor_scalar", ["s_ones", C, treg], [treg], rhoT[:, 0:N], ones[:, 0:N], RHO[:, gp:gp + 1], None, ALU.mult)
                    for ri in range(2):
                        X("pe", "matmul", ["s_BbT", ("s_UT", c)], [f"PS{2 + ri}"], bk[2 + ri][:, 0:N], lhsT=BbT[:, gp, ri, :], rhs=UTb[:, c, 0:N],
                          start=True, stop=True)
                    pr, pi = bk[2][:, 0:N], bk[3][:, 0:N]
                    if samp:
                        pr = pr.rearrange("p (s t) -> p s t", t=8); pi = pi.rearrange("p (s t) -> p s t", t=8)
                    X("dve", "tensor_tensor", ["PS2", treg], ["s_w0" + Z], V(wk4[0]), pr, TV(cosT), ALU.mult)
                    X("dve", "tensor_tensor", ["PS3", treg], ["s_w1" + Z], V(wk4[1]), pi, TV(sinT), ALU.mult)
                    X("dve", "tensor_tensor", ["PS3", treg], ["s_w2" + Z], V(wk4[2]), pi, TV(cosT), ALU.mult)
                    X("dve", "tensor_tensor", ["PS2", treg], ["s_w3" + Z], V(wk4[3]), pr, TV(sinT), ALU.mult)
                    X("dve", "tensor_tensor", ["s_w0" + Z, "s_w1" + Z], ["s_BR" + Z], BR[:, 0:N], wk4[0][:, 0:N], wk4[1][:, 0:N], ALU.add)
                    X("dve", "tensor_tensor", ["s_w2" + Z, "s_w3" + Z], ["s_BI" + Z], BI[:, 0:N], wk4[2][:, 0:N], wk4[3][:, 0:N], ALU.subtract)
                    if samp or bidx > 0:
                        hp = H0 if samp else HE
                        hreg = "s_H0" if samp else "s_HE"
                        hr, hi = hp[:, 0, gp, 0:ns], hp[:, 1, gp, 0:ns]
                        X("dve", "tensor_scalar", [hreg, C], ["s_inj"], inj[:, 0, 0:ns], hr, LBR[:, gp:gp + 1], None, ALU.mult)
                        X("dve", "scalar_tensor_tensor", [hreg, C, "s_inj"], ["s_inj"], inj[:, 0, 0:ns], hi, NLBI[:, gp:gp + 1], inj[:, 0, 0:ns], ALU.mult, ALU.add)
                        X("dve", "tensor_scalar", [hreg, C, "s_inj"], ["s_inj"], inj[:, 1, 0:ns], hi, LBR[:, gp:gp + 1], None, ALU.mult)
                        X("dve", "scalar_tensor_tensor", [hreg, C, "s_inj"], ["s_inj"], inj[:, 1, 0:ns], hr, LBI[:, gp:gp + 1], inj[:, 1, 0:ns], ALU.mult, ALU.add)
                        X("dve", "tensor_tensor", ["s_BR" + Z, "s_inj"], ["s_BR" + Z], starts(BR), starts(BR), inj[:, 0, 0:ns], ALU.add)
                        X("dve", "tensor_tensor", ["s_BI" + Z, "s_inj"], ["s_BI" + Z], starts(BI), starts(BI), inj[:, 1, 0:ns], ALU.add)
                    X("dve", "tensor_tensor_scan", [treg, "s_BR" + Z], ["s_GR" + Z], GR[:, 0:N], rhoT[:, 0:N], BR[:, 0:N], 0.0, ALU.mult, ALU.add)
                    X("dve", "tensor_tensor_scan", [treg, "s_BI" + Z], ["s_GI" + Z], GI[:, 0:N], rhoT[:, 0:N], BI[:, 0:N], 0.0, ALU.mult, ALU.add)
                    hs = gp % 2
                    X("dve", "tensor_tensor", ["s_GR" + Z, treg], ["s_w0" + Z], V(wk4[0]), V(GR), TV(cosT), ALU.mult)
                    X("dve", "tensor_tensor", ["s_GI" + Z, treg], ["s_w1" + Z], V(wk4[1]), V(GI), TV(sinT), ALU.mult)
                    X("dve", "tensor_tensor", ["s_GR" + Z, treg], ["s_w2" + Z], V(wk4[2]), V(GR), TV(sinT), ALU.mult)
                    X("dve", "tensor_tensor", ["s_GI" + Z, treg], ["s_w3" + Z], V(wk4[3]), V(GI), TV(cosT), ALU.mult)
                    X("dve", "tensor_tensor", ["s_w0" + Z, "s_w1" + Z], [f"s_HR{hs}"], HRb[hs][:, 0:N], wk4[0][:, 0:N], wk4[1][:, 0:N], ALU.subtract)
                    X("dve", "tensor_tensor", ["s_w2" + Z, "s_w3" + Z], [f"s_HI{hs}"], HIb[hs][:, 0:N], wk4[2][:, 0:N], wk4[3][:, 0:N], ALU.add)
                    X("dve", "tensor_tensor", ["s_w0" + Z, "s_w1" + Z], ["s_HE"], HE[:, 0, gp, 0:ns], ends(wk4[0]), ends(wk4[1]), ALU.subtract)
                    X("dve", "tensor_tensor", ["s_w2" + Z, "s_w3" + Z], ["s_HE"], HE[:, 1, gp, 0:ns], ends(wk4[2]), ends(wk4[3]), ALU.add)
                    X("pe", "matmul", ["s_CT", f"s_HR{hs}"], [f"PS{by}"], bk[by][:, 0:N], lhsT=CT[:, gp, 0, :], rhs=HRb[hs][:, 0:N], start=False, stop=False)
                    X("pe", "matmul", ["s_CT", f"s_HI{hs}"], [f"PS{by}"], bk[by][:, 0:N], lhsT=CT[:, gp, 1, :], rhs=HIb[hs][:, 0:N], start=False, stop=(q == 3))
                X("act", "activation", [f"PS{by}"], [("s_GT", c)], GTb[:, c, 0:N], bk[by][:, 0:N], AF.Gelu_apprx_tanh)
            P.pop_scope()
            if samp or bidx == nblk - 1:
                nm = ("s5_re_s", "s5_im_s") if samp else ("s5_re_p", "s5_im_p")
                P.push_scope()
                so = P.sb([16, 2, 4096], F32, "s_so")
                for ri in range(2):
                    if samp:
                        for g8 in range(8):
                            b = g8 % 2
                            for j in range(4):
                                gp = g8 * 4 + j
                                X("pe", "transpose", ["s_HE", "ident"], [f"PS{b}"], bk[b][0:16, j * 128:(j + 1) * 128], HE[:, ri, gp, :], self.ident[:])
                            self.ev(so[:, ri, g8 * 512:(g8 + 1) * 512], bk[b][0:16, :], reads=[f"PS{b}"], writes=["s_so"])
                        X("sp_dma", "dma_start", ["s_so"], [], out=self.dout[nm[ri]], in_=so[:, ri, :])
                    else:
                        X("pe", "transpose", ["s_HE", "ident"], [f"PS{ri}"], bk[ri][0:32, 0:128], HE[:, ri, :, 0], self.ident[:])
                        X("dve", "tensor_copy", [f"PS{ri}"], ["tmp"], self.tmp[0:32, ri * 128:(ri + 1) * 128], bk[ri][0:32, 0:128])
                        X("sp_dma", "dma_start", ["tmp"], [], out=self.dout[nm[ri]], in_=self.tmp[0:32, ri * 128:(ri + 1) * 128])
                P.pop_scope()
            P.push_scope()
            w_glu = P.sb([128, 8, 1024], BF16, "sw_glu")
            w_out = P.sb([128, 8, 1024], BF16, "sw_out")
            OT = P.sb([128, 8, NB], BF16, "s_OT")
            SG = [P.sb([128, 512], F32, f"s_SG{j}") for j in range(2)]
            self.load_w(w_glu, d["s5_w_glu"], "sw_glu")
            self.load_w(w_out, d["s5_w_out"], "sw_out")
            for oc in range(8):
                b = 2 + oc % 4
                for k in range(8):
                    P.I("pe", "matmul", bk[b][:, 0:N], lhsT=w_glu[:, k, oc * 128:(oc + 1) * 128], rhs=GTb[:, k, 0:N],
                                                               start=(k == 0), stop=(k == 7), reads=["sw_glu", ("s_GT", k)], writes=[f"PS{b}"])
                sg = SG[oc % 2]
                P.I("act", "activation", sg[:, 0:N], bk[b][:, 0:N], AF.Sigmoid, bias=bglu[:, oc:oc + 1],
                     reads=[f"PS{b}", "s_bglu"], writes=[f"s_SG{oc % 2}"])
                P.I("dve", "tensor_tensor", OT[:, oc, 0:N], GTb[:, oc, 0:N], sg[:, 0:N], ALU.mult,
                     reads=[("s_GT", oc), f"s_SG{oc % 2}"], writes=["s_OT"])
            for n, i in enumerate(tiles):
                self.final_proj_ln(i, OT, "s_OT", w_out, "sw_out", 8, tok0=n * 128)
            P.pop_scope()
        P.pop_scope()


    def ssd(self, l):
        P = self.P
        nc = self.nc
        TP, NT = self.TP, self.NT
        bk = self.bank
        d = self.din
        LP = TP * 128
        BLK = min(4, TP)
        NB = BLK * 128
        scr_z = nc.dram_tensor("scr_z", [NT * 128, 2048], F32, kind="Internal").ap()
        scr_x = nc.dram_tensor("scr_x", [24, 128, 3 + LP], F32, kind="Internal").ap()
        scr_xs = nc.dram_tensor("scr_xs", [24, 128, NSEQ * 11], F32, kind="Internal").ap()
        P.push_scope()
        DT = P.sb([128, NT, 32], F32, "d_DT")
        hd = P.sb([128, 96], F32, "d_hd")
        cwT = P.sb([128, 24, 4], F32, "d_cw")
        cbT = P.sb([128, 24], F32, "d_cb")
        ngT = P.sb([128, 16], F32, "d_ng")
        P.Dm("sp", "dma_start", out=hd[:], in_=d["ssd_hd"][0:1, :].to_broadcast([128, 96]), writes=["d_hd"])
        P.Dm("sp", "dma_start", out=cwT[:], in_=d["ssd_conv_wT"], writes=["d_cw"])
        P.Dm("sp", "dma_start", out=cbT[:], in_=d["ssd_conv_bT"], writes=["d_cb"])
        P.Dm("sp", "dma_start", out=ngT[:], in_=d["ssd_norm_gT"], writes=["d_ng"])
        P.I("act", "activation", hd[:, 32:64], hd[:, 32:64], AF.Exp, reads=["d_hd"], writes=["d_hd"])
        P.I("dve", "tensor_scalar", hd[:, 32:64], hd[:, 32:64], -1.0, None, ALU.mult, reads=["d_hd"], writes=["d_hd"])
        self.load_ln(self.ln1_g, self.ln1_b, l)
        P.push_scope()
        w_in = P.sb([128, 8, 5152], BF16, "dw_in")
        HTb = P.sb([128, 8, NB], BF16, "d_HT")
        stage = [P.sb([128, 512], F32, f"d_stg{j}") for j in range(2)]
        zst = P.sb([128, 2048], F32, "d_zst")
        cstt = P.sb([48, 3072], F32, "d_cst")
        self.load_w(w_in, d["ssd_w_in"], "dw_in")
        P.I("dve", "memset", zst[:, 0:72], 0.0, writes=["d_zst"])
        P.Dm("sp", "dma_start", out=scr_x[:, :, 0:3].rearrange("c p t -> p c t"), in_=zst[:, 0:72].rearrange("p (c t) -> p c t", t=3),
             reads=["d_zst"], writes=["scr_x"])
        P.Dm("sp", "dma_start", out=cstt[:], in_=d["state_ssd_conv"], writes=["d_cst"])
        for c in range(24):
            b = c % 2
            P.I("pe", "transpose", bk[b][:, 0:48], cstt[:, c * 128:(c + 1) * 128], self.ident[0:48, 0:48],
                reads=["d_cst", "ident"], writes=[f"PS{b}"])
            sg = stage[c % 2]
            self.ev(sg[:, 0:48], bk[b][:, 0:48], reads=[f"PS{b}"], writes=[f"d_stg{c % 2}"])
            P.Dm("sp", "dma_start", out=scr_xs[c].rearrange("p (s t) -> p s t", t=11)[:, :, 0:3],
                 in_=sg[:, 0:48].rearrange("p (s t) -> p s t", t=3), reads=[f"d_stg{c % 2}"], writes=["scr_xs"])
        nblk = (TP + BLK - 1) // BLK
        blocks = [(list(range(bi * BLK, min(TP, (bi + 1) * BLK))), False) for bi in range(nblk)] + [([TP], True)]
        for bidx, (tiles, samp) in enumerate(blocks):
            N = len(tiles) * 128
            tok0 = tiles[0] * 128
            self.make_hT(tiles, HTb, "d_HT")
            for c in range(24):
                b = 2 + c % 4
                for k in range(8):
                    P.I("pe", "matmul", bk[b][:, 0:N], lhsT=w_in[:, k, 2048 + c * 128:2048 + (c + 1) * 128], rhs=HTb[:, k, 0:N],
                        start=(k == 0), stop=(k == 7), reads=["dw_in", "d_HT"], writes=[f"PS{b}"])
                sg = stage[c % 2]
                self.ev(sg[:, 0:N], bk[b][:, 0:N], reads=[f"PS{b}"], writes=[f"d_stg{c % 2}"])
                if samp:
                    P.Dm("sp", "dma_start", out=scr_xs[c].rearrange("p (s t) -> p s t", t=11)[:, :, 3:11],
                         in_=sg[:, 0:128].rearrange("p (s t) -> p s t", t=8), reads=[f"d_stg{c % 2}"], writes=["scr_xs"])
                else:
                    P.Dm("sp", "dma_start", out=scr_x[c, :, 3 + tok0:3 + tok0 + N], in_=sg[:, 0:N],
                         reads=[f"d_stg{c % 2}"], writes=["scr_x"])
            for n, i in enumerate(tiles):
                hTi = HTb[:, :, n * 128:(n + 1) * 128]
                for nn in range(4):
                    b = 2 + nn
                    for k in range(8):
                        P.I("pe", "matmul", bk[b], lhsT=hTi[:, k, :], rhs=w_in[:, k, nn * 512:(nn + 1) * 512],
                            start=(k == 0), stop=(k == 7), reads=["dw_in", "d_HT"], writes=[f"PS{b}"])
                    self.ev(zst[:, nn * 512:(nn + 1) * 512], bk[b], reads=[f"PS{b}"], writes=["d_zst"])
                P.Dm("sp", "dma_start", out=scr_z[i * 128:(i + 1) * 128, :], in_=zst[:], reads=["d_zst"], writes=["scr_z"])
                for k in range(8):
                    P.I("pe", "matmul", bk[6][:, 0:32], lhsT=hTi[:, k, :], rhs=w_in[:, k, 5120:5152],
                        start=(k == 0), stop=(k == 7), reads=["dw_in", "d_HT"], writes=["PS6"])
                P.I("dve", "tensor_tensor", DT[:, i, :], bk[6][:, 0:32], hd[:, 0:32], ALU.add, reads=["PS6", "d_hd"], writes=["d_DT"])
                P.I("act", "activation", DT[:, i, :], DT[:, i, :], AF.Exp, reads=["d_DT"], writes=["d_DT"])
                P.I("act", "activation", DT[:, i, :], DT[:, i, :], AF.Ln, bias=self.one_t[:, 0:1], reads=["d_DT", "one"], writes=["d_DT"])
                if samp or i == TP - 1:
                    for hf in range(2):
                        for nn in range(3):
                            b = 2 + nn
                            c0 = 2048 + hf * 1536 + nn * 512
                            for k in range(8):
                                P.I("pe", "matmul", bk[b], lhsT=hTi[:, k, :], rhs=w_in[:, k, c0:c0 + 512],
                                    start=(k == 0), stop=(k == 7), reads=["dw_in", "d_HT"], writes=[f"PS{b}"])
                            self.ev(zst[:, nn * 512:(nn + 1) * 512], bk[b], reads=[f"PS{b}"], writes=["d_zst"])
                        if samp:
                            for j in range(NSEQ):
                                P.Dm("sp", "dma_start", out=self.dout["conv_s"][j * 3:j * 3 + 3, hf * 1536:(hf + 1) * 1536],
                                     in_=zst[j * 8 + 5:j * 8 + 8, 0:1536], reads=["d_zst"])
                        else:
                            P.Dm("sp", "dma_start", out=self.dout["conv_p"][:, hf * 1536:(hf + 1) * 1536], in_=zst[125:128, 0:1536],
                                 reads=["d_zst"])
        P.pop_scope()
        P.push_scope()
        w_out = P.sb([128, 16, 1024], BF16, "dw_out")
        self.load_w(w_out, d["ssd_w_out"], "dw_out")
        for k in range(16):
            P.I("dve", "tensor_scalar", w_out[:, k, :], w_out[:, k, :], ngT[:, k:k + 1], None, ALU.mult,
                reads=["dw_out", "d_ng"], writes=["dw_out"])
        mk = P.sb([128, 8, 128], F32, "d_mk")
        ONES, TRI, STRICT, BONES, TRIS = [mk[:, j, :] for j in range(5)]
        mki = P.sb([128, 2, 128], I32, "d_mki")
        mcol = P.sb([128, 24], F32, "d_mcol")
        M = "d_mk"
        P.I("dve", "memset", ONES, 1.0, writes=[M])
        P.I("pool", "affine_select", TRI, ONES, pattern=[[1, 128]], compare_op=ALU.is_ge, fill=0.0, base=0, channel_multiplier=-1,
            reads=[M], writes=[M])
        P.I("pool", "affine_select", STRICT, ONES, pattern=[[-1, 128]], compare_op=ALU.is_gt, fill=0.0, base=0, channel_multiplier=1,
            reads=[M], writes=[M])
        P.I("pool", "iota", mki[:, 0, :], pattern=[[1, 128]], base=0, channel_multiplier=0, reads=[], writes=["d_mki"])
        P.I("pool", "iota", mki[:, 1, :], pattern=[[0, 128]], base=0, channel_multiplier=1, reads=["d_mki"], writes=["d_mki"])
        P.I("dve", "tensor_single_scalar", mki[:].rearrange("p a n -> p (a n)"), mki[:].rearrange("p a n -> p (a n)"), 3, ALU.arith_shift_right,
            reads=["d_mki"], writes=["d_mki"])
        P.I("dve", "tensor_copy", mk[:, 5:7, :], mki[:], reads=["d_mki"], writes=[M])
        P.I("dve", "tensor_tensor", BONES, mk[:, 5, :], mk[:, 6, :], ALU.is_equal, reads=[M], writes=[M])
        P.I("dve", "tensor_tensor", TRIS, BONES, TRI, ALU.mult, reads=[M], writes=[M])
        P.I("dve", "tensor_tensor", mcol[:, 0:16], self.iota16[:], mk[:, 6, 0:16], ALU.is_equal, reads=[M, "iota16"], writes=["d_mcol"])
        P.I("dve", "tensor_single_scalar", mcol[:, 16:17], mk[:, 6, 0:1], 8.0, ALU.is_lt, reads=[M, "d_mcol"], writes=["d_mcol"])
        P.I("dve", "tensor_single_scalar", mcol[:, 17:18], mk[:, 6, 0:1], 8.0, ALU.is_ge, reads=[M, "d_mcol"], writes=["d_mcol"])
        XB = P.sb([128, 8, 176], F32, "d_XB")
        CA2 = [P.sb([128, 128], F32, f"d_CA{j}") for j in range(2)]
        XCf = P.sb([128, 8, 128], F32, "d_XCf")
        BTb = P.sb([128, 4, 128], BF16, "d_BTb")
        CTb = P.sb([128, 4, 128], BF16, "d_CTb")
        Xb = P.sb([128, 2048], BF16, "d_Xb")
        Btb = P.sb([128, 512], BF16, "d_Btb")
        Xw = P.sb([128, 2048], BF16, "d_Xw")
        Y = P.sb([128, 2048], F32, "d_Y")
        zt = P.sb([128, 2048], F32, "d_zt")
        CBm = P.sb([128, 4, 128], F32, "d_CBm")
        R16 = P.sb([128, 16, 128], F32, "d_R16")
        DEC = [P.sb([128, 128], F32, f"d_DEC{j}") for j in range(4)]
        MTb = [P.sb([128, 128], BF16, f"d_MT{j}") for j in range(4)]
        YT = P.sb([128, 16, 128], BF16, "d_YT")
        sv = P.sb([128, 8, 32], F32, "d_sv")
        DTA, ACS, ALAST, EA, DE, WSC, CDP, STMP = [sv[:, j, :] for j in range(8)]
        ss = P.sb([128, 8], F32, "d_ss")
        S = "d_sv"
        A32, D32 = hd[:, 32:64], hd[:, 64:96]

        def bc(a):
            return a.unsqueeze(2).to_broadcast([128, 32, 64])

        def v3(t):
            return t[:, :].rearrange("p (h e) -> p h e", e=64)
        ps4 = lambda b0: self.psum[:, b0:b0 + 4, :].rearrange("p a n -> p (a n)")

        def tile_front(i, samp):
            tri = TRIS if samp else TRI
            tok0 = i * 128
            for cg in range(3):
                if samp:
                    P.Dm("sp", "dma_start", out=XB[:, :, 0:176], in_=scr_xs[cg * 8:(cg + 1) * 8].rearrange("c p t -> p c t"),
                         reads=["scr_xs"], writes=["d_XB"])
                else:
                    P.Dm("sp", "dma_start", out=XB[:, :, 0:131], in_=scr_x[cg * 8:(cg + 1) * 8, :, tok0:tok0 + 131].rearrange("c p t -> p c t"),
                         reads=["scr_x"], writes=["d_XB"])
                for cc in range(8):
                    c = cg * 8 + cc
                    CA, CAr = CA2[cc % 2], f"d_CA{cc % 2}"
                    if samp:
                        xv = lambda k: XB[:, cc, :].rearrange("p (s t) -> p s t", t=11)[:, :, k:k + 8]
                        cav = CA[:, :].rearrange("p (s t) -> p s t", t=8)
                    else:
                        xv = lambda k: XB[:, cc, k:k + 128]
                        cav = CA[:, :]
                    P.I("dve", "tensor_scalar", cav, xv(0), cwT[:, c, 0:1], None, ALU.mult, reads=["d_XB", "d_cw"], writes=[CAr])
                    for k in range(1, 4):
                        P.I("dve", "scalar_tensor_tensor", cav, xv(k), cwT[:, c, k:k + 1], cav, ALU.mult, ALU.add,
                            reads=["d_XB", "d_cw", CAr], writes=[CAr])
                    if cg < 2:
                        P.I("act", "activation", XCf[:, cc, :], CA[:, :], AF.Silu, bias=cbT[:, c:c + 1], reads=[CAr, "d_cb"], writes=["d_XCf"])
                    else:
                        if cc < 4:
                            P.I("act", "activation", XCf[:, cc, :], CA[:, :], AF.Silu, bias=cbT[:, c:c + 1], reads=[CAr, "d_cb"], writes=["d_XCf"])
                            P.I("dve", "tensor_copy", BTb[:, cc, :], XCf[:, cc, :], reads=["d_XCf"], writes=["d_BTb"])
                        else:
                            P.I("act", "activation", CTb[:, cc - 4, :], CA[:, :], AF.Silu, bias=cbT[:, c:c + 1], reads=[CAr, "d_cb"], writes=["d_CTb"])
                if cg < 2:
                    for g4 in range(2):
                        b = g4
                        for c4 in range(4):
                            P.I("pe", "transpose", bk[b][:, c4 * 128:(c4 + 1) * 128], XCf[:, g4 * 4 + c4, :], self.ident[:],
                                reads=["d_XCf", "ident"], writes=[f"PS{b}"])
                        self.ev(Xb[:, cg * 1024 + g4 * 512:cg * 1024 + (g4 + 1) * 512], bk[b], reads=[f"PS{b}"], writes=["d_Xb"])
                else:
                    for c4 in range(4):
                        P.I("pe", "transpose", bk[0][:, c4 * 128:(c4 + 1) * 128], XCf[:, c4, :], self.ident[:],
                            reads=["d_XCf", "ident"], writes=["PS0"])
                    self.ev(Btb[:, :], bk[0], reads=["PS0"], writes=["d_Btb"])
            dtv = DT[:, i, :]
            P.I("dve", "tensor_tensor", DTA, dtv, A32, ALU.mult, reads=["d_DT", "d_hd"], writes=[S])
            P.I("pe", "matmul", bk[4][:, 0:32], lhsT=tri, rhs=DTA, start=True, stop=True, reads=[M, S], writes=["PS4"])
            P.I("pe", "matmul", bk[4][:, 32:64], lhsT=(BONES if samp else ONES), rhs=DTA, start=True, stop=True, reads=[M, S], writes=["PS4"])
            P.I("dve", "tensor_copy", sv[:, 1:3, :], bk[4][:, 0:64].rearrange("p (a n) -> p a n", a=2), reads=["PS4"], writes=[S])
            P.I("act", "activation", EA, ACS, AF.Exp, reads=[S], writes=[S])
            P.I("dve", "tensor_tensor", STMP, ALAST, ACS, ALU.subtract, reads=[S], writes=[S])
            P.I("act", "activation", DE, STMP, AF.Exp, reads=[S], writes=[S])
            P.I("dve", "tensor_tensor", WSC, dtv, DE, ALU.mult, reads=[S, "d_DT"], writes=[S])
            P.I("act", "activation", CDP, ALAST, AF.Exp, reads=[S], writes=[S])
            P.I("dve", "tensor_tensor", v3(Xw), v3(Xb), bc(WSC), ALU.mult, reads=["d_Xb", S], writes=["d_Xw"])
            for g in range(4):
                P.I("pe", "matmul", bk[5][:, g * 128:(g + 1) * 128], lhsT=BTb[:, g, :], rhs=CTb[:, g, :], start=True, stop=True,
                    reads=["d_BTb", "d_CTb"], writes=["PS5"])
            P.I("dve", "tensor_tensor", CBm[:], bk[5].rearrange("p (g n) -> p g n", g=4), tri.unsqueeze(1).to_broadcast([128, 4, 128]), ALU.mult,
                reads=["PS5", M], writes=["d_CBm"])

        def tile_back(i, samp, yoff_ap, yoff_regs):
            tri = TRIS if samp else TRI
            dtv = DT[:, i, :]
            P.I("dve", "tensor_tensor", v3(Y), v3(Xb), bc(D32), ALU.mult, reads=["d_Xb", "d_hd"], writes=["d_Y"])
            P.I("dve", "tensor_tensor", v3(zt), yoff_ap.rearrange("p (h e) -> p h e", e=64), bc(EA), ALU.mult,
                reads=list(yoff_regs) + [S], writes=["d_zt"])
            P.I("pool", "tensor_tensor", Y[:], Y[:], zt[:], ALU.add, reads=["d_Y", "d_zt"], writes=["d_Y"])
            for hh in range(32):
                g = hh // 8
                j = hh % 4
                if hh % 16 == 0:
                    P.I("dve", "tensor_tensor", R16[:], DTA[:, hh:hh + 16].unsqueeze(2).to_broadcast([128, 16, 128]),
                        tri.unsqueeze(1).to_broadcast([128, 16, 128]), ALU.mult, reads=[M, S], writes=["d_R16"])
                P.I("pe", "matmul", bk[4 + j][:, 0:128], lhsT=STRICT, rhs=R16[:, hh % 16, :], start=True, stop=True, reads=[M, "d_R16"], writes=[f"PS{4 + j}"])
                P.I("act", "activation", DEC[j][:], bk[4 + j][:, 0:128], AF.Exp, reads=[f"PS{4 + j}"], writes=[f"d_DEC{j}"])
                P.I("dve", "scalar_tensor_tensor", MTb[j][:], DEC[j][:], dtv[:, hh:hh + 1], CBm[:, g, :], ALU.mult, ALU.mult,
                    reads=[f"d_DEC{j}", "d_DT", "d_CBm"], writes=[f"d_MT{j}"])
                P.I("pe", "matmul", bk[g][:, (hh % 8) * 64:(hh % 8 + 1) * 64], lhsT=MTb[j][:], rhs=Xb[:, hh * 64:(hh + 1) * 64], start=True, stop=True,
                    reads=[f"d_MT{j}", "d_Xb"], writes=[f"PS{g}"])
            P.I("dve", "tensor_tensor", Y[:], Y[:], ps4(0), ALU.add, reads=["d_Y", "PS0", "PS1", "PS2", "PS3"], writes=["d_Y"])
            P.Dm("sp", "dma_start", out=zt[:], in_=scr_z[i * 128:(i + 1) * 128, :], reads=["scr_z"], writes=["d_zt"])
            P.I("act", "activation", zt[:], zt[:], AF.Silu, reads=["d_zt"], writes=["d_zt"])
            P.I("dve", "tensor_tensor", Y[:], Y[:], zt[:], ALU.mult, reads=["d_Y", "d_zt"], writes=["d_Y"])
            for g in range(4):
                P.I("act", "activation", zt[:, g * 512:(g + 1) * 512], Y[:, g * 512:(g + 1) * 512], AF.Square, accum_out=ss[:, g:g + 1],
                    reads=["d_Y", "d_zt", "d_ss"], writes=["d_zt", "d_ss"])
            P.I("act", "activation", ss[:, 4:8], ss[:, 0:4], AF.Sqrt, bias=self.eps_t[:, 0:1], scale=1.0 / 512.0, reads=["d_ss", "eps"], writes=["d_ss"])
            P.I("dve", "reciprocal", ss[:, 4:8], ss[:, 4:8], reads=["d_ss"], writes=["d_ss"])
            for g in range(4):
                P.I("dve", "tensor_scalar", Y[:, g * 512:(g + 1) * 512], Y[:, g * 512:(g + 1) * 512], ss[:, 4 + g:5 + g], None, ALU.mult,
                    reads=["d_Y", "d_ss"], writes=["d_Y"])
            self.transpose_f32(Y, 16, YT, "d_Y", "d_YT", banks=(0, 1))
            self.final_proj_ln(i, YT, "d_YT", w_out, "dw_out", 16)

        P.push_scope()
        h0 = P.sb([128, 16, 128], F32, "d_h0")
        h0T = [P.sb([128, 2048], BF16, f"d_h0T{j}") for j in range(2)]
        Bm = P.sb([128, 512], BF16, "d_Bm")
        Ej = P.sb([128, 128], F32, "d_Ej")
        cdb = P.sb([128, NSEQ, 32], F32, "d_cdb")
        cdc = P.sb([128, NSEQ, 16], F32, "d_cdc")
        tile_front(TP, True)
        for j in range(NSEQ):
            P.I("dve", "tensor_scalar", Ej[:], ONES, mcol[:, j:j + 1], None, ALU.mult, reads=[M, "d_mcol"], writes=["d_Ej"])
            P.I("pe", "matmul", bk[4][:, j * 32:(j + 1) * 32], lhsT=Ej[:], rhs=DTA, start=True, stop=True, reads=["d_Ej", S], writes=["PS4"])
        P.I("act", "activation", cdb[:].rearrange("p j h -> p (j h)"), bk[4], AF.Exp, reads=["PS4"], writes=["d_cdb"])
        cdb4 = cdb[:].rearrange("p j (r two) -> p j r two", two=2)
        P.I("dve", "tensor_scalar", cdc[:], cdb4[:, :, :, 0], mcol[:, 16:17], None, ALU.mult, reads=["d_cdb", "d_mcol"], writes=["d_cdc"])
        P.I("dve", "scalar_tensor_tensor", cdc[:], cdb4[:, :, :, 1], mcol[:, 17:18], cdc[:], ALU.mult, ALU.add,
            reads=["d_cdb", "d_mcol", "d_cdc"], writes=["d_cdc"])
        P.I("dve", "memset", zt[:], 0.0, writes=["d_zt"])
        for j in range(NSEQ):
            hT_j = h0T[j % 2]
            hreg = f"d_h0T{j % 2}"
            P.Dm("sp", "dma_start", out=h0[:], in_=d["state_ssd"][j].rearrange("(r p) n -> p r n", p=128), writes=["d_h0"])
            self.transpose_f32(h0[:].rearrange("p r n -> p (r n)"), 16, hT_j[:, :].rearrange("p (r m) -> p r m", m=128), "d_h0", hreg, banks=(0, 1))
            for g in range(4):
                b = 2 + g % 2
                P.I("pe", "matmul", bk[b], lhsT=CTb[:, g, :], rhs=hT_j[:, g * 512:(g + 1) * 512], start=True, stop=True,
                    reads=["d_CTb", hreg], writes=[f"PS{b}"])
                P.I("dve", "scalar_tensor_tensor", zt[:, g * 512:(g + 1) * 512], bk[b], mcol[:, j:j + 1], zt[:, g * 512:(g + 1) * 512], ALU.mult, ALU.add,
                    reads=[f"PS{b}", "d_mcol", "d_zt"], writes=["d_zt"])
            P.I("pool", "tensor_scalar", Bm[:], Btb[:], mcol[:, j:j + 1], None, ALU.mult, reads=["d_Btb", "d_mcol"], writes=["d_Bm"])
            for r in range(16):
                b = 4 + r // 4
                P.I("pe", "matmul", bk[b][:, (r % 4) * 128:(r % 4 + 1) * 128], lhsT=Xw[:, r * 128:(r + 1) * 128], rhs=Bm[:, (r // 4) * 128:(r // 4 + 1) * 128],
                    start=True, stop=True, reads=["d_Xw", "d_Bm"], writes=[f"PS{b}"])
            P.I("dve", "tensor_tensor", h0[:], h0[:], cdc[:, j, :].unsqueeze(2).to_broadcast([128, 16, 128]), ALU.mult,
                reads=["d_h0", "d_cdc"], writes=["d_h0"])
            P.I("dve", "tensor_tensor", h0[:].rearrange("p r n -> p (r n)"), h0[:].rearrange("p r n -> p (r n)"), ps4(4), ALU.add,
                reads=["d_h0", "PS4", "PS5", "PS6", "PS7"], writes=["d_h0"])
            P.Dm("sp", "dma_start", out=self.dout["ssd_s"][j * 2048:(j + 1) * 2048, :].rearrange("(r p) n -> p r n", p=128), in_=h0[:], reads=["d_h0"])
        P.I("dve", "tensor_copy", Y[:], zt[:], reads=["d_zt"], writes=["d_Y"])
        P.I("pool", "tensor_copy", h0[:].rearrange("p r n -> p (r n)"), Y[:], reads=["d_Y"], writes=["d_h0"])
        tile_back(TP, True, h0[:].rearrange("p r n -> p (r n)"), ["d_h0"])
        P.pop_scope()
        P.push_scope()
        hT = P.sb([128, 2048], F32, "d_hT")
        hTb = P.sb([128, 2048], BF16, "d_hTb")
        P.I("dve", "memset", hT[:], 0.0, writes=["d_hT"])
        P.I("dve", "memset", hTb[:], 0.0, writes=["d_hTb"])
        for i in range(TP):
            tile_front(i, False)
            for g in range(4):
                P.I("pe", "matmul", bk[g], lhsT=CTb[:, g, :], rhs=hTb[:, g * 512:(g + 1) * 512], start=True, stop=True,
                    reads=["d_CTb", "d_hTb"], writes=[f"PS{g}"])
            for g in range(4):
                P.I("pe", "matmul", bk[4 + g], lhsT=Btb[:, g * 128:(g + 1) * 128], rhs=Xw[:, g * 512:(g + 1) * 512], start=True, stop=True,
                    reads=["d_Btb", "d_Xw"], writes=[f"PS{4 + g}"])
            P.I("dve", "tensor_tensor", v3(hT), v3(hT), bc(CDP), ALU.mult, reads=["d_hT", S], writes=["d_hT"])
            P.I("dve", "tensor_tensor", hT[:], hT[:], ps4(4), ALU.add, reads=["d_hT", "PS4", "PS5", "PS6", "PS7"], writes=["d_hT"])
            tile_back(i, False, ps4(0), ["PS0", "PS1", "PS2", "PS3"])
            P.I("act", "copy", hTb[:], hT[:], reads=["d_hT"], writes=["d_hTb"])
        self.transpose_f32(hT, 16, Y[:, :].rearrange("p (r n) -> p r n", n=128), "d_hT", "d_Y", banks=(0, 1))
        P.Dm("sp", "dma_start", out=self.dout["ssd_p"].rearrange("(r p) n -> p r n", p=128), in_=Y[:, :].rearrange("p (r n) -> p r n", n=128),
             reads=["d_Y"])
        P.pop_scope()
        P.pop_scope()
        P.pop_scope()

    def build(self):
        P = self.P
        self.setup()
        self.eps_t = P.sb([128, 1], F32, "eps")
        P.I("dve", "memset", self.eps_t[:], LN_EPS, writes=["eps"])
        self.one_t = P.sb([128, 1], F32, "one")
        P.I("dve", "memset", self.one_t[:], 1.0, writes=["one"])
        if "ffn" in self.dbg:
            self.outp("dbg_ffn", [128, D])
            self.outp("dbg_idx", [128, 128], I32)
            self.outp("dbg_act", [128, 128]); self.outp("dbg_wgt", [128, 128]); self.outp("dbg_gate", [128, 128])
        self.load_ln_dummy = None
        if self.do_peer:
            stg0 = self.stg0 = [P.sb([128, D], F32, "cv0_s")]
            cb0 = self.cb0 = [P.sb([128, D], BF16, "cv0_b")]
            P.defer_begin("cv0")
            self.convert_tables(self.layers[0], stg0, cb0)
            P.defer_end()
        for l in self.layers:
            if self.do_mix:
                getattr(self, ("s5", "pool", "cmlp", "ssd")[l])(l)
            if self.do_peer:
                self.peer(l)
        self.finish()
        return self.nc


def make_in_map(inp, c, names, TP=16):
    f = np.ascontiguousarray
    sl = slice(c * NSEQ, (c + 1) * NSEQ)
    m = {}
    for n in names:
        if n == "x_p":
            m[n] = f(inp["x_prompt"][c, :TP * 128])
        elif n == "x_s":
            m[n] = f(inp["x_sample"][sl].reshape(128, D))
        elif n == "peer_w_q":
            m[n] = f(inp["peer_w_q"])
        elif n == "peer_keysT":
            m[n] = f(inp["peer_keys"].transpose(0, 4, 1, 2, 3).reshape(4, 128, 2048))
        elif n in ("peer_u", "peer_v"):
            m[n] = f(inp[n].reshape(4 * NEXP, D))
        elif n == "s5_aT":
            def lay(a):
                return a.reshape(32, 2, 64).transpose(1, 2, 0).reshape(128, 32)
            ld = np.broadcast_to(inp["s5_log_dt"][:, None], (64, 64))
            m[n] = f(np.stack([lay(inp["s5_a_re"]), lay(inp["s5_a_im"]), lay(ld)], axis=1))
        elif n == "s5_bT":
            def layb(b):
                return b.reshape(32, 2, 64, 16).transpose(1, 2, 0, 3).reshape(128, 32, 16)
            m[n] = f(np.stack([layb(inp["s5_b_re"]), layb(inp["s5_b_im"])], axis=1))
        elif n == "s5_cT":
            def layc(cc):
                return cc.reshape(32, 2, 16, 64).transpose(1, 3, 0, 2).reshape(128, 32, 16)
            m[n] = f(np.stack([layc(inp["s5_c_re"]), layc(inp["s5_c_im"])], axis=1))
        elif n == "s5_dT":
            m[n] = f(inp["s5_d"].reshape(8, 128).T)
        elif n == "s5_b_gluT":
            m[n] = f(inp["s5_b_glu"].reshape(8, 128).T)
        elif n in ("state_s5_re", "state_s5_im"):
            m[n] = f(inp[n][sl].reshape(NSEQ, 4096))
        elif n == "ssd_hd":
            m[n] = f(np.concatenate([inp["ssd_dt_bias"], inp["ssd_a_log"], inp["ssd_d"]]).reshape(1, 96))
        elif n == "ssd_conv_wT":
            m[n] = f(inp["ssd_conv_w"].reshape(4, 24, 128).transpose(2, 1, 0))
        elif n == "ssd_conv_bT":
            m[n] = f(inp["ssd_conv_b"].reshape(24, 128).T)
        elif n == "ssd_norm_gT":
            m[n] = f(inp["ssd_norm_g"].reshape(16, 128).T)
        elif n == "state_ssd_conv":
            m[n] = f(inp[n][sl].reshape(NSEQ * 3, 3072))
        elif n == "state_ssd":
            m[n] = f(inp[n][sl].reshape(NSEQ, 2048, 128))
        elif n == "pool_scaleT":
            m[n] = f(inp["pool_scale"].reshape(8, 128).T)
        elif n == "state_pool":
            m[n] = f(inp["state_pool"][sl].reshape(NSEQ * 15, D))
        elif n in ("cmlp_b_in", "cmlp_ln_g", "cmlp_ln_b"):
            m[n] = f(inp[n].reshape(1, -1))
        elif n == "cmlp_w_sT":
            m[n] = f(inp["cmlp_w_s"].transpose(0, 2, 1))
        elif n == "cmlp_b_sT":
            m[n] = f(inp["cmlp_b_s"].T)
        else:
            m[n] = f(inp[n])
    return m


_CACHE = {}


def kernel(**inputs):
    inp = {k: np.asarray(v) for k, v in inputs.items()}
    ncores = 8
    if "k" not in _CACHE:
        k = K(TP=16, layers=(0, 1, 2, 3))
        k.build()
        _CACHE["k"] = k
    k = _CACHE["k"]
    names = list(k.din)
    in_maps = [make_in_map(inp, c, names, TP=16) for c in range(ncores)]
    res = run_bass_kernel_spmd(k.nc, in_maps, core_ids=list(range(ncores)))
    R = res.results

    def cat(name, shp):
        return np.ascontiguousarray(np.stack([np.asarray(R[c][name]).reshape(shp) for c in range(ncores)]))
    y_prompt = cat("y_p", (2048, D))
    y_sample = cat("y_s", (NSEQ, DSEQ, D)).reshape(128, DSEQ, D)
    s5_re_p = cat("s5_re_p", (64, 64))
    s5_im_p = cat("s5_im_p", (64, 64))
    pool_p = cat("pool_p", (15, D))
    conv_p = cat("conv_p", (3, 3072))
    ssd_p = cat("ssd_p", (32, 64, 128))
    s5_re_s = cat("s5_re_s", (NSEQ, 64, 64)).reshape(128, 64, 64)
    s5_im_s = cat("s5_im_s", (NSEQ, 64, 64)).reshape(128, 64, 64)
    pool_s = cat("pool_s", (NSEQ, 15, D)).reshape(128, 15, D)
    cmlp_v_s = cat("cmlp_v_s", (NSEQ, DSEQ, D)).reshape(128, DSEQ, D)
    conv_s = cat("conv_s", (NSEQ, 3, 3072)).reshape(128, 3, 3072)
    ssd_s = cat("ssd_s", (NSEQ, 32, 64, 128)).reshape(128, 32, 64, 128)
    outs = (y_prompt, y_sample, s5_re_p, s5_im_p, pool_p, conv_p, ssd_p,
            s5_re_s, s5_im_s, pool_s, cmlp_v_s, conv_s, ssd_s)
    return tuple(np.asarray(o, dtype=np.float32) for o in outs)
```

```python
import numpy as np
from contextlib import ExitStack
import concourse.bass as bass
import concourse.mybir as mybir
from concourse.bass_utils import run_bass_kernel_spmd

F32 = mybir.dt.float32
BF16 = mybir.dt.bfloat16
I32 = mybir.dt.int32
U32 = mybir.dt.uint32
ALU = mybir.AluOpType
AF = mybir.ActivationFunctionType
AX = mybir.AxisListType

COMPUTE = ("pe", "dve", "act", "pool")
NDSEM = {"sp": 12, "pool": 12, "act": 4}
SAME_ENGINE_SYNC = {"pe": False, "dve": True, "act": True, "pool": True, "sp": True}

D = 1024
ALPHA = 8.0 ** 0.25
LN_EPS = 1e-5
NSEQ = 16
DSEQ = 8
NEXP = 16384


class Op:
    __slots__ = ("eng", "fn", "waits", "kind", "dsem", "dval", "cidx")


class Prog:
    def __init__(self, nc):
        self.nc = nc
        self.es = ExitStack()
        self.ops = {e: [] for e in ("pe", "dve", "act", "pool", "sp")}
        self.ncomp = {e: 0 for e in COMPUTE}
        self.last_w = {}
        self.readers = {}
        self.dcount = {}
        self.dlast = {}
        self.drr = {e: 0 for e in NDSEM}
        self.nbuf = 0
        self.bar = []
        self.scopes = []

    def sb(self, shape, dt=F32, name=None):
        self.nbuf += 1
        name = f"{name or 'sb'}_{self.nbuf}"
        es = self.scopes[-1] if self.scopes else self.es
        return es.enter_context(self.nc.sbuf_tensor(name, list(shape), dt))

    def ps(self, shape, dt=F32, name=None):
        self.nbuf += 1
        name = name or f"ps{self.nbuf}"
        return self.es.enter_context(self.nc.psum_tensor(name, list(shape), dt))

    def push_scope(self):
        self.scopes.append(ExitStack())

    def pop_scope(self):
        self.barrier()
        self.scopes.pop().close()

    def barrier(self):
        bar = []
        for e in COMPUTE:
            if self.ncomp[e]:
                bar.append(("c", e, self.ncomp[e] - 1))
        bar.extend(self.dlast.values())
        self.bar = bar

    def _deps(self, reads, writes):
        deps = list(self.bar)
        for r in reads:
            w = self.last_w.get(r)
            if w is not None:
                deps.append(w)
        for w_ in writes:
            w = self.last_w.get(w_)
            if w is not None:
                deps.append(w)
            deps.extend(self.readers.get(w_, ()))
        return deps

    def _commit(self, token, reads, writes):
        for w_ in writes:
            self.last_w[w_] = token
            self.readers[w_] = []
        for r in reads:
            if r not in writes:
                self.readers.setdefault(r, []).append(token)

    @staticmethod
    def _excl(reads, writes):
        ex = [r for r in reads if isinstance(r, str) and r.startswith("PS")]
        if ex:
            reads = [r for r in reads if r not in ex]
            writes = list(writes) + [r for r in ex if r not in writes]
        return reads, writes

    def defer_begin(self, name="f"):
        if not hasattr(self, "dq"):
            self.dq = {}
        self.dq[name] = []
        self.deferring = name

    def defer_end(self):
        n = len(self.dq[self.deferring])
        self.deferring = None
        return n

    def flush(self, n, name="f"):
        q = getattr(self, "dq", {}).get(name)
        if not q:
            return
        k = len(q) if n is None else min(n, len(q))
        for _ in range(k):
            kind, a = q.pop(0)
            (self.op if kind == "c" else self.dma)(*a, _now=True)

    def op(self, eng, fn, reads=(), writes=(), _now=False):
        if getattr(self, "deferring", None) and not _now:
            self.dq[self.deferring].append(("c", (eng, fn, reads, writes)))
            return None
        reads, writes = self._excl(list(reads), list(writes))
        o = Op()
        o.eng, o.fn, o.kind = eng, fn, "c"
        o.waits = self._deps(reads, writes)
        o.cidx = self.ncomp[eng]
        self.ncomp[eng] += 1
        self.ops[eng].append(o)
        self._commit(("c", eng, o.cidx), reads, writes)
        return o

    def I(self, eng, meth, *args, reads=(), writes=(), **kw):
        return self.op(eng, lambda h: getattr(h, meth)(*args, **kw), reads=reads, writes=writes)

    def Dm(self, q, meth, *args, reads=(), writes=(), **kw):
        return self.dma(q, lambda h: getattr(h, meth)(*args, **kw), reads=reads, writes=writes)

    def dma(self, q, fn, reads=(), writes=(), _now=False):
        if getattr(self, "deferring", None) and not _now:
            self.dq[self.deferring].append(("d", (q, fn, reads, writes)))
            return None
        o = Op()
        o.eng, o.fn, o.kind = q, fn, "d"
        deps = self._deps(reads, writes)
        k = self.drr[q]
        self.drr[q] = (k + 1) % NDSEM[q]
        key = (q, k)
        prev = self.dlast.get(key)
        if prev is not None:
            deps.append(prev)
        cnt = self.dcount.get(key, 0) + 1
        self.dcount[key] = cnt
        o.dsem, o.dval = key, 16 * cnt
        tok = ("d", key, o.dval)
        self.dlast[key] = tok
        o.waits = deps
        self.ops[q].append(o)
        self._commit(tok, reads, writes)
        return o

    def emit(self):
        nc = self.nc
        es = self.es
        csem = {e: es.enter_context(nc.semaphore(f"c_{e}")) for e in COMPUTE}
        dsem = {}
        for q, n in NDSEM.items():
            for k in range(n):
                dsem[(q, k)] = es.enter_context(nc.semaphore(f"d_{q}{k}"))
        ops = self.ops
        final = dict(self.dcount)

        def run(ename, h):
            waited = {}
            for o in ops[ename]:
                need = {}
                for d in o.waits:
                    if d[0] == "c":
                        _, e2, idx = d
                        if e2 == ename and not SAME_ENGINE_SYNC[ename]:
                            continue
                        key, val = ("c", e2), idx + 1
                    else:
                        _, dk, val = d
                        key = ("d", dk)
                    if need.get(key, 0) < val:
                        need[key] = val
                for key, val in need.items():
                    if waited.get(key, 0) >= val:
                        continue
                    waited[key] = val
                    s = csem[key[1]] if key[0] == "c" else dsem[key[1]]
                    h.wait_ge(s, val)
                inst = o.fn(h)
                if o.kind == "c":
                    inst.then_inc(csem[ename], 1)
                else:
                    inst.then_inc(dsem[o.dsem], 16)
            if ename == "sp":
                for key, cnt in final.items():
                    h.wait_ge(dsem[key], 16 * cnt)

        with nc.Block() as block:
            @block.sync
            def _(h):
                run("sp", h)

            @block.tensor
            def _(h):
                run("pe", h)

            @block.vector
            def _(h):
                run("dve", h)

            @block.scalar
            def _(h):
                run("act", h)

            @block.gpsimd
            def _(h):
                run("pool", h)
        es.close()


class K:
    def __init__(self, TP=16, layers=(0, 1, 2, 3), mixers=True, peer=True, dbg=()):
        self.TP = TP
        self.NT = TP + 1
        self.layers = layers
        self.do_mix = mixers
        self.do_peer = peer
        self.dbg = set(dbg)
        nc = bass.Bass("TRN2", target_bir_lowering=False)
        self.nc = nc
        self.P = Prog(nc)
        self.din = {}
        self.dout = {}
        self.evk = 0
        self.NG = 9

    def inp(self, name, shape, dt=F32):
        t = self.nc.dram_tensor(name, list(shape), dt, kind="ExternalInput").ap()
        self.din[name] = t
        return t

    def outp(self, name, shape, dt=F32):
        t = self.nc.dram_tensor(name, list(shape), dt, kind="ExternalOutput").ap()
        self.dout[name] = t
        return t

    def ev(self, out, in_, reads, writes, eng=None):
        if eng is None:
            eng = ("act", "dve")[self.evk % 2]
            self.evk += 1
        if eng == "act":
            self.P.I("act", "copy", out, in_, reads=reads, writes=writes)
        elif eng == "dve":
            self.P.I("dve", "tensor_copy", out, in_, reads=reads, writes=writes)
        else:
            self.P.I("pool", "tensor_copy", out, in_, reads=reads, writes=writes)

    def setup(self):
        P = self.P
        TP, NT = self.TP, self.NT
        L = TP * 128
        self.x_p = self.inp("x_p", [L, D])
        self.x_s = self.inp("x_s", [128, D])
        self.y_p = self.outp("y_p", [L, D])
        self.y_s = self.outp("y_s", [128, D])
        self.ln1_g = self.inp("ln1_g", [4, D]); self.ln1_b = self.inp("ln1_b", [4, D])
        self.ln2_g = self.inp("ln2_g", [4, D]); self.ln2_b = self.inp("ln2_b", [4, D])
        if self.do_peer:
            self.d_wq = self.inp("peer_w_q", [4, D, 2048])
            self.d_keysT = self.inp("peer_keysT", [4, 128, 2048])
            self.d_u = self.inp("peer_u", [4 * NEXP, D])
            self.d_v = self.inp("peer_v", [4 * NEXP, D])
            self.uvb = self.nc.dram_tensor("uvb", [4 * NEXP, 2048], BF16, kind="Internal").ap()
        if self.do_mix:
            if 0 in self.layers:
                for n in ("s5_w_in", "s5_w_glu", "s5_w_out"):
                    self.inp(n, [D, D])
                self.inp("s5_aT", [128, 3, 32]); self.inp("s5_bT", [128, 2, 32, 16]); self.inp("s5_cT", [128, 2, 32, 16])
                self.inp("s5_dT", [128, 8]); self.inp("s5_b_gluT", [128, 8])
                self.inp("state_s5_re", [NSEQ, 4096]); self.inp("state_s5_im", [NSEQ, 4096])
                self.outp("s5_re_p", [32, 128]); self.outp("s5_im_p", [32, 128])
                self.outp("s5_re_s", [NSEQ, 4096]); self.outp("s5_im_s", [NSEQ, 4096])
            if 3 in self.layers:
                self.inp("ssd_w_in", [D, 5152]); self.inp("ssd_w_out", [2048, D]); self.inp("ssd_hd", [1, 96])
                self.inp("ssd_conv_wT", [128, 24, 4]); self.inp("ssd_conv_bT", [128, 24]); self.inp("ssd_norm_gT", [128, 16])
                self.inp("state_ssd_conv", [NSEQ * 3, 3072]); self.inp("state_ssd", [NSEQ, 2048, 128])
                self.outp("conv_p", [3, 3072]); self.outp("ssd_p", [2048, 128])
                self.outp("conv_s", [NSEQ * 3, 3072]); self.outp("ssd_s", [NSEQ * 2048, 128])
            if 1 in self.layers:
                self.inp("pool_w_in", [D, D]); self.inp("pool_w_out", [D, D]); self.inp("pool_w_grp", [4, 256, 256])
                self.inp("pool_scaleT", [128, 8]); self.inp("state_pool", [NSEQ * 15, D])
                self.outp("pool_p", [15, D]); self.outp("pool_s", [NSEQ * 15, D])
            if 2 in self.layers:
                self.inp("cmlp_w_in", [D, 2048]); self.inp("cmlp_w_out", [D, D]); self.inp("cmlp_b_in", [1, 2048])
                self.inp("cmlp_ln_g", [1, D]); self.inp("cmlp_ln_b", [1, D])
                self.inp("cmlp_w_sT", [4, 128, 128]); self.inp("cmlp_b_sT", [128, 4])
                self.outp("cmlp_v_s", [128, D])
        self.H = P.sb([128, NT, D], F32, "H")
        self.ident = P.sb([128, 128], F32, "ident")
        self.identb = P.sb([128, 128], BF16, "identb")
        self.iota16 = P.sb([128, 16], F32, "iota16")
        self.lng = P.sb([128, D], F32, "lng")
        self.lnb = P.sb([128, D], F32, "lnb")
        self.tmp = P.sb([128, D], F32, "tmp")
        self.small = P.sb([128, 32], F32, "small")
        self.psum = P.ps([128, 8, 512], F32, "psum")
        self.bank = [self.psum[:, i, :] for i in range(8)]
        iot = P.sb([128, 128], F32, "iot")
        P.I("pool", "iota", iot[:], pattern=[[1, 128]], base=0, channel_multiplier=-1,
                                      allow_small_or_imprecise_dtypes=True, writes=["iot"])
        P.I("dve", "tensor_single_scalar", self.ident[:], iot[:], 0.0, ALU.is_equal,
             reads=["iot"], writes=["ident"])
        P.I("dve", "tensor_copy", self.identb[:], self.ident[:], reads=["ident"], writes=["identb"])
        P.I("pool", "iota", self.iota16[:], pattern=[[1, 16]], base=0, channel_multiplier=0,
                                      allow_small_or_imprecise_dtypes=True, writes=["iota16"])
        self.iot = iot
        for i in range(TP):
            P.Dm("sp", "dma_start", out=self.H[:, i, :], in_=self.x_p[i * 128:(i + 1) * 128, :],
                  writes=[("H", i)])
        P.Dm("sp", "dma_start", out=self.H[:, TP, :], in_=self.x_s, writes=[("H", TP)])

    def finish(self):
        P = self.P
        TP = self.TP
        for i in range(TP):
            P.Dm("sp", "dma_start", out=self.y_p[i * 128:(i + 1) * 128, :], in_=self.H[:, i, :],
                  reads=[("H", i)])
        P.Dm("sp", "dma_start", out=self.y_s, in_=self.H[:, TP, :], reads=[("H", TP)])
        P.emit()

    def transpose_f32(self, src, nch, dst, src_reg, dst_reg, banks=(0, 1)):
        P = self.P
        for g in range(0, nch, 4):
            b = banks[(g // 4) % len(banks)]
            n = min(4, nch - g)
            for c in range(n):
                P.I("pe", "transpose", self.bank[b][:, c * 128:(c + 1) * 128],
                                                                src[:, (g + c) * 128:(g + c + 1) * 128], self.ident[:],
                     reads=[src_reg, "ident"], writes=[f"PS{b}"])
            self.ev(dst[:, g:g + n, :], self.bank[b][:, 0:n * 128].rearrange("p (c n) -> p c n", c=n),
                    reads=[f"PS{b}"], writes=[dst_reg])

    def load_ln(self, g_ap, b_ap, l):
        P = self.P
        P.Dm("sp", "dma_start", out=self.lng[:], in_=g_ap[l:l + 1, :].to_broadcast([128, D]), writes=["lng"])
        P.Dm("sp", "dma_start", out=self.lnb[:], in_=b_ap[l:l + 1, :].to_broadcast([128, D]), writes=["lnb"])

    def resid_ln(self, i, mix_ap, mix_regs):
        P = self.P
        Hi = self.H[:, i, :]
        tmp, sm = self.tmp, self.small
        P.I("dve", "scalar_tensor_tensor", tmp[:], Hi, ALPHA, mix_ap, ALU.mult, ALU.add,
             reads=[("H", i)] + list(mix_regs), writes=["tmp"])
        self.ln_inplace(tmp, "tmp", Hi, ("H", i), self.lng, self.lnb)

    def ln_inplace(self, src, src_reg, dst_ap, dst_reg, g_t, b_t, n=D):
        P = self.P
        sm = self.small
        nchk = n // 512
        for j in range(nchk):
            P.I("dve", "bn_stats", sm[:, j * 6:(j + 1) * 6], src[:, j * 512:(j + 1) * 512],
                 reads=[src_reg], writes=["small"])
        P.I("dve", "bn_aggr", sm[:, 12:14], sm[:, 0:6 * nchk], reads=["small"], writes=["small"])
        P.I("act", "activation", sm[:, 14:15], sm[:, 13:14], AF.Sqrt, bias=self.eps_t[:, 0:1],
             reads=["small", "eps"], writes=["small"])
        P.I("dve", "reciprocal", sm[:, 15:16], sm[:, 14:15], reads=["small"], writes=["small"])
        P.I("dve", "tensor_scalar", src[:, 0:n], src[:, 0:n], sm[:, 12:13], sm[:, 15:16], ALU.subtract, ALU.mult,
             reads=["small", src_reg], writes=[src_reg])
        P.I("dve", "tensor_tensor", src[:, 0:n], src[:, 0:n], g_t[:, 0:n], ALU.mult,
             reads=[src_reg, "lng"], writes=[src_reg])
        P.I("dve", "tensor_tensor", dst_ap, src[:, 0:n], b_t[:, 0:n], ALU.add,
             reads=[src_reg, "lnb"], writes=[dst_reg])

    def convert_tables(self, l, stg, cb):
        P = self.P
        n = len(stg)
        for ch in range(128):
            r0 = l * NEXP + ch * 128
            for hf, src in enumerate((self.d_u, self.d_v)):
                k = (ch * 2 + hf) % n
                P.Dm("sp", "dma_start", out=stg[k][:], in_=src[r0:r0 + 128, :], writes=[f"cv_s{k}"])
                P.I("act", "copy", cb[k][:], stg[k][:], reads=[f"cv_s{k}"], writes=[f"cv_b{k}"])
                P.Dm("sp", "dma_start", out=self.uvb[r0:r0 + 128, hf * 1024:(hf + 1) * 1024], in_=cb[k][:],
                     reads=[f"cv_b{k}"], writes=[("uvb", l)])

    def peer(self, l):
        P = self.P
        NT = self.NT
        P.flush(None, "cv0")
        P.push_scope()
        wq = P.sb([128, 8, 2048], BF16, "wq")
        keysT = P.sb([128, 16, 128], F32, "keysT")
        hT = P.sb([128, 8, 128], BF16, "hT")
        qT = P.sb([128, 16, 128], F32, "qT")
        sbig = P.sb([128, 16, 128], F32, "sbig")
        oh = qT[:].rearrange("p c n -> p (c n)").rearrange("p (h a b) -> p h a b", h=8, a=16)
        QALL = [("qT", g) for g in range(4)]
        tv = P.sb([128, 16, 16], F32, "tv")
        ti = P.sb([128, 16, 16], U32, "ti")
        tif = P.sb([128, 16, 16], F32, "tif")
        wk2 = [P.sb([128, 256], F32, f"wk{j}") for j in range(2)]
        bs = P.sb([128, 8, 16], F32, "bs")
        bj = P.sb([128, 8, 16], U32, "bj")
        ja = P.sb([128, 8, 16], U32, "ja")
        jaf = P.sb([128, 8, 16], F32, "jaf")
        jbf = P.sb([128, 8, 16], F32, "jbf")
        i0 = P.sb([128, 8, 16], F32, "i0")
        i1 = P.sb([128, 8, 16], F32, "i1")
        idx2 = [P.sb([128, 128], I32, f"idx{j}") for j in range(2)]
        gate2 = [P.sb([128, 8, 16], F32, f"gate{j}") for j in range(2)]
        gsum = P.sb([128, 8], F32, "gsum")
        act = P.sb([128, 128], F32, "actv")
        wgt = P.sb([128, 128], F32, "wgt")
        NG = self.NG
        gb = [P.sb([128, 2048], BF16, f"gb{j}") for j in range(NG)]
        junk2 = [P.sb([128, D], BF16, f"junk{j}") for j in range(2)]
        tmpv = [P.sb([128, D], BF16, f"tmpv{j}") for j in range(2)]
        cstg = [self.stg0[0]]
        ccb = [self.cb0[0]]
        bk = self.bank
        for k in range(8):
            P.Dm("pool", "dma_start", out=wq[:, k, :], in_=self.d_wq[l, k * 128:(k + 1) * 128, :], writes=["wq"])
        P.Dm("sp", "dma_start", out=keysT[:].rearrange("p c n -> p (c n)"), in_=self.d_keysT[l], writes=["keysT"])
        self.load_ln(self.ln2_g, self.ln2_b, l)
        tv4 = tv[:].rearrange("p (h i) k -> p h i k", i=2)
        tif4 = tif[:].rearrange("p (h i) k -> p h i k", i=2)
        cand = sbig[:].rearrange("p c n -> p (c n)").rearrange("p (h a b) -> p h a b", h=8, a=16)
        cand3 = sbig[:].rearrange("p c n -> p (c n)").rearrange("p (h x) -> p h x", h=8)
        sball = [("sbig", g) for g in range(4)]
        io4 = self.iota16[:].unsqueeze(1).unsqueeze(1).to_broadcast([128, 8, 16, 16])

        def front(i):
            sfx = i % 2
            idx, gate = idx2[sfx], gate2[sfx]
            ireg, greg = f"idx{sfx}", f"gate{sfx}"
            Hi = self.H[:, i, :]
            self.transpose_f32(Hi, 8, hT, ("H", i), "hT", banks=(6, 7))
            for c in range(16):
                b = 4 + (c // 4) % 2
                for k in range(8):
                    P.I("pe", "matmul", bk[b][:, (c % 4) * 128:(c % 4 + 1) * 128], lhsT=wq[:, k, c * 128:(c + 1) * 128],
                        rhs=hT[:, k, :], start=(k == 0), stop=(k == 7), reads=["wq", "hT"], writes=[f"PS{b}"])
                if c % 4 == 3:
                    self.ev(qT[:, c - 3:c + 1, :], bk[b][:, :].rearrange("p (c n) -> p c n", c=4),
                            reads=[f"PS{b}"], writes=[("qT", c // 4)], eng="act")
            for c in range(16):
                b = 6 + (c // 4) % 2
                P.I("pe", "matmul", bk[b][:, (c % 4) * 128:(c % 4 + 1) * 128], lhsT=qT[:, c, :], rhs=keysT[:, c, :],
                    start=True, stop=True, reads=[("qT", c // 4), "keysT"], writes=[f"PS{b}"])
                if c % 4 == 3:
                    self.ev(sbig[:, c - 3:c + 1, :], bk[b][:, :].rearrange("p (c n) -> p c n", c=4),
                            reads=[f"PS{b}"], writes=[("sbig", c // 4)], eng="act")
            TVA = [("tv", c) for c in range(16)]
            TIA = [("ti", c) for c in range(16)]
            for c in range(16):
                sr = ("sbig", c // 4)
                w_, wr_ = wk2[c % 2], f"wk{c % 2}"
                P.I("dve", "max", tv[:, c, 0:8], sbig[:, c, :], reads=[sr], writes=[("tv", c)])
                P.I("dve", "max_index", ti[:, c, 0:8], tv[:, c, 0:8], sbig[:, c, :], reads=[sr, ("tv", c)], writes=[("ti", c)])
                P.I("dve", "match_replace", w_[:, 0:128], tv[:, c, 0:8], sbig[:, c, :], -1e30, reads=[sr, ("tv", c)], writes=[wr_])
                P.I("dve", "max", tv[:, c, 8:16], w_[:, 0:128], reads=[wr_], writes=[("tv", c)])
                P.I("dve", "max_index", ti[:, c, 8:16], tv[:, c, 8:16], w_[:, 0:128], reads=[wr_, ("tv", c)], writes=[("ti", c)])
            P.I("dve", "tensor_copy", tif[:], ti[:], reads=TIA, writes=["tif"])
            P.I("dve", "tensor_tensor", cand, tv4[:, :, 0, :].unsqueeze(3).to_broadcast([128, 8, 16, 16]),
                tv4[:, :, 1, :].unsqueeze(2).to_broadcast([128, 8, 16, 16]), ALU.add, reads=TVA + sball, writes=sball)
            BSA = [("bs", hh) for hh in range(8)]
            BJA = [("bj", hh) for hh in range(8)]
            for hh in range(8):
                w_, wr_ = wk2[hh % 2], f"wk{hh % 2}"
                P.I("dve", "max", bs[:, hh, 0:8], cand3[:, hh, :], reads=sball, writes=[("bs", hh)])
                P.I("dve", "max_index", bj[:, hh, 0:8], bs[:, hh, 0:8], cand3[:, hh, :], reads=sball + [("bs", hh)], writes=[("bj", hh)])
                P.I("dve", "match_replace", w_[:], bs[:, hh, 0:8], cand3[:, hh, :], -1e30, reads=sball + [("bs", hh)], writes=[wr_])
                P.I("dve", "max", bs[:, hh, 8:16], w_[:], reads=[wr_], writes=[("bs", hh)])
                P.I("dve", "max_index", bj[:, hh, 8:16], bs[:, hh, 8:16], w_[:], reads=[wr_, ("bs", hh)], writes=[("bj", hh)])
            P.I("dve", "tensor_single_scalar", ja[:], bj[:], 4, ALU.logical_shift_right, reads=BJA, writes=["ja"])
            P.I("dve", "tensor_copy", jaf[:], ja[:], reads=["ja"], writes=["jaf"])
            P.I("dve", "tensor_single_scalar", ja[:], bj[:], 15, ALU.bitwise_and, reads=BJA + ["jaf"], writes=["ja"])
            P.I("dve", "tensor_copy", jbf[:], ja[:], reads=["ja"], writes=["jbf"])
            for (jf, half, dst, nm) in ((jaf, 0, i0, "i0"), (jbf, 1, i1, "i1")):
                P.I("dve", "tensor_tensor", oh, io4, jf[:].unsqueeze(3).to_broadcast([128, 8, 16, 16]), ALU.is_equal,
                    reads=["iota16", "jaf", "jbf"], writes=QALL)
                P.I("dve", "tensor_tensor", oh, oh, tif4[:, :, half, :].unsqueeze(2).to_broadcast([128, 8, 16, 16]), ALU.mult,
                    reads=QALL + ["tif"], writes=QALL)
                P.I("dve", "tensor_reduce", dst[:], oh, AX.X, ALU.add, reads=QALL, writes=[nm])
            P.I("dve", "scalar_tensor_tensor", i0[:], i0[:], 128.0, i1[:], ALU.mult, ALU.add, reads=["i0", "i1"], writes=["i0"])
            P.I("dve", "tensor_scalar", idx[:], i0[:].rearrange("p h k -> p (h k)"), float(l * NEXP), None, ALU.add,
                reads=["i0"], writes=[ireg])
            P.I("dve", "tensor_tensor", gate[:], bs[:], bs[:, :, 0:1].to_broadcast([128, 8, 16]), ALU.subtract, reads=BSA, writes=[greg])
            P.I("act", "activation", gate[:], gate[:], AF.Exp, reads=[greg], writes=[greg])
            P.I("dve", "tensor_reduce", gsum[:], gate[:], AX.X, ALU.add, reads=[greg], writes=["gsum"])
            P.I("dve", "reciprocal", gsum[:], gsum[:], reads=["gsum"], writes=["gsum"])
            P.I("dve", "tensor_tensor", gate[:], gate[:], gsum[:].unsqueeze(2).to_broadcast([128, 8, 16]), ALU.mult,
                reads=[greg, "gsum"], writes=[greg])

        def back(i, nflush, ncv):
            sfx = i % 2
            idx, gate = idx2[sfx], gate2[sfx]
            ireg, greg = f"idx{sfx}", f"gate{sfx}"
            Hi = self.H[:, i, :]
            gflat = gate[:].rearrange("p h k -> p (h k)")
            GS = 2
            for gi in range(128 // GS):
                js = list(range(gi * GS, (gi + 1) * GS))
                wr = ("wgt", gi % 8)
                ars = [("actv", j % 16) for j in js]
                for j in js:
                    g, gr = gb[j % NG], f"gb{j % NG}"
                    P.Dm("pool", "indirect_dma_start", out=g[:], out_offset=None, in_=self.uvb,
                         in_offset=bass.IndirectOffsetOnAxis(ap=idx[:, j:j + 1], axis=0), reads=[ireg, ("uvb", l)], writes=[gr])
                    P.I("dve", "scalar_tensor_tensor", junk2[j % 2][:], g[:, 0:1024], 1.0, Hi, ALU.mult, ALU.mult, accum_out=act[:, j:j + 1],
                        reads=[gr, ("H", i)], writes=[f"junk{j % 2}", ("actv", j % 16)])
                    P.flush(nflush)
                    if j % 3 == 2:
                        P.flush(ncv, "cv")
                P.I("act", "activation", wgt[:, js[0]:js[-1] + 1], act[:, js[0]:js[-1] + 1], AF.Gelu_apprx_tanh, reads=ars, writes=[wr])
                P.I("dve", "tensor_tensor", wgt[:, js[0]:js[-1] + 1], wgt[:, js[0]:js[-1] + 1], gflat[:, js[0]:js[-1] + 1], ALU.mult,
                    reads=[wr, greg], writes=[wr])
                for j in js:
                    g, gr = gb[j % NG], f"gb{j % NG}"
                    tb, tr = tmpv[j % 2], f"tmpv{j % 2}"
                    P.I("act", "activation", tb[:], g[:, 1024:2048], AF.Copy, scale=wgt[:, j:j + 1], reads=[gr, wr], writes=[tr])
                    for n in range(2):
                        P.I("pe", "matmul", bk[n], lhsT=self.identb[:], rhs=tb[:, n * 512:(n + 1) * 512], start=(j == 0), stop=(j == 127),
                            reads=[tr, "identb"], writes=[f"PS{n}"])
            P.flush(None)
            self.resid_ln(i, self.bank2(0), ["PS0", "PS1"])

        if l + 1 < 4 and (l + 1) in self.layers:
            P.defer_begin("cv")
            self.convert_tables(l + 1, cstg, ccb)
            ncv_total = P.defer_end()
        else:
            ncv_total = 0
        ncv = (ncv_total + (NT - 1) * 128 - 1) // max(1, (NT - 1) * 128) if ncv_total else 0
        front(0)
        for i in range(NT):
            if i + 1 < NT:
                P.defer_begin()
                front(i + 1)
                n = P.defer_end()
                back(i, (n + 99) // 100, ncv)
            else:
                back(i, 0, ncv)
        P.flush(None, "cv")
        P.pop_scope()

    def load_w(self, dst, src, reg):
        nk = src.shape[0] // 128
        for k in range(nk):
            self.P.Dm("pool", "dma_start", out=dst[:, k, :], in_=src[k * 128:(k + 1) * 128, :],
                       writes=[reg])

    def bank2(self, b):
        return self.psum[:, b:b + 2, :].rearrange("p a n -> p (a n)")

    def final_proj_ln(self, i, XT, xreg, w, wreg, nk, tok0=0):
        P = self.P
        for n in range(2):
            b = 6 + n
            for k in range(nk):
                P.I("pe", "matmul",
                    self.bank[b], lhsT=XT[:, k, tok0:tok0 + 128], rhs=w[:, k, n * 512:(n + 1) * 512],
                    start=(k == 0), stop=(k == nk - 1), reads=[xreg, wreg], writes=[f"PS{b}"])
        self.resid_ln(i, self.bank2(6), ["PS6", "PS7"])

    def make_hT(self, tiles, dst, reg):
        for n, i in enumerate(tiles):
            self.transpose_f32(self.H[:, i, :], 8, dst[:, :, n * 128:(n + 1) * 128], ("H", i), reg, banks=(0, 1))

    def cmlp(self, l):
        P = self.P
        TP, NT = self.TP, self.NT
        bk = self.bank
        d = self.din
        P.push_scope()
        w_in = P.sb([128, 8, 2048], BF16, "cw_in")
        w_out = P.sb([128, 8, 1024], BF16, "cw_out")
        b_in = P.sb([128, 2048], F32, "cb_in")
        cg = P.sb([128, D], F32, "c_lng")
        cb = P.sb([128, D], F32, "c_lnb")
        wsf = P.sb([128, 4, 128], F32, "wsf")
        wsp = P.sb([128, 4, 128], BF16, "wsp")
        wss = P.sb([128, 4, 128], BF16, "wss")
        bsp = P.sb([128, 4], F32, "bsp")
        bss = P.sb([128, 4], F32, "bss")
        hT = P.sb([128, 8, 128], BF16, "c_hT")
        zs = P.sb([128, 2048], F32, "zs")
        vb = P.sb([128, D], BF16, "vb")
        o = P.sb([128, D], F32, "c_o")
        oT = P.sb([128, 8, 128], BF16, "c_oT")
        self.load_w(w_in, d["cmlp_w_in"], "cw_in")
        self.load_w(w_out, d["cmlp_w_out"], "cw_out")
        P.Dm("sp", "dma_start", out=b_in[:], in_=d["cmlp_b_in"][0:1, :].to_broadcast([128, 2048]), writes=["cb_in"])
        P.Dm("sp", "dma_start", out=cg[:], in_=d["cmlp_ln_g"][0:1, :].to_broadcast([128, D]), writes=["c_lng"])
        P.Dm("sp", "dma_start", out=cb[:], in_=d["cmlp_ln_b"][0:1, :].to_broadcast([128, D]), writes=["c_lnb"])
        P.Dm("sp", "dma_start", out=bsp[:], in_=d["cmlp_b_sT"], writes=["bsp"])
        for j in range(NSEQ):
            P.Dm("sp", "dma_start", out=bss[j * 8:(j + 1) * 8, :], in_=d["cmlp_b_sT"][0:8, :], writes=["bss"])
        P.Dm("sp", "dma_start", out=wsf[:], in_=d["cmlp_w_sT"].rearrange("h s t -> s h t"), writes=["wsf"])
        for hh in range(4):
            P.I("pool", "affine_select", wsf[:, hh, :], wsf[:, hh, :], pattern=[[1, 128]],
                                                       compare_op=ALU.is_ge, fill=0.0, base=0, channel_multiplier=-1,
                 reads=["wsf"], writes=["wsf"])
        P.I("dve", "tensor_copy", wsp[:], wsf[:], reads=["wsf"], writes=["wsp"])
        wsf2 = P.sb([128, 4, 128], F32, "wsf2")
        P.I("dve", "memset", wsf2[:], 0.0, writes=["wsf2"])
        for j in range(NSEQ):
            P.Dm("sp", "dma_start", out=wsf2[j * 8:(j + 1) * 8, :, j * 8:(j + 1) * 8],
                                                  in_=d["cmlp_w_sT"][:, 0:8, 0:8].rearrange("h s t -> s h t"),
                  reads=[], writes=["wsf2"])
        for hh in range(4):
            P.I("pool", "affine_select", wsf2[:, hh, :], wsf2[:, hh, :], pattern=[[1, 128]],
                                                       compare_op=ALU.is_ge, fill=0.0, base=0, channel_multiplier=-1,
                 reads=["wsf2"], writes=["wsf2"])
        P.I("dve", "tensor_copy", wss[:], wsf2[:], reads=["wsf2"], writes=["wss"])
        self.load_ln(self.ln1_g, self.ln1_b, l)
        for i in range(NT):
            samp = (i == TP)
            ws_t, bs_t = (wss, bss) if samp else (wsp, bsp)
            wreg, breg = ("wss", "bss") if samp else ("wsp", "bsp")
            self.make_hT([i], hT, "c_hT")
            for n in range(4):
                b = 2 + n
                for k in range(8):
                    P.I("pe", "matmul", bk[b], lhsT=hT[:, k, :], rhs=w_in[:, k, n * 512:(n + 1) * 512],
                                                              start=(k == 0), stop=(k == 7),
                         reads=["c_hT", "cw_in"], writes=[f"PS{b}"])
                P.I("dve", "tensor_tensor", zs[:, n * 512:(n + 1) * 512], bk[b], b_in[:, n * 512:(n + 1) * 512], ALU.add,
                     reads=[f"PS{b}", "cb_in"], writes=[("zs", n)])
                P.I("act", "activation", zs[:, n * 512:(n + 1) * 512], zs[:, n * 512:(n + 1) * 512], AF.Gelu_apprx_tanh,
                     reads=[("zs", n)], writes=[("zs", n)])
            vv = zs[:, 1024:2048]
            sm = self.small
            for j in range(2):
                P.I("dve", "bn_stats", sm[:, j * 6:(j + 1) * 6], zs[:, 1024 + j * 512:1024 + (j + 1) * 512],
                     reads=[("zs", 2 + j)], writes=["small"])
            P.I("dve", "bn_aggr", sm[:, 12:14], sm[:, 0:12], reads=["small"], writes=["small"])
            P.I("act", "activation", sm[:, 14:15], sm[:, 13:14], AF.Sqrt, bias=self.eps_t[:, 0:1],
                 reads=["small", "eps"], writes=["small"])
            P.I("dve", "reciprocal", sm[:, 15:16], sm[:, 14:15], reads=["small"], writes=["small"])
            vregs = [("zs", 2), ("zs", 3)]
            P.I("dve", "tensor_scalar", vv, vv, sm[:, 12:13], sm[:, 15:16], ALU.subtract, ALU.mult,
                 reads=["small"] + vregs, writes=vregs)
            P.I("dve", "tensor_tensor", vv, vv, cg[:], ALU.mult, reads=vregs + ["c_lng"], writes=vregs)
            P.I("dve", "tensor_tensor", vv, vv, cb[:], ALU.add, reads=vregs + ["c_lnb"], writes=vregs)
            if samp:
                P.Dm("sp", "dma_start", out=self.dout["cmlp_v_s"], in_=vv, reads=vregs)
            P.I("act", "copy", vb[:], vv, reads=vregs, writes=["vb"])
            for hh in range(4):
                b = hh // 2
                P.I("pe", "matmul", bk[b][:, (hh % 2) * 256:(hh % 2 + 1) * 256], lhsT=ws_t[:, hh, :],
                                                          rhs=vb[:, hh * 256:(hh + 1) * 256], start=True, stop=True,
                     reads=[wreg, "vb"], writes=[f"PS{b}"])
            for hh in range(4):
                b = hh // 2
                P.I("dve", "scalar_tensor_tensor",
                    o[:, hh * 256:(hh + 1) * 256], bk[b][:, (hh % 2) * 256:(hh % 2 + 1) * 256], bs_t[:, hh:hh + 1],
                    zs[:, hh * 256:(hh + 1) * 256], ALU.add, ALU.mult,
                    reads=[f"PS{b}", breg, ("zs", hh // 2)], writes=["c_o"])
            self.transpose_f32(o, 8, oT, "c_o", "c_oT", banks=(2, 3))
            self.final_proj_ln(i, oT, "c_oT", w_out, "cw_out", 8)
        P.pop_scope()

    def pool(self, l):
        P = self.P
        TP, NT = self.TP, self.NT
        bk = self.bank
        d = self.din
        P.push_scope()
        w_in = P.sb([128, 8, 1024], BF16, "pw_in")
        w_out = P.sb([128, 8, 1024], BF16, "pw_out")
        w_grp = P.sb([128, 8, 256], BF16, "pw_grp")
        scl = P.sb([128, 8], F32, "p_scl")
        rc = P.sb([128, 4, 16], F32, "p_rc")
        BLK = min(4, TP)
        NB = BLK * 128
        HTb = P.sb([128, 8, NB], BF16, "p_HT")
        UT = P.sb([128, 8, 16 + NB], F32, "p_UT")
        SA = P.sb([128, max(16 + NB, 384)], F32, "p_SA")
        SB = P.sb([128, max(16 + NB, 384)], F32, "p_SB")
        ufm = P.sb([128, D], F32, "p_ufm")
        PT = P.sb([128, 8, NB], BF16, "p_PT")
        MT = P.sb([128, 8, NB], BF16, "p_MT")
        US = P.sb([128, 8, NSEQ, 24], F32, "p_US")
        stt = P.sb([120, 2, D], F32, "p_stt")
        utok = P.sb([128, D], F32, "p_utok")
        self.load_w(w_in, d["pool_w_in"], "pw_in")
        self.load_w(w_out, d["pool_w_out"], "pw_out")
        self.load_w(w_grp, d["pool_w_grp"].rearrange("g k n -> (g k) n"), "pw_grp")
        P.Dm("sp", "dma_start", out=scl[:], in_=d["pool_scaleT"], writes=["p_scl"])
        self.load_ln(self.ln1_g, self.ln1_b, l)
        for g in range(4):
            w = 2 ** (g + 1)
            P.I("dve", "tensor_scalar", rc[:, g, :], self.iota16[:], 1.0, float(w), ALU.add, ALU.min,
                 reads=["iota16"], writes=["p_rc"])
        P.I("dve", "reciprocal", rc[:].rearrange("p g t -> p (g t)"), rc[:].rearrange("p g t -> p (g t)"),
             reads=["p_rc"], writes=["p_rc"])
        P.I("dve", "memset", UT[:, :, 0:16], 0.0, writes=["p_UT"])

        def window(c, src3, lo_hi_views, first_block, is_sample):
            raise NotImplementedError

        def pooled_chunk(c, U, SAv, SBv, n0, PTout, first_block, ureg):
            g = c // 2
            lv = g + 1
            cur, curreg = U, ureg
            bufs = [(SAv, "p_SA"), (SBv, "p_SB")]
            for k in range(lv):
                sh = 2 ** k
                lo = 2 ** (k + 1)
                dst, dreg = bufs[k % 2]
                eng = "pool" if (k % 2 == 1 and cur is not U) else "dve"
                P.I(eng, "tensor_tensor",
                    dst(lo, None), cur(lo, None), cur(lo - sh, -sh), ALU.add,
                    reads=[curreg], writes=[dreg])
                cur, curreg = dst, dreg
            w = 2 ** lv
            P.I("dve", "scalar_tensor_tensor", PTout, cur(n0, None), 1.0 / w, U(n0, None), ALU.mult, ALU.subtract,
                 reads=[curreg, ureg], writes=["p_PT"])
            return cur, curreg

        nblk = (TP + BLK - 1) // BLK
        for blk in range(nblk):
            tiles = list(range(blk * BLK, min(TP, (blk + 1) * BLK)))
            nb = len(tiles) * 128
            self.make_hT(tiles, HTb, "p_HT")
            for c in range(8):
                b = 2 + c % 4
                for k in range(8):
                    P.I("pe", "matmul", bk[b][:, 0:nb], lhsT=w_in[:, k, c * 128:(c + 1) * 128], rhs=HTb[:, k, 0:nb],
                                                              start=(k == 0), stop=(k == 7),
                         reads=["pw_in", "p_HT"], writes=[f"PS{b}"])
                self.ev(UT[:, c, 16:16 + nb], bk[b][:, 0:nb], reads=[f"PS{b}"], writes=[("p_UT", c)])
            for c in range(8):
                g = c // 2
                U = lambda a, e, c=c: UT[:, c, a:(16 + nb + e) if e else 16 + nb]
                SAv = lambda a, e: SA[:, a:(16 + nb + e) if e else 16 + nb]
                SBv = lambda a, e: SB[:, a:(16 + nb + e) if e else 16 + nb]
                cur, curreg = pooled_chunk(c, U, SAv, SBv, 16, PT[:, c, 0:nb], blk == 0, ("p_UT", c))
                if blk == 0:
                    P.I("dve", "tensor_tensor", cur(16, None)[:, 0:16], cur(16, None)[:, 0:16], rc[:, g, :], ALU.mult,
                         reads=[curreg, "p_rc"], writes=[curreg])
                    P.I("dve", "tensor_tensor", PT[:, c, 0:16], cur(16, None)[:, 0:16], UT[:, c, 16:32], ALU.subtract,
                         reads=[curreg, ("p_UT", c)], writes=["p_PT"])
            if blk == nblk - 1:
                for c in range(8):
                    P.I("pe", "transpose", bk[0][0:16, c % 4 * 128:(c % 4 + 1) * 128], UT[:, c, nb:nb + 16], self.ident[:],
                         reads=[("p_UT", c), "ident"], writes=["PS0"])
                    if c % 4 == 3:
                        self.ev(utok[0:16, (c - 3) * 128:(c + 1) * 128], bk[0][0:16, :], reads=["PS0"], writes=["p_utok"])
                P.Dm("sp", "dma_start", out=self.dout["pool_p"], in_=utok[1:16, :], reads=["p_utok"])
            else:
                for c in range(8):
                    self.ev(UT[:, c, 1:16], UT[:, c, nb + 1:nb + 16], reads=[("p_UT", c)], writes=[("p_UT", c)], eng="pool")
            self.pool_tail(tiles, nb, PT, MT, w_grp, scl, w_out)
        for half in range(2):
            P.Dm("sp", "dma_start", out=stt[:, half, :], in_=d["state_pool"][half * 120:(half + 1) * 120, :],
                  writes=["p_stt"])
        P.Dm("sp", "dma_start", out=self.dout["pool_s"].rearrange("(s j) d -> s j d", j=15)[:, 0:7, :],
                                          in_=d["state_pool"].rearrange("(s j) d -> s j d", j=15)[:, 8:15, :], reads=[])
        for c in range(8):
            b = c % 2
            for half in range(2):
                P.I("pe", "transpose", bk[b][:, half * 120:(half + 1) * 120],
                                                                 stt[:, half, c * 128:(c + 1) * 128], self.ident[0:120, 0:120],
                     reads=["p_stt", "ident"], writes=[f"PS{b}"])
            self.ev(US[:, c, :, 1:16], bk[b][:, 0:240].rearrange("p (s j) -> p s j", j=15), reads=[f"PS{b}"], writes=[("p_US", c)])
        self.make_hT([TP], HTb, "p_HT")
        for c in range(8):
            b = 2 + c % 4
            for k in range(8):
                P.I("pe", "matmul", bk[b][:, 0:128], lhsT=w_in[:, k, c * 128:(c + 1) * 128], rhs=HTb[:, k, 0:128],
                                                          start=(k == 0), stop=(k == 7),
                     reads=["pw_in", "p_HT"], writes=[f"PS{b}"])
            self.ev(US[:, c, :, 16:24], bk[b][:, 0:128].rearrange("p (s t) -> p s t", t=8), reads=[f"PS{b}"], writes=[("p_US", c)])
        SA3 = SA[:, 0:NSEQ * 24].rearrange("p (s t) -> p s t", t=24)
        SB3 = SB[:, 0:NSEQ * 24].rearrange("p (s t) -> p s t", t=24)
        for c in range(8):
            U = lambda a, e, c=c: US[:, c, :, a:(24 + e) if e else 24]
            SAv = lambda a, e: SA3[:, :, a:(24 + e) if e else 24]
            SBv = lambda a, e: SB3[:, :, a:(24 + e) if e else 24]
            pooled_chunk(c, U, SAv, SBv, 16, PT[:, c, 0:128].rearrange("p (s t) -> p s t", t=8), False, ("p_US", c))
        for c in range(8):
            P.I("act", "copy", ufm[:, c * 128:(c + 1) * 128].rearrange("p (s t) -> p s t", t=8), US[:, c, :, 16:24],
                 reads=[("p_US", c)], writes=["p_ufm"])
        self.transpose_f32_fm(ufm, 8, utok, "p_ufm", "p_utok")
        for j in range(NSEQ):
            P.Dm("sp", "dma_start", out=self.dout["pool_s"][j * 15 + 7:j * 15 + 15, :], in_=utok[j * 8:(j + 1) * 8, :],
                  reads=["p_utok"])
        self.pool_tail([TP], 128, PT, MT, w_grp, scl, w_out)
        P.pop_scope()

    def transpose_f32_fm(self, src, nch, dst, src_reg, dst_reg, banks=(0, 1)):
        P = self.P
        for g in range(0, nch, 4):
            b = banks[(g // 4) % len(banks)]
            for c in range(4):
                P.I("pe", "transpose", self.bank[b][:, c * 128:(c + 1) * 128],
                                                                src[:, (g + c) * 128:(g + c + 1) * 128], self.ident[:],
                     reads=[src_reg, "ident"], writes=[f"PS{b}"])
            self.ev(dst[:, g * 128:(g + 4) * 128], self.bank[b], reads=[f"PS{b}"], writes=[dst_reg])

    def pool_tail(self, tiles, nb, PT, MT, w_grp, scl, w_out):
        P = self.P
        bk = self.bank
        for c in range(8):
            g, oc = c // 2, c % 2
            b = 2 + c % 4
            for kc in range(2):
                P.I("pe", "matmul", bk[b][:, 0:nb], lhsT=w_grp[:, g * 2 + kc, oc * 128:(oc + 1) * 128],
                                                                   rhs=PT[:, g * 2 + kc, 0:nb], start=(kc == 0), stop=(kc == 1),
                     reads=["pw_grp", "p_PT"], writes=[f"PS{b}"])
            P.I("dve", "tensor_scalar", MT[:, c, 0:nb], bk[b][:, 0:nb], scl[:, c:c + 1], None, ALU.mult,
                 reads=[f"PS{b}", "p_scl"], writes=["p_MT"])
        for n, i in enumerate(tiles):
            self.final_proj_ln(i, MT, "p_MT", w_out, "pw_out", 8, tok0=n * 128)


    def sin_turns(self, out, x, n, xreg, oreg, wi, wf, shift=0.0, eng="dve"):
        P = self.P
        e = eng
        if shift:
            P.I(e, "tensor_scalar", wf, x, shift, None, ALU.add, reads=[xreg], writes=["s_wf"])
            src, sreg = wf, "s_wf"
        else:
            src, sreg = x, xreg
        P.I(e, "tensor_copy", wi, src, reads=[sreg], writes=["s_wi"])
        P.I(e, "tensor_copy", out, wi, reads=["s_wi"], writes=[oreg])
        P.I(e, "tensor_tensor", out, src, out, ALU.subtract, reads=[sreg, oreg], writes=[oreg])
        P.I("dve", "scalar_tensor_tensor", out, out, 0.5, out, ALU.is_gt, ALU.subtract, reads=[oreg], writes=[oreg])
        P.I("dve", "scalar_tensor_tensor", out, out, 0.5, out, ALU.is_gt, ALU.subtract, reads=[oreg], writes=[oreg])
        P.I("act", "activation", out, out, AF.Sin, scale=2.0 * np.pi * (1.0 - 1e-6), reads=[oreg], writes=[oreg])

    def s5(self, l):
        P = self.P
        TP, NT = self.TP, self.NT
        bk = self.bank
        d = self.din
        BLK = min(4, TP)
        NB = BLK * 128
        P.push_scope()
        dcol = P.sb([128, 8], F32, "s_d")
        bglu = P.sb([128, 8], F32, "s_bglu")
        cst = P.sb([128, 12, 32], F32, "s_cst")
        RHO, RT, LBR, LBI, NLBI, FR, FI, T0, T1_, T2_, T3_, T4_ = [cst[:, j, :] for j in range(12)]
        wi32 = P.sb([128, 512], I32, "s_wi")
        wf32 = P.sb([128, 512], F32, "s_wf")
        BbT = P.sb([128, 32, 2, 128], BF16, "s_BbT")
        CT = P.sb([128, 32, 2, 128], BF16, "s_CT")
        Dg = P.sb([128, 8, 128], BF16, "s_Dg")
        HE = P.sb([128, 2, 32, 16], F32, "s_HE")
        H0 = P.sb([128, 2, 32, 16], F32, "s_H0")
        iotaS = P.sb([128, 512], F32, "s_iota")
        mask01 = P.sb([128, 128], F32, "s_m01")
        ones = P.sb([128, 512], F32, "s_ones")
        UTb = P.sb([128, 8, NB], BF16, "s_UT")
        GTb = P.sb([128, 8, NB], BF16, "s_GT")
        P.push_scope()
        aT = P.sb([128, 3, 32], F32, "s_aT")
        bT = P.sb([128, 2, 32, 16], F32, "s_bT")
        cT = P.sb([128, 2, 32, 16], F32, "s_cT")
        bb = P.sb([128, 2, 32, 16], F32, "s_bb")
        P.Dm("sp", "dma_start", out=aT[:], in_=d["s5_aT"], writes=["s_aT"])
        P.Dm("sp", "dma_start", out=bT[:], in_=d["s5_bT"], writes=["s_bT"])
        P.Dm("sp", "dma_start", out=cT[:], in_=d["s5_cT"], writes=["s_cT"])
        P.Dm("sp", "dma_start", out=dcol[:], in_=d["s5_dT"], writes=["s_d"])
        P.Dm("sp", "dma_start", out=bglu[:], in_=d["s5_b_gluT"], writes=["s_bglu"])
        self.load_ln(self.ln1_g, self.ln1_b, l)
        P.I("pool", "iota", iotaS[:], pattern=[[1, 512]], base=0, channel_multiplier=0,
                                      allow_small_or_imprecise_dtypes=True, writes=["s_iota"])
        P.I("dve", "memset", ones[:], 1.0, writes=["s_ones"])
        P.I("dve", "memset", mask01[:], 1.0, writes=["s_m01"])
        P.I("dve", "memset", mask01[:].rearrange("p (s t) -> p s t", t=8)[:, :, 0:1], 0.0, writes=["s_m01"])
        P.I("dve", "memset", HE[:], 0.0, writes=["s_HE"])
        A_RE, A_IM, LDT = aT[:, 0, :], aT[:, 1, :], aT[:, 2, :]
        C = "s_cst"

        def tt(out, a, b, op, eng="dve", extra=()):
            P.I(eng, "tensor_tensor", out, a, b, op, reads=[C, "s_aT"] + list(extra), writes=[C])
        P.I("act", "activation", T0, LDT, AF.Exp, reads=["s_aT"], writes=[C])
        tt(T1_, A_RE, T0, ALU.mult)
        P.I("act", "activation", RHO, T1_, AF.Exp, reads=[C], writes=[C])
        tt(T2_, A_IM, T0, ALU.mult)
        P.I("dve", "tensor_scalar", RT, T2_, 1.0 / (2.0 * np.pi), None, ALU.mult, reads=[C], writes=[C])
        self.sin_turns(T3_, RT, 32, C, C, wi32[:, 0:32], wf32[:, 0:32])
        self.sin_turns(T4_, RT, 32, C, C, wi32[:, 0:32], wf32[:, 0:32], shift=0.25)
        tt(LBR, RHO, T4_, ALU.mult)
        tt(LBI, RHO, T3_, ALU.mult)
        P.I("dve", "tensor_scalar", NLBI, LBI, -1.0, None, ALU.mult, reads=[C], writes=[C])
        tt(T0, A_RE, A_RE, ALU.mult)
        tt(T1_, A_IM, A_IM, ALU.mult)
        tt(T0, T0, T1_, ALU.add)
        P.I("dve", "reciprocal", T0, T0, reads=[C], writes=[C])
        P.I("dve", "tensor_scalar", T1_, LBR, -1.0, None, ALU.add, reads=[C], writes=[C])
        tt(T2_, T1_, A_RE, ALU.mult)
        tt(T3_, LBI, A_IM, ALU.mult)
        tt(T2_, T2_, T3_, ALU.add)
        tt(FR, T2_, T0, ALU.mult)
        tt(T2_, LBI, A_RE, ALU.mult)
        tt(T3_, T1_, A_IM, ALU.mult)
        tt(T2_, T2_, T3_, ALU.subtract)
        tt(FI, T2_, T0, ALU.mult)
        frb = FR.unsqueeze(2).to_broadcast([128, 32, 16])
        fib = FI.unsqueeze(2).to_broadcast([128, 32, 16])
        tmpb = P.sb([128, 32, 16], F32, "s_tmpb")
        P.I("dve", "tensor_tensor", bb[:, 0], frb, bT[:, 0], ALU.mult, reads=[C, "s_bT"], writes=["s_bb"])
        P.I("dve", "tensor_tensor", tmpb[:], fib, bT[:, 1], ALU.mult, reads=[C, "s_bT"], writes=["s_tmpb"])
        P.I("dve", "tensor_tensor", bb[:, 0], bb[:, 0], tmpb[:], ALU.subtract, reads=["s_bb", "s_tmpb"], writes=["s_bb"])
        P.I("dve", "tensor_tensor", bb[:, 1], frb, bT[:, 1], ALU.mult, reads=[C, "s_bT", "s_bb"], writes=["s_bb"])
        P.I("dve", "tensor_tensor", tmpb[:], fib, bT[:, 0], ALU.mult, reads=[C, "s_bT", "s_bb"], writes=["s_tmpb"])
        P.I("dve", "tensor_tensor", bb[:, 1], bb[:, 1], tmpb[:], ALU.add, reads=["s_bb", "s_tmpb"], writes=["s_bb"])
        pads = P.sb([128, 2, 4, 128], F32, "s_pads")
        P.I("dve", "memset", pads[:], 0.0, writes=["s_pads"])
        for gp in range(32):
            q = gp % 4
            for ri in range(2):
                for g2 in range(2):
                    P.I("pool", "tensor_copy",
                        pads[g2 * 64:(g2 + 1) * 64, ri, q, q * 32 + g2 * 16:q * 32 + g2 * 16 + 16],
                        bb[g2 * 64:(g2 + 1) * 64, ri, gp, :], reads=["s_bb", "s_pads"], writes=["s_pads"])
            b = gp % 2
            for ri in range(2):
                P.I("pe", "transpose", bk[b][:, ri * 128:(ri + 1) * 128], pads[:, ri, q, :], self.ident[:],
                     reads=["s_pads", "ident"], writes=[f"PS{b}"])
            self.ev(BbT[:, gp, :, :], bk[b][:, 0:256].rearrange("p (r n) -> p r n", r=2), reads=[f"PS{b}"], writes=["s_BbT"])
        P.I("dve", "memset", CT[:], 0.0, writes=["s_CT"])
        P.I("dve", "tensor_scalar", cT[:, 1], cT[:, 1], -1.0, None, ALU.mult, reads=["s_cT"], writes=["s_cT"])
        for q in range(4):
            for ri in range(2):
                for g2 in range(2):
                    dst = CT[g2 * 64:(g2 + 1) * 64, :, ri, :].rearrange("p (a q) r -> p a q r", q=4)[:, :, q, q * 32 + g2 * 16:q * 32 + g2 * 16 + 16]
                    src = cT[g2 * 64:(g2 + 1) * 64, ri].rearrange("p (a q) i -> p a q i", q=4)[:, :, q, :]
                    P.I("dve", "tensor_copy", dst, src, reads=["s_cT"], writes=["s_CT"])
        for c in range(8):
            P.I("dve", "tensor_scalar", Dg[:, c, :], self.ident[:], dcol[:, c:c + 1], None, ALU.mult,
                 reads=["ident", "s_d"], writes=["s_Dg"])
        st = P.sb([16, 2, 4096], F32, "s_st")
        P.Dm("sp", "dma_start", out=st[:, 0, :], in_=d["state_s5_re"], writes=["s_st"])
        P.Dm("sp", "dma_start", out=st[:, 1, :], in_=d["state_s5_im"], writes=["s_st"])
        for ri in range(2):
            for g8 in range(4):
                b = g8 % 2
                for j in range(8):
                    gp = g8 * 8 + j
                    P.I("pe", "transpose", bk[b][:, j * 16:(j + 1) * 16], st[:, ri, gp * 128:(gp + 1) * 128],
                                                                      self.ident[0:16, 0:16],
                         reads=["s_st", "ident"], writes=[f"PS{b}"])
                self.ev(H0[:, ri, g8 * 8:(g8 + 1) * 8, :], bk[b][:, 0:128].rearrange("p (g s) -> p g s", s=16),
                        reads=[f"PS{b}"], writes=["s_H0"])

        P.pop_scope()
        nblk = (TP + BLK - 1) // BLK
        blocks = [(list(range(bi * BLK, min(TP, (bi + 1) * BLK))), False) for bi in range(nblk)] + [([TP], True)]
        for bidx, (tiles, samp) in enumerate(blocks):
            N = len(tiles) * 128
            P.push_scope()
            w_in = P.sb([128, 8, 1024], BF16, "sw_in")
            HTb = P.sb([128, 8, NB], BF16, "s_HT")
            self.load_w(w_in, d["s5_w_in"], "sw_in")
            self.make_hT(tiles, HTb, "s_HT")
            for c in range(8):
                b = 2 + c % 4
                for k in range(8):
                    P.I("pe", "matmul", bk[b][:, 0:N], lhsT=w_in[:, k, c * 128:(c + 1) * 128], rhs=HTb[:, k, 0:N],
                                                              start=(k == 0), stop=(k == 7), reads=["sw_in", "s_HT"], writes=[f"PS{b}"])
                self.ev(UTb[:, c, 0:N], bk[b][:, 0:N], reads=[f"PS{b}"], writes=[("s_UT", c)])
            P.pop_scope()
            P.push_scope()
            tab = [P.sb([128, 3, 512], F32, f"s_tab{j}") for j in range(2)]
            wk4_2 = [[P.sb([128, 512], F32, f"s_w{j}_{z}") for j in range(4)] for z in range(2)]
            BR_2 = [P.sb([128, 512], F32, f"s_BR{z}") for z in range(2)]; BI_2 = [P.sb([128, 512], F32, f"s_BI{z}") for z in range(2)]
            GR_2 = [P.sb([128, 512], F32, f"s_GR{z}") for z in range(2)]; GI_2 = [P.sb([128, 512], F32, f"s_GI{z}") for z in range(2)]
            HRb = [P.sb([128, 512], BF16, f"s_HR{j}") for j in range(2)]
            HIb = [P.sb([128, 512], BF16, f"s_HI{j}") for j in range(2)]
            inj = P.sb([128, 4, 16], F32, "s_inj")
            ns = NSEQ if samp else 1

            def V(t, n=N):
                return t[:, 0:n].rearrange("p (s t) -> p s t", t=8) if samp else t[:, 0:n]

            def TV(t):
                return t[:, 0:8].unsqueeze(1).to_broadcast([128, NSEQ, 8]) if samp else t[:, 0:N]

            def starts(t):
                return t[:, 0:N].rearrange("p (s t) -> p s t", t=8)[:, :, 0] if samp else t[:, 0:1]

            def ends(t):
                return t[:, 0:N].rearrange("p (s t) -> p s t", t=8)[:, :, 7] if samp else t[:, N - 1:N]
            nloc = 8 if samp else N

            def X(eng, meth, reads, writes, *args, **kw):
                if eng == "sp_dma":
                    P.dma("sp", lambda h: getattr(h, meth)(*args, **kw), reads=reads, writes=writes)
                else:
                    P.op(eng, lambda h: getattr(h, meth)(*args, **kw), reads=reads, writes=writes)
            for c in range(8):
                by = c % 2
                X("pe", "matmul", ["s_Dg", ("s_UT", c)], [f"PS{by}"], bk[by][:, 0:N], lhsT=Dg[:, c, :], rhs=UTb[:, c, 0:N], start=True, stop=False)
                for q in range(4):
                    gp = 4 * c + q
                    sl = gp % 2
                    P.flush(6, "cv0")
                    wk4, BR, BI, GR, GI = wk4_2[sl], BR_2[sl], BI_2[sl], GR_2[sl], GI_2[sl]
                    Z = f"_{sl}"
                    tb, treg = tab[sl], f"s_tab{sl}"
                    cosT, sinT, rhoT = tb[:, 0, :], tb[:, 1, :], tb[:, 2, :]
                    X("dve", "tensor_scalar", ["s_iota", C], [treg], rhoT[:, 0:nloc], iotaS[:, 0:nloc], RT[:, gp:gp + 1], None, ALU.mult)
                    self.sin_turns(sinT[:, 0:nloc], rhoT[:, 0:nloc], nloc, treg, treg, wi32[:, 0:nloc], wf32[:, 0:nloc], eng="dve")
                    self.sin_turns(cosT[:, 0:nloc], rhoT[:, 0:nloc], nloc, treg, treg, wi32[:, 0:nloc], wf32[:, 0:nloc], shift=0.25, eng="dve")
                    if samp:
                        X("dve", "tensor_scalar", ["s_m01", C, treg], [treg], rhoT[:, 0:N], mask01[:, 0:N], RHO[:, gp:gp + 1], None, ALU.mult)
                    else:
                        X("dve", "tensor_scalar", ["s_ones", C, treg], [treg], rhoT[:, 0:N], ones[:, 0:N], RHO[:, gp:gp + 1], None, ALU.mult)
                    for ri in range(2):
                        X("pe", "matmul", ["s_BbT", ("s_UT", c)], [f"PS{2 + ri}"], bk[2 + ri][:, 0:N], lhsT=BbT[:, gp, ri, :], rhs=UTb[:, c, 0:N],
                          start=True, stop=True)
                    pr, pi = bk[2][:, 0:N], bk[3][:, 0:N]
                    if samp:
                        pr = pr.rearrange("p (s t) -> p s t", t=8); pi = pi.rearrange("p (s t) -> p s t", t=8)
                    X("dve", "tensor_tensor", ["PS2", treg], ["s_w0" + Z], V(wk4[0]), pr, TV(cosT), ALU.mult)
                    X("dve", "tensor_tensor", ["PS3", treg], ["s_w1" + Z], V(wk4[1]), pi, TV(sinT), ALU.mult)
                    X("dve", "tensor_tensor", ["PS3", treg], ["s_w2" + Z], V(wk4[2]), pi, TV(cosT), ALU.mult)
                    X("dve", "tensor_tensor", ["PS2", treg], ["s_w3" + Z], V(wk4[3]), pr, TV(sinT), ALU.mult)
                    X("dve", "tensor_tensor", ["s_w0" + Z, "s_w1" + Z], ["s_BR" + Z], BR[:, 0:N], wk4[0][:, 0:N], wk4[1][:, 0:N], ALU.add)
                    X("dve", "tensor_tensor", ["s_w2" + Z, "s_w3" + Z], ["s_BI" + Z], BI[:, 0:N], wk4[2][:, 0:N], wk4[3][:, 0:N], ALU.subtract)
                    if samp or bidx > 0:
                        hp = H0 if samp else HE
                        hreg = "s_H0" if samp else "s_HE"
                        hr, hi = hp[:, 0, gp, 0:ns], hp[:, 1, gp, 0:ns]
                        X("dve", "tensor_scalar", [hreg, C], ["s_inj"], inj[:, 0, 0:ns], hr, LBR[:, gp:gp + 1], None, ALU.mult)
                        X("dve", "scalar_tensor_tensor", [hreg, C, "s_inj"], ["s_inj"], inj[:, 0, 0:ns], hi, NLBI[:, gp:gp + 1], inj[:, 0, 0:ns], ALU.mult, ALU.add)
                        X("dve", "tensor_scalar", [hreg, C, "s_inj"], ["s_inj"], inj[:, 1, 0:ns], hi, LBR[:, gp:gp + 1], None, ALU.mult)
                        X("dve", "scalar_tensor_tensor", [hreg, C, "s_inj"], ["s_inj"], inj[:, 1, 0:ns], hr, LBI[:, gp:gp + 1], inj[:, 1, 0:ns], ALU.mult, ALU.add)
                        X("dve", "tensor_tensor", ["s_BR" + Z, "s_inj"], ["s_BR" + Z], starts(BR), starts(BR), inj[:, 0, 0:ns], ALU.add)
                        X("dve", "tensor_tensor", ["s_BI" + Z, "s_inj"], ["s_BI" + Z], starts(BI), starts(BI), inj[:, 1, 0:ns], ALU.add)
                    X("dve", "tensor_tensor_scan", [treg, "s_BR" + Z], ["s_GR" + Z], GR[:, 0:N], rhoT[:, 0:N], BR[:, 0:N], 0.0, ALU.mult, ALU.add)
                    X("dve", "tensor_tensor_scan", [treg, "s_BI" + Z], ["s_GI" + Z], GI[:, 0:N], rhoT[:, 0:N], BI[:, 0:N], 0.0, ALU.mult, ALU.add)
                    hs = gp % 2
                    X("dve", "tensor_tensor", ["s_GR" + Z, treg], ["s_w0" + Z], V(wk4[0]), V(GR), TV(cosT), ALU.mult)
                    X("dve", "tensor_tensor", ["s_GI" + Z, treg], ["s_w1" + Z], V(wk4[1]), V(GI), TV(sinT), ALU.mult)
                    X("dve", "tensor_tensor", ["s_GR" + Z, treg], ["s_w2" + Z], V(wk4[2]), V(GR), TV(sinT), ALU.mult)
                    X("dve", "tensor_tensor", ["s_GI" + Z, treg], ["s_w3" + Z], V(wk4[3]), V(GI), TV(cosT), ALU.mult)
                    X("dve", "tensor_tensor", ["s_w0" + Z, "s_w1" + Z], [f"s_HR{hs}"], HRb[hs][:, 0:N], wk4[0][:, 0:N], wk4[1][:, 0:N], ALU.subtract)
                    X("dve", "tensor_tensor", ["s_w2" + Z, "s_w3" + Z], [f"s_HI{hs}"], HIb[hs][:, 0:N], wk4[2][:, 0:N], wk4[3][:, 0:N], ALU.add)
                    X("dve", "tensor_tensor", ["s_w0" + Z, "s_w1" + Z], ["s_HE"], HE[:, 0, gp, 0:ns], ends(wk4[0]), ends(wk4[1]), ALU.subtract)
                    X("dve", "tensor_tensor", ["s_w2" + Z, "s_w3" + Z], ["s_HE"], HE[:, 1, gp, 0:ns], ends(wk4[2]), ends(wk4[3]), ALU.add)
                    X("pe", "matmul", ["s_CT", f"s_HR{hs}"], [f"PS{by}"], bk[by][:, 0:N], lhsT=CT[:, gp, 0, :], rhs=HRb[hs][:, 0:N], start=False, stop=False)
                    X("pe", "matmul", ["s_CT", f"s_HI{hs}"], [f"PS{by}"], bk[by][:, 0:N], lhsT=CT[:, gp, 1, :], rhs=HIb[hs][:, 0:N], start=False, stop=(q == 3))
                X("act", "activation", [f"PS{by}"], [("s_GT", c)], GTb[:, c, 0:N], bk[by][:, 0:N], AF.Gelu_apprx_tanh)
            P.pop_scope()
            if samp or bidx == nblk - 1:
                nm = ("s5_re_s", "s5_im_s") if samp else ("s5_re_p", "s5_im_p")
                P.push_scope()
                so = P.sb([16, 2, 4096], F32, "s_so")
                for ri in range(2):
                    if samp:
                        for g8 in range(8):
                            b = g8 % 2
                            for j in range(4):
                                gp = g8 * 4 + j
                                X("pe", "transpose", ["s_HE", "ident"], [f"PS{b}"], bk[b][0:16, j * 128:(j + 1) * 128], HE[:, ri, gp, :], self.ident[:])
                            self.ev(so[:, ri, g8 * 512:(g8 + 1) * 512], bk[b][0:16, :], reads=[f"PS{b}"], writes=["s_so"])
                        X("sp_dma", "dma_start", ["s_so"], [], out=self.dout[nm[ri]], in_=so[:, ri, :])
                    else:
                        X("pe", "transpose", ["s_HE", "ident"], [f"PS{ri}"], bk[ri][0:32, 0:128], HE[:, ri, :, 0], self.ident[:])
                        X("dve", "tensor_copy", [f"PS{ri}"], ["tmp"], self.tmp[0:32, ri * 128:(ri + 1) * 128], bk[ri][0:32, 0:128])
                        X("sp_dma", "dma_start", ["tmp"], [], out=self.dout[nm[ri]], in_=self.tmp[0:32, ri * 128:(ri + 1) * 128])
                P.pop_scope()
            P.push_scope()
            w_glu = P.sb([128, 8, 1024], BF16, "sw_glu")
            w_out = P.sb([128, 8, 1024], BF16, "sw_out")
            OT = P.sb([128, 8, NB], BF16, "s_OT")
            SG = [P.sb([128, 512], F32, f"s_SG{j}") for j in range(2)]
            self.load_w(w_glu, d["s5_w_glu"], "sw_glu")
            self.load_w(w_out, d["s5_w_out"], "sw_out")
            for oc in range(8):
                b = 2 + oc % 4
                for k in range(8):
                    P.I("pe", "matmul", bk[b][:, 0:N], lhsT=w_glu[:, k, oc * 128:(oc + 1) * 128], rhs=GTb[:, k, 0:N],
                                                               start=(k == 0), stop=(k == 7), reads=["sw_glu", ("s_GT", k)], writes=[f"PS{b}"])
                sg = SG[oc % 2]
                P.I("act", "activation", sg[:, 0:N], bk[b][:, 0:N], AF.Sigmoid, bias=bglu[:, oc:oc + 1],
                     reads=[f"PS{b}", "s_bglu"], writes=[f"s_SG{oc % 2}"])
                P.I("dve", "tensor_tensor", OT[:, oc, 0:N], GTb[:, oc, 0:N], sg[:, 0:N], ALU.mult,
                     reads=[("s_GT", oc), f"s_SG{oc % 2}"], writes=["s_OT"])
            for n, i in enumerate(tiles):
                self.final_proj_ln(i, OT, "s_OT", w_out, "sw_out", 8, tok0=n * 128)
            P.pop_scope()
        P.pop_scope()


    def ssd(self, l):
        P = self.P
        nc = self.nc
        TP, NT = self.TP, self.NT
        bk = self.bank
        d = self.din
        LP = TP * 128
        BLK = min(4, TP)
        NB = BLK * 128
        scr_z = nc.dram_tensor("scr_z", [NT * 128, 2048], F32, kind="Internal").ap()
        scr_x = nc.dram_tensor("scr_x", [24, 128, 3 + LP], F32, kind="Internal").ap()
        scr_xs = nc.dram_tensor("scr_xs", [24, 128, NSEQ * 11], F32, kind="Internal").ap()
        P.push_scope()
        DT = P.sb([128, NT, 32], F32, "d_DT")
        hd = P.sb([128, 96], F32, "d_hd")
        cwT = P.sb([128, 24, 4], F32, "d_cw")
        cbT = P.sb([128, 24], F32, "d_cb")
        ngT = P.sb([128, 16], F32, "d_ng")
        P.Dm("sp", "dma_start", out=hd[:], in_=d["ssd_hd"][0:1, :].to_broadcast([128, 96]), writes=["d_hd"])
        P.Dm("sp", "dma_start", out=cwT[:], in_=d["ssd_conv_wT"], writes=["d_cw"])
        P.Dm("sp", "dma_start", out=cbT[:], in_=d["ssd_conv_bT"], writes=["d_cb"])
        P.Dm("sp", "dma_start", out=ngT[:], in_=d["ssd_norm_gT"], writes=["d_ng"])
        P.I("act", "activation", hd[:, 32:64], hd[:, 32:64], AF.Exp, reads=["d_hd"], writes=["d_hd"])
        P.I("dve", "tensor_scalar", hd[:, 32:64], hd[:, 32:64], -1.0, None, ALU.mult, reads=["d_hd"], writes=["d_hd"])
        self.load_ln(self.ln1_g, self.ln1_b, l)
        P.push_scope()
        w_in = P.sb([128, 8, 5152], BF16, "dw_in")
        HTb = P.sb([128, 8, NB], BF16, "d_HT")
        stage = [P.sb([128, 512], F32, f"d_stg{j}") for j in range(2)]
        zst = P.sb([128, 2048], F32, "d_zst")
        cstt = P.sb([48, 3072], F32, "d_cst")
        self.load_w(w_in, d["ssd_w_in"], "dw_in")
        P.I("dve", "memset", zst[:, 0:72], 0.0, writes=["d_zst"])
        P.Dm("sp", "dma_start", out=scr_x[:, :, 0:3].rearrange("c p t -> p c t"), in_=zst[:, 0:72].rearrange("p (c t) -> p c t", t=3),
             reads=["d_zst"], writes=["scr_x"])
        P.Dm("sp", "dma_start", out=cstt[:], in_=d["state_ssd_conv"], writes=["d_cst"])
        for c in range(24):
            b = c % 2
            P.I("pe", "transpose", bk[b][:, 0:48], cstt[:, c * 128:(c + 1) * 128], self.ident[0:48, 0:48],
                reads=["d_cst", "ident"], writes=[f"PS{b}"])
            sg = stage[c % 2]
            self.ev(sg[:, 0:48], bk[b][:, 0:48], reads=[f"PS{b}"], writes=[f"d_stg{c % 2}"])
            P.Dm("sp", "dma_start", out=scr_xs[c].rearrange("p (s t) -> p s t", t=11)[:, :, 0:3],
                 in_=sg[:, 0:48].rearrange("p (s t) -> p s t", t=3), reads=[f"d_stg{c % 2}"], writes=["scr_xs"])
        nblk = (TP + BLK - 1) // BLK
        blocks = [(list(range(bi * BLK, min(TP, (bi + 1) * BLK))), False) for bi in range(nblk)] + [([TP], True)]
        for bidx, (tiles, samp) in enumerate(blocks):
            N = len(tiles) * 128
            tok0 = tiles[0] * 128
            self.make_hT(tiles, HTb, "d_HT")
            for c in range(24):
                b = 2 + c % 4
                for k in range(8):
                    P.I("pe", "matmul", bk[b][:, 0:N], lhsT=w_in[:, k, 2048 + c * 128:2048 + (c + 1) * 128], rhs=HTb[:, k, 0:N],
                        start=(k == 0), stop=(k == 7), reads=["dw_in", "d_HT"], writes=[f"PS{b}"])
                sg = stage[c % 2]
                self.ev(sg[:, 0:N], bk[b][:, 0:N], reads=[f"PS{b}"], writes=[f"d_stg{c % 2}"])
                if samp:
                    P.Dm("sp", "dma_start", out=scr_xs[c].rearrange("p (s t) -> p s t", t=11)[:, :, 3:11],
                         in_=sg[:, 0:128].rearrange("p (s t) -> p s t", t=8), reads=[f"d_stg{c % 2}"], writes=["scr_xs"])
                else:
                    P.Dm("sp", "dma_start", out=scr_x[c, :, 3 + tok0:3 + tok0 + N], in_=sg[:, 0:N],
                         reads=[f"d_stg{c % 2}"], writes=["scr_x"])
            for n, i in enumerate(tiles):
                hTi = HTb[:, :, n * 128:(n + 1) * 128]
                for nn in range(4):
                    b = 2 + nn
                    for k in range(8):
                        P.I("pe", "matmul", bk[b], lhsT=hTi[:, k, :], rhs=w_in[:, k, nn * 512:(nn + 1) * 512],
                            start=(k == 0), stop=(k == 7), reads=["dw_in", "d_HT"], writes=[f"PS{b}"])
                    self.ev(zst[:, nn * 512:(nn + 1) * 512], bk[b], reads=[f"PS{b}"], writes=["d_zst"])
                P.Dm("sp", "dma_start", out=scr_z[i * 128:(i + 1) * 128, :], in_=zst[:], reads=["d_zst"], writes=["scr_z"])
                for k in range(8):
                    P.I("pe", "matmul", bk[6][:, 0:32], lhsT=hTi[:, k, :], rhs=w_in[:, k, 5120:5152],
                        start=(k == 0), stop=(k == 7), reads=["dw_in", "d_HT"], writes=["PS6"])
                P.I("dve", "tensor_tensor", DT[:, i, :], bk[6][:, 0:32], hd[:, 0:32], ALU.add, reads=["PS6", "d_hd"], writes=["d_DT"])
                P.I("act", "activation", DT[:, i, :], DT[:, i, :], AF.Exp, reads=["d_DT"], writes=["d_DT"])
                P.I("act", "activation", DT[:, i, :], DT[:, i, :], AF.Ln, bias=self.one_t[:, 0:1], reads=["d_DT", "one"], writes=["d_DT"])
                if samp or i == TP - 1:
                    for hf in range(2):
                        for nn in range(3):
                            b = 2 + nn
                            c0 = 2048 + hf * 1536 + nn * 512
                            for k in range(8):
                                P.I("pe", "matmul", bk[b], lhsT=hTi[:, k, :], rhs=w_in[:, k, c0:c0 + 512],
                                    start=(k == 0), stop=(k == 7), reads=["dw_in", "d_HT"], writes=[f"PS{b}"])
                            self.ev(zst[:, nn * 512:(nn + 1) * 512], bk[b], reads=[f"PS{b}"], writes=["d_zst"])
                        if samp:
                            for j in range(NSEQ):
                                P.Dm("sp", "dma_start", out=self.dout["conv_s"][j * 3:j * 3 + 3, hf * 1536:(hf + 1) * 1536],
                                     in_=zst[j * 8 + 5:j * 8 + 8, 0:1536], reads=["d_zst"])
                        else:
                            P.Dm("sp", "dma_start", out=self.dout["conv_p"][:, hf * 1536:(hf + 1) * 1536], in_=zst[125:128, 0:1536],
                                 reads=["d_zst"])
        P.pop_scope()
        P.push_scope()
        w_out = P.sb([128, 16, 1024], BF16, "dw_out")
        self.load_w(w_out, d["ssd_w_out"], "dw_out")
        for k in range(16):
            P.I("dve", "tensor_scalar", w_out[:, k, :], w_out[:, k, :], ngT[:, k:k + 1], None, ALU.mult,
                reads=["dw_out", "d_ng"], writes=["dw_out"])
        mk = P.sb([128, 8, 128], F32, "d_mk")
        ONES, TRI, STRICT, BONES, TRIS = [mk[:, j, :] for j in range(5)]
        mki = P.sb([128, 2, 128], I32, "d_mki")
        mcol = P.sb([128, 24], F32, "d_mcol")
        M = "d_mk"
        P.I("dve", "memset", ONES, 1.0, writes=[M])
        P.I("pool", "affine_select", TRI, ONES, pattern=[[1, 128]], compare_op=ALU.is_ge, fill=0.0, base=0, channel_multiplier=-1,
            reads=[M], writes=[M])
        P.I("pool", "affine_select", STRICT, ONES, pattern=[[-1, 128]], compare_op=ALU.is_gt, fill=0.0, base=0, channel_multiplier=1,
            reads=[M], writes=[M])
        P.I("pool", "iota", mki[:, 0, :], pattern=[[1, 128]], base=0, channel_multiplier=0, reads=[], writes=["d_mki"])
        P.I("pool", "iota", mki[:, 1, :], pattern=[[0, 128]], base=0, channel_multiplier=1, reads=["d_mki"], writes=["d_mki"])
        P.I("dve", "tensor_single_scalar", mki[:].rearrange("p a n -> p (a n)"), mki[:].rearrange("p a n -> p (a n)"), 3, ALU.arith_shift_right,
            reads=["d_mki"], writes=["d_mki"])
        P.I("dve", "tensor_copy", mk[:, 5:7, :], mki[:], reads=["d_mki"], writes=[M])
        P.I("dve", "tensor_tensor", BONES, mk[:, 5, :], mk[:, 6, :], ALU.is_equal, reads=[M], writes=[M])
        P.I("dve", "tensor_tensor", TRIS, BONES, TRI, ALU.mult, reads=[M], writes=[M])
        P.I("dve", "tensor_tensor", mcol[:, 0:16], self.iota16[:], mk[:, 6, 0:16], ALU.is_equal, reads=[M, "iota16"], writes=["d_mcol"])
        P.I("dve", "tensor_single_scalar", mcol[:, 16:17], mk[:, 6, 0:1], 8.0, ALU.is_lt, reads=[M, "d_mcol"], writes=["d_mcol"])
        P.I("dve", "tensor_single_scalar", mcol[:, 17:18], mk[:, 6, 0:1], 8.0, ALU.is_ge, reads=[M, "d_mcol"], writes=["d_mcol"])
        XB = P.sb([128, 8, 176], F32, "d_XB")
        CA2 = [P.sb([128, 128], F32, f"d_CA{j}") for j in range(2)]
        XCf = P.sb([128, 8, 128], F32, "d_XCf")
        BTb = P.sb([128, 4, 128], BF16, "d_BTb")
        CTb = P.sb([128, 4, 128], BF16, "d_CTb")
        Xb = P.sb([128, 2048], BF16, "d_Xb")
        Btb = P.sb([128, 512], BF16, "d_Btb")
        Xw = P.sb([128, 2048], BF16, "d_Xw")
        Y = P.sb([128, 2048], F32, "d_Y")
        zt = P.sb([128, 2048], F32, "d_zt")
        CBm = P.sb([128, 4, 128], F32, "d_CBm")
        R16 = P.sb([128, 16, 128], F32, "d_R16")
        DEC = [P.sb([128, 128], F32, f"d_DEC{j}") for j in range(4)]
        MTb = [P.sb([128, 128], BF16, f"d_MT{j}") for j in range(4)]
        YT = P.sb([128, 16, 128], BF16, "d_YT")
        sv = P.sb([128, 8, 32], F32, "d_sv")
        DTA, ACS, ALAST, EA, DE, WSC, CDP, STMP = [sv[:, j, :] for j in range(8)]
        ss = P.sb([128, 8], F32, "d_ss")
        S = "d_sv"
        A32, D32 = hd[:, 32:64], hd[:, 64:96]

        def bc(a):
            return a.unsqueeze(2).to_broadcast([128, 32, 64])

        def v3(t):
            return t[:, :].rearrange("p (h e) -> p h e", e=64)
        ps4 = lambda b0: self.psum[:, b0:b0 + 4, :].rearrange("p a n -> p (a n)")

        def tile_front(i, samp):
            tri = TRIS if samp else TRI
            tok0 = i * 128
            for cg in range(3):
                if samp:
                    P.Dm("sp", "dma_start", out=XB[:, :, 0:176], in_=scr_xs[cg * 8:(cg + 1) * 8].rearrange("c p t -> p c t"),
                         reads=["scr_xs"], writes=["d_XB"])
                else:
                    P.Dm("sp", "dma_start", out=XB[:, :, 0:131], in_=scr_x[cg * 8:(cg + 1) * 8, :, tok0:tok0 + 131].rearrange("c p t -> p c t"),
                         reads=["scr_x"], writes=["d_XB"])
                for cc in range(8):
                    c = cg * 8 + cc
                    CA, CAr = CA2[cc % 2], f"d_CA{cc % 2}"
                    if samp:
                        xv = lambda k: XB[:, cc, :].rearrange("p (s t) -> p s t", t=11)[:, :, k:k + 8]
                        cav = CA[:, :].rearrange("p (s t) -> p s t", t=8)
                    else:
                        xv = lambda k: XB[:, cc, k:k + 128]
                        cav = CA[:, :]
                    P.I("dve", "tensor_scalar", cav, xv(0), cwT[:, c, 0:1], None, ALU.mult, reads=["d_XB", "d_cw"], writes=[CAr])
                    for k in range(1, 4):
                        P.I("dve", "scalar_tensor_tensor", cav, xv(k), cwT[:, c, k:k + 1], cav, ALU.mult, ALU.add,
                            reads=["d_XB", "d_cw", CAr], writes=[CAr])
                    if cg < 2:
                        P.I("act", "activation", XCf[:, cc, :], CA[:, :], AF.Silu, bias=cbT[:, c:c + 1], reads=[CAr, "d_cb"], writes=["d_XCf"])
                    else:
                        if cc < 4:
                            P.I("act", "activation", XCf[:, cc, :], CA[:, :], AF.Silu, bias=cbT[:, c:c + 1], reads=[CAr, "d_cb"], writes=["d_XCf"])
                            P.I("dve", "tensor_copy", BTb[:, cc, :], XCf[:, cc, :], reads=["d_XCf"], writes=["d_BTb"])
                        else:
                            P.I("act", "activation", CTb[:, cc - 4, :], CA[:, :], AF.Silu, bias=cbT[:, c:c + 1], reads=[CAr, "d_cb"], writes=["d_CTb"])
                if cg < 2:
                    for g4 in range(2):
                        b = g4
                        for c4 in range(4):
                            P.I("pe", "transpose", bk[b][:, c4 * 128:(c4 + 1) * 128], XCf[:, g4 * 4 + c4, :], self.ident[:],
                                reads=["d_XCf", "ident"], writes=[f"PS{b}"])
                        self.ev(Xb[:, cg * 1024 + g4 * 512:cg * 1024 + (g4 + 1) * 512], bk[b], reads=[f"PS{b}"], writes=["d_Xb"])
                else:
                    for c4 in range(4):
                        P.I("pe", "transpose", bk[0][:, c4 * 128:(c4 + 1) * 128], XCf[:, c4, :], self.ident[:],
                            reads=["d_XCf", "ident"], writes=["PS0"])
                    self.ev(Btb[:, :], bk[0], reads=["PS0"], writes=["d_Btb"])
            dtv = DT[:, i, :]
            P.I("dve", "tensor_tensor", DTA, dtv, A32, ALU.mult, reads=["d_DT", "d_hd"], writes=[S])
            P.I("pe", "matmul", bk[4][:, 0:32], lhsT=tri, rhs=DTA, start=True, stop=True, reads=[M, S], writes=["PS4"])
            P.I("pe", "matmul", bk[4][:, 32:64], lhsT=(BONES if samp else ONES), rhs=DTA, start=True, stop=True, reads=[M, S], writes=["PS4"])
            P.I("dve", "tensor_copy", sv[:, 1:3, :], bk[4][:, 0:64].rearrange("p (a n) -> p a n", a=2), reads=["PS4"], writes=[S])
            P.I("act", "activation", EA, ACS, AF.Exp, reads=[S], writes=[S])
            P.I("dve", "tensor_tensor", STMP, ALAST, ACS, ALU.subtract, reads=[S], writes=[S])
            P.I("act", "activation", DE, STMP, AF.Exp, reads=[S], writes=[S])
            P.I("dve", "tensor_tensor", WSC, dtv, DE, ALU.mult, reads=[S, "d_DT"], writes=[S])
            P.I("act", "activation", CDP, ALAST, AF.Exp, reads=[S], writes=[S])
            P.I("dve", "tensor_tensor", v3(Xw), v3(Xb), bc(WSC), ALU.mult, reads=["d_Xb", S], writes=["d_Xw"])
            for g in range(4):
                P.I("pe", "matmul", bk[5][:, g * 128:(g + 1) * 128], lhsT=BTb[:, g, :], rhs=CTb[:, g, :], start=True, stop=True,
                    reads=["d_BTb", "d_CTb"], writes=["PS5"])
            P.I("dve", "tensor_tensor", CBm[:], bk[5].rearrange("p (g n) -> p g n", g=4), tri.unsqueeze(1).to_broadcast([128, 4, 128]), ALU.mult,
                reads=["PS5", M], writes=["d_CBm"])

        def tile_back(i, samp, yoff_ap, yoff_regs):
            tri = TRIS if samp else TRI
            dtv = DT[:, i, :]
            P.I("dve", "tensor_tensor", v3(Y), v3(Xb), bc(D32), ALU.mult, reads=["d_Xb", "d_hd"], writes=["d_Y"])
            P.I("dve", "tensor_tensor", v3(zt), yoff_ap.rearrange("p (h e) -> p h e", e=64), bc(EA), ALU.mult,
                reads=list(yoff_regs) + [S], writes=["d_zt"])
            P.I("pool", "tensor_tensor", Y[:], Y[:], zt[:], ALU.add, reads=["d_Y", "d_zt"], writes=["d_Y"])
            for hh in range(32):
                g = hh // 8
                j = hh % 4
                if hh % 16 == 0:
                    P.I("dve", "tensor_tensor", R16[:], DTA[:, hh:hh + 16].unsqueeze(2).to_broadcast([128, 16, 128]),
                        tri.unsqueeze(1).to_broadcast([128, 16, 128]), ALU.mult, reads=[M, S], writes=["d_R16"])
                P.I("pe", "matmul", bk[4 + j][:, 0:128], lhsT=STRICT, rhs=R16[:, hh % 16, :], start=True, stop=True, reads=[M, "d_R16"], writes=[f"PS{4 + j}"])
                P.I("act", "activation", DEC[j][:], bk[4 + j][:, 0:128], AF.Exp, reads=[f"PS{4 + j}"], writes=[f"d_DEC{j}"])
                P.I("dve", "scalar_tensor_tensor", MTb[j][:], DEC[j][:], dtv[:, hh:hh + 1], CBm[:, g, :], ALU.mult, ALU.mult,
                    reads=[f"d_DEC{j}", "d_DT", "d_CBm"], writes=[f"d_MT{j}"])
                P.I("pe", "matmul", bk[g][:, (hh % 8) * 64:(hh % 8 + 1) * 64], lhsT=MTb[j][:], rhs=Xb[:, hh * 64:(hh + 1) * 64], start=True, stop=True,
                    reads=[f"d_MT{j}", "d_Xb"], writes=[f"PS{g}"])
            P.I("dve", "tensor_tensor", Y[:], Y[:], ps4(0), ALU.add, reads=["d_Y", "PS0", "PS1", "PS2", "PS3"], writes=["d_Y"])
            P.Dm("sp", "dma_start", out=zt[:], in_=scr_z[i * 128:(i + 1) * 128, :], reads=["scr_z"], writes=["d_zt"])
            P.I("act", "activation", zt[:], zt[:], AF.Silu, reads=["d_zt"], writes=["d_zt"])
            P.I("dve", "tensor_tensor", Y[:], Y[:], zt[:], ALU.mult, reads=["d_Y", "d_zt"], writes=["d_Y"])
            for g in range(4):
                P.I("act", "activation", zt[:, g * 512:(g + 1) * 512], Y[:, g * 512:(g + 1) * 512], AF.Square, accum_out=ss[:, g:g + 1],
                    reads=["d_Y", "d_zt", "d_ss"], writes=["d_zt", "d_ss"])
            P.I("act", "activation", ss[:, 4:8], ss[:, 0:4], AF.Sqrt, bias=self.eps_t[:, 0:1], scale=1.0 / 512.0, reads=["d_ss", "eps"], writes=["d_ss"])
            P.I("dve", "reciprocal", ss[:, 4:8], ss[:, 4:8], reads=["d_ss"], writes=["d_ss"])
            for g in range(4):
                P.I("dve", "tensor_scalar", Y[:, g * 512:(g + 1) * 512], Y[:, g * 512:(g + 1) * 512], ss[:, 4 + g:5 + g], None, ALU.mult,
                    reads=["d_Y", "d_ss"], writes=["d_Y"])
            self.transpose_f32(Y, 16, YT, "d_Y", "d_YT", banks=(0, 1))
            self.final_proj_ln(i, YT, "d_YT", w_out, "dw_out", 16)

        P.push_scope()
        h0 = P.sb([128, 16, 128], F32, "d_h0")
        h0T = [P.sb([128, 2048], BF16, f"d_h0T{j}") for j in range(2)]
        Bm = P.sb([128, 512], BF16, "d_Bm")
        Ej = P.sb([128, 128], F32, "d_Ej")
        cdb = P.sb([128, NSEQ, 32], F32, "d_cdb")
        cdc = P.sb([128, NSEQ, 16], F32, "d_cdc")
        tile_front(TP, True)
        for j in range(NSEQ):
            P.I("dve", "tensor_scalar", Ej[:], ONES, mcol[:, j:j + 1], None, ALU.mult, reads=[M, "d_mcol"], writes=["d_Ej"])
            P.I("pe", "matmul", bk[4][:, j * 32:(j + 1) * 32], lhsT=Ej[:], rhs=DTA, start=True, stop=True, reads=["d_Ej", S], writes=["PS4"])
        P.I("act", "activation", cdb[:].rearrange("p j h -> p (j h)"), bk[4], AF.Exp, reads=["PS4"], writes=["d_cdb"])
        cdb4 = cdb[:].rearrange("p j (r two) -> p j r two", two=2)
        P.I("dve", "tensor_scalar", cdc[:], cdb4[:, :, :, 0], mcol[:, 16:17], None, ALU.mult, reads=["d_cdb", "d_mcol"], writes=["d_cdc"])
        P.I("dve", "scalar_tensor_tensor", cdc[:], cdb4[:, :, :, 1], mcol[:, 17:18], cdc[:], ALU.mult, ALU.add,
            reads=["d_cdb", "d_mcol", "d_cdc"], writes=["d_cdc"])
        P.I("dve", "memset", zt[:], 0.0, writes=["d_zt"])
        for j in range(NSEQ):
            hT_j = h0T[j % 2]
            hreg = f"d_h0T{j % 2}"
            P.Dm("sp", "dma_start", out=h0[:], in_=d["state_ssd"][j].rearrange("(r p) n -> p r n", p=128), writes=["d_h0"])
            self.transpose_f32(h0[:].rearrange("p r n -> p (r n)"), 16, hT_j[:, :].rearrange("p (r m) -> p r m", m=128), "d_h0", hreg, banks=(0, 1))
            for g in range(4):
                b = 2 + g % 2
                P.I("pe", "matmul", bk[b], lhsT=CTb[:, g, :], rhs=hT_j[:, g * 512:(g + 1) * 512], start=True, stop=True,
                    reads=["d_CTb", hreg], writes=[f"PS{b}"])
                P.I("dve", "scalar_tensor_tensor", zt[:, g * 512:(g + 1) * 512], bk[b], mcol[:, j:j + 1], zt[:, g * 512:(g + 1) * 512], ALU.mult, ALU.add,
                    reads=[f"PS{b}", "d_mcol", "d_zt"], writes=["d_zt"])
            P.I("pool", "tensor_scalar", Bm[:], Btb[:], mcol[:, j:j + 1], None, ALU.mult, reads=["d_Btb", "d_mcol"], writes=["d_Bm"])
            for r in range(16):
                b = 4 + r // 4
                P.I("pe", "matmul", bk[b][:, (r % 4) * 128:(r % 4 + 1) * 128], lhsT=Xw[:, r * 128:(r + 1) * 128], rhs=Bm[:, (r // 4) * 128:(r // 4 + 1) * 128],
                    start=True, stop=True, reads=["d_Xw", "d_Bm"], writes=[f"PS{b}"])
            P.I("dve", "tensor_tensor", h0[:], h0[:], cdc[:, j, :].unsqueeze(2).to_broadcast([128, 16, 128]), ALU.mult,
                reads=["d_h0", "d_cdc"], writes=["d_h0"])
            P.I("dve", "tensor_tensor", h0[:].rearrange("p r n -> p (r n)"), h0[:].rearrange("p r n -> p (r n)"), ps4(4), ALU.add,
                reads=["d_h0", "PS4", "PS5", "PS6", "PS7"], writes=["d_h0"])
            P.Dm("sp", "dma_start", out=self.dout["ssd_s"][j * 2048:(j + 1) * 2048, :].rearrange("(r p) n -> p r n", p=128), in_=h0[:], reads=["d_h0"])
        P.I("dve", "tensor_copy", Y[:], zt[:], reads=["d_zt"], writes=["d_Y"])
        P.I("pool", "tensor_copy", h0[:].rearrange("p r n -> p (r n)"), Y[:], reads=["d_Y"], writes=["d_h0"])
        tile_back(TP, True, h0[:].rearrange("p r n -> p (r n)"), ["d_h0"])
        P.pop_scope()
        P.push_scope()
        hT = P.sb([128, 2048], F32, "d_hT")
        hTb = P.sb([128, 2048], BF16, "d_hTb")
        P.I("dve", "memset", hT[:], 0.0, writes=["d_hT"])
        P.I("dve", "memset", hTb[:], 0.0, writes=["d_hTb"])
        for i in range(TP):
            tile_front(i, False)
            for g in range(4):
                P.I("pe", "matmul", bk[g], lhsT=CTb[:, g, :], rhs=hTb[:, g * 512:(g + 1) * 512], start=True, stop=True,
                    reads=["d_CTb", "d_hTb"], writes=[f"PS{g}"])
            for g in range(4):
                P.I("pe", "matmul", bk[4 + g], lhsT=Btb[:, g * 128:(g + 1) * 128], rhs=Xw[:, g * 512:(g + 1) * 512], start=True, stop=True,
                    reads=["d_Btb", "d_Xw"], writes=[f"PS{4 + g}"])
            P.I("dve", "tensor_tensor", v3(hT), v3(hT), bc(CDP), ALU.mult, reads=["d_hT", S], writes=["d_hT"])
            P.I("dve", "tensor_tensor", hT[:], hT[:], ps4(4), ALU.add, reads=["d_hT", "PS4", "PS5", "PS6", "PS7"], writes=["d_hT"])
            tile_back(i, False, ps4(0), ["PS0", "PS1", "PS2", "PS3"])
            P.I("act", "copy", hTb[:], hT[:], reads=["d_hT"], writes=["d_hTb"])
        self.transpose_f32(hT, 16, Y[:, :].rearrange("p (r n) -> p r n", n=128), "d_hT", "d_Y", banks=(0, 1))
        P.Dm("sp", "dma_start", out=self.dout["ssd_p"].rearrange("(r p) n -> p r n", p=128), in_=Y[:, :].rearrange("p (r n) -> p r n", n=128),
             reads=["d_Y"])
        P.pop_scope()
        P.pop_scope()
        P.pop_scope()

    def build(self):
        P = self.P
        self.setup()
        self.eps_t = P.sb([128, 1], F32, "eps")
        P.I("dve", "memset", self.eps_t[:], LN_EPS, writes=["eps"])
        self.one_t = P.sb([128, 1], F32, "one")
        P.I("dve", "memset", self.one_t[:], 1.0, writes=["one"])
        if "ffn" in self.dbg:
            self.outp("dbg_ffn", [128, D])
            self.outp("dbg_idx", [128, 128], I32)
            self.outp("dbg_act", [128, 128]); self.outp("dbg_wgt", [128, 128]); self.outp("dbg_gate", [128, 128])
        self.load_ln_dummy = None
        if self.do_peer:
            stg0 = self.stg0 = [P.sb([128, D], F32, "cv0_s")]
            cb0 = self.cb0 = [P.sb([128, D], BF16, "cv0_b")]
            P.defer_begin("cv0")
            self.convert_tables(self.layers[0], stg0, cb0)
            P.defer_end()
        for l in self.layers:
            if self.do_mix:
                getattr(self, ("s5", "pool", "cmlp", "ssd")[l])(l)
            if self.do_peer:
                self.peer(l)
        self.finish()
        return self.nc


def make_in_map(inp, c, names, TP=16):
    f = np.ascontiguousarray
    sl = slice(c * NSEQ, (c + 1) * NSEQ)
    m = {}
    for n in names:
        if n == "x_p":
            m[n] = f(inp["x_prompt"][c, :TP * 128])
        elif n == "x_s":
            m[n] = f(inp["x_sample"][sl].reshape(128, D))
        elif n == "peer_w_q":
            m[n] = f(inp["peer_w_q"])
        elif n == "peer_keysT":
            m[n] = f(inp["peer_keys"].transpose(0, 4, 1, 2, 3).reshape(4, 128, 2048))
        elif n in ("peer_u", "peer_v"):
            m[n] = f(inp[n].reshape(4 * NEXP, D))
        elif n == "s5_aT":
            def lay(a):
                return a.reshape(32, 2, 64).transpose(1, 2, 0).reshape(128, 32)
            ld = np.broadcast_to(inp["s5_log_dt"][:, None], (64, 64))
            m[n] = f(np.stack([lay(inp["s5_a_re"]), lay(inp["s5_a_im"]), lay(ld)], axis=1))
        elif n == "s5_bT":
            def layb(b):
                return b.reshape(32, 2, 64, 16).transpose(1, 2, 0, 3).reshape(128, 32, 16)
            m[n] = f(np.stack([layb(inp["s5_b_re"]), layb(inp["s5_b_im"])], axis=1))
        elif n == "s5_cT":
            def layc(cc):
                return cc.reshape(32, 2, 16, 64).transpose(1, 3, 0, 2).reshape(128, 32, 16)
            m[n] = f(np.stack([layc(inp["s5_c_re"]), layc(inp["s5_c_im"])], axis=1))
        elif n == "s5_dT":
            m[n] = f(inp["s5_d"].reshape(8, 128).T)
        elif n == "s5_b_gluT":
            m[n] = f(inp["s5_b_glu"].reshape(8, 128).T)
        elif n in ("state_s5_re", "state_s5_im"):
            m[n] = f(inp[n][sl].reshape(NSEQ, 4096))
        elif n == "ssd_hd":
            m[n] = f(np.concatenate([inp["ssd_dt_bias"], inp["ssd_a_log"], inp["ssd_d"]]).reshape(1, 96))
        elif n == "ssd_conv_wT":
            m[n] = f(inp["ssd_conv_w"].reshape(4, 24, 128).transpose(2, 1, 0))
        elif n == "ssd_conv_bT":
            m[n] = f(inp["ssd_conv_b"].reshape(24, 128).T)
        elif n == "ssd_norm_gT":
            m[n] = f(inp["ssd_norm_g"].reshape(16, 128).T)
        elif n == "state_ssd_conv":
            m[n] = f(inp[n][sl].reshape(NSEQ * 3, 3072))
        elif n == "state_ssd":
            m[n] = f(inp[n][sl].reshape(NSEQ, 2048, 128))
        elif n == "pool_scaleT":
            m[n] = f(inp["pool_scale"].reshape(8, 128).T)
        elif n == "state_pool":
            m[n] = f(inp["state_pool"][sl].reshape(NSEQ * 15, D))
        elif n in ("cmlp_b_in", "cmlp_ln_g", "cmlp_ln_b"):
            m[n] = f(inp[n].reshape(1, -1))
        elif n == "cmlp_w_sT":
            m[n] = f(inp["cmlp_w_s"].transpose(0, 2, 1))
        elif n == "cmlp_b_sT":
            m[n] = f(inp["cmlp_b_s"].T)
        else:
            m[n] = f(inp[n])
    return m


_CACHE = {}


def kernel(**inputs):
    inp = {k: np.asarray(v) for k, v in inputs.items()}
    ncores = 8
    if "k" not in _CACHE:
        k = K(TP=16, layers=(0, 1, 2, 3))
        k.build()
        _CACHE["k"] = k
    k = _CACHE["k"]
    names = list(k.din)
    in_maps = [make_in_map(inp, c, names, TP=16) for c in range(ncores)]
    res = run_bass_kernel_spmd(k.nc, in_maps, core_ids=list(range(ncores)))
    R = res.results

    def cat(name, shp):
        return np.ascontiguousarray(np.stack([np.asarray(R[c][name]).reshape(shp) for c in range(ncores)]))
    y_prompt = cat("y_p", (2048, D))
    y_sample = cat("y_s", (NSEQ, DSEQ, D)).reshape(128, DSEQ, D)
    s5_re_p = cat("s5_re_p", (64, 64))
    s5_im_p = cat("s5_im_p", (64, 64))
    pool_p = cat("pool_p", (15, D))
    conv_p = cat("conv_p", (3, 3072))
    ssd_p = cat("ssd_p", (32, 64, 128))
    s5_re_s = cat("s5_re_s", (NSEQ, 64, 64)).reshape(128, 64, 64)
    s5_im_s = cat("s5_im_s", (NSEQ, 64, 64)).reshape(128, 64, 64)
    pool_s = cat("pool_s", (NSEQ, 15, D)).reshape(128, 15, D)
    cmlp_v_s = cat("cmlp_v_s", (NSEQ, DSEQ, D)).reshape(128, DSEQ, D)
    conv_s = cat("conv_s", (NSEQ, 3, 3072)).reshape(128, 3, 3072)
    ssd_s = cat("ssd_s", (NSEQ, 32, 64, 128)).reshape(128, 32, 64, 128)
    outs = (y_prompt, y_sample, s5_re_p, s5_im_p, pool_p, conv_p, ssd_p,
            s5_re_s, s5_im_s, pool_s, cmlp_v_s, conv_s, ssd_s)
    return tuple(np.asarray(o, dtype=np.float32) for o in outs)
```

```python
import numpy as np
from contextlib import ExitStack
import concourse.bass as bass
import concourse.mybir as mybir
from concourse.bass_utils import run_bass_kernel_spmd

F32 = mybir.dt.float32
BF16 = mybir.dt.bfloat16
I32 = mybir.dt.int32
U32 = mybir.dt.uint32
ALU = mybir.AluOpType
AF = mybir.ActivationFunctionType
AX = mybir.AxisListType

COMPUTE = ("pe", "dve", "act", "pool")
NDSEM = {"sp": 12, "pool": 12, "act": 4}
SAME_ENGINE_SYNC = {"pe": False, "dve": True, "act": True, "pool": True, "sp": True}

D = 1024
ALPHA = 8.0 ** 0.25
LN_EPS = 1e-5
NSEQ = 16
DSEQ = 8
NEXP = 16384


class Op:
    __slots__ = ("eng", "fn", "waits", "kind", "dsem", "dval", "cidx")


class Prog:
    def __init__(self, nc):
        self.nc = nc
        self.es = ExitStack()
        self.ops = {e: [] for e in ("pe", "dve", "act", "pool", "sp")}
        self.ncomp = {e: 0 for e in COMPUTE}
        self.last_w = {}
        self.readers = {}
        self.dcount = {}
        self.dlast = {}
        self.drr = {e: 0 for e in NDSEM}
        self.nbuf = 0
        self.bar = []
        self.scopes = []

    def sb(self, shape, dt=F32, name=None):
        self.nbuf += 1
        name = f"{name or 'sb'}_{self.nbuf}"
        es = self.scopes[-1] if self.scopes else self.es
        return es.enter_context(self.nc.sbuf_tensor(name, list(shape), dt))

    def ps(self, shape, dt=F32, name=None):
        self.nbuf += 1
        name = name or f"ps{self.nbuf}"
        return self.es.enter_context(self.nc.psum_tensor(name, list(shape), dt))

    def push_scope(self):
        self.scopes.append(ExitStack())

    def pop_scope(self):
        self.barrier()
        self.scopes.pop().close()

    def barrier(self):
        bar = []
        for e in COMPUTE:
            if self.ncomp[e]:
                bar.append(("c", e, self.ncomp[e] - 1))
        bar.extend(self.dlast.values())
        self.bar = bar

    def _deps(self, reads, writes):
        deps = list(self.bar)
        for r in reads:
            w = self.last_w.get(r)
            if w is not None:
                deps.append(w)
        for w_ in writes:
            w = self.last_w.get(w_)
            if w is not None:
                deps.append(w)
            deps.extend(self.readers.get(w_, ()))
        return deps

    def _commit(self, token, reads, writes):
        for w_ in writes:
            self.last_w[w_] = token
            self.readers[w_] = []
        for r in reads:
            if r not in writes:
                self.readers.setdefault(r, []).append(token)

    @staticmethod
    def _excl(reads, writes):
        ex = [r for r in reads if isinstance(r, str) and r.startswith("PS")]
        if ex:
            reads = [r for r in reads if r not in ex]
            writes = list(writes) + [r for r in ex if r not in writes]
        return reads, writes

    def defer_begin(self, name="f"):
        if not hasattr(self, "dq"):
            self.dq = {}
        self.dq[name] = []
        self.deferring = name

    def defer_end(self):
        n = len(self.dq[self.deferring])
        self.deferring = None
        return n

    def flush(self, n, name="f"):
        q = getattr(self, "dq", {}).get(name)
        if not q:
            return
        k = len(q) if n is None else min(n, len(q))
        for _ in range(k):
            kind, a = q.pop(0)
            (self.op if kind == "c" else self.dma)(*a, _now=True)

    def op(self, eng, fn, reads=(), writes=(), _now=False):
        if getattr(self, "deferring", None) and not _now:
            self.dq[self.deferring].append(("c", (eng, fn, reads, writes)))
            return None
        reads, writes = self._excl(list(reads), list(writes))
        o = Op()
        o.eng, o.fn, o.kind = eng, fn, "c"
        o.waits = self._deps(reads, writes)
        o.cidx = self.ncomp[eng]
        self.ncomp[eng] += 1
        self.ops[eng].append(o)
        self._commit(("c", eng, o.cidx), reads, writes)
        return o

    def I(self, eng, meth, *args, reads=(), writes=(), **kw):
        return self.op(eng, lambda h: getattr(h, meth)(*args, **kw), reads=reads, writes=writes)

    def Dm(self, q, meth, *args, reads=(), writes=(), **kw):
        return self.dma(q, lambda h: getattr(h, meth)(*args, **kw), reads=reads, writes=writes)

    def dma(self, q, fn, reads=(), writes=(), _now=False):
        if getattr(self, "deferring", None) and not _now:
            self.dq[self.deferring].append(("d", (q, fn, reads, writes)))
            return None
        o = Op()
        o.eng, o.fn, o.kind = q, fn, "d"
        deps = self._deps(reads, writes)
        k = self.drr[q]
        self.drr[q] = (k + 1) % NDSEM[q]
        key = (q, k)
        prev = self.dlast.get(key)
        if prev is not None:
            deps.append(prev)
        cnt = self.dcount.get(key, 0) + 1
        self.dcount[key] = cnt
        o.dsem, o.dval = key, 16 * cnt
        tok = ("d", key, o.dval)
        self.dlast[key] = tok
        o.waits = deps
        self.ops[q].append(o)
        self._commit(tok, reads, writes)
        return o

    def emit(self):
        nc = self.nc
        es = self.es
        csem = {e: es.enter_context(nc.semaphore(f"c_{e}")) for e in COMPUTE}
        dsem = {}
        for q, n in NDSEM.items():
            for k in range(n):
                dsem[(q, k)] = es.enter_context(nc.semaphore(f"d_{q}{k}"))
        ops = self.ops
        final = dict(self.dcount)

        def run(ename, h):
            waited = {}
            for o in ops[ename]:
                need = {}
                for d in o.waits:
                    if d[0] == "c":
                        _, e2, idx = d
                        if e2 == ename and not SAME_ENGINE_SYNC[ename]:
                            continue
                        key, val = ("c", e2), idx + 1
                    else:
                        _, dk, val = d
                        key = ("d", dk)
                    if need.get(key, 0) < val:
                        need[key] = val
                for key, val in need.items():
                    if waited.get(key, 0) >= val:
                        continue
                    waited[key] = val
                    s = csem[key[1]] if key[0] == "c" else dsem[key[1]]
                    h.wait_ge(s, val)
                inst = o.fn(h)
                if o.kind == "c":
                    inst.then_inc(csem[ename], 1)
                else:
                    inst.then_inc(dsem[o.dsem], 16)
            if ename == "sp":
                for key, cnt in final.items():
                    h.wait_ge(dsem[key], 16 * cnt)

        with nc.Block() as block:
            @block.sync
            def _(h):
                run("sp", h)

            @block.tensor
            def _(h):
                run("pe", h)

            @block.vector
            def _(h):
                run("dve", h)

            @block.scalar
            def _(h):
                run("act", h)

            @block.gpsimd
            def _(h):
                run("pool", h)
        es.close()


class K:
    def __init__(self, TP=16, layers=(0, 1, 2, 3), mixers=True, peer=True, dbg=()):
        self.TP = TP
        self.NT = TP + 1
        self.layers = layers
        self.do_mix = mixers
        self.do_peer = peer
        self.dbg = set(dbg)
        nc = bass.Bass("TRN2", target_bir_lowering=False)
        self.nc = nc
        self.P = Prog(nc)
        self.din = {}
        self.dout = {}
        self.evk = 0
        self.NG = 11

    def inp(self, name, shape, dt=F32):
        t = self.nc.dram_tensor(name, list(shape), dt, kind="ExternalInput").ap()
        self.din[name] = t
        return t

    def outp(self, name, shape, dt=F32):
        t = self.nc.dram_tensor(name, list(shape), dt, kind="ExternalOutput").ap()
        self.dout[name] = t
        return t

    def ev(self, out, in_, reads, writes, eng=None):
        if eng is None:
            eng = ("act", "dve")[self.evk % 2]
            self.evk += 1
        if eng == "act":
            self.P.I("act", "copy", out, in_, reads=reads, writes=writes)
        elif eng == "dve":
            self.P.I("dve", "tensor_copy", out, in_, reads=reads, writes=writes)
        else:
            self.P.I("pool", "tensor_copy", out, in_, reads=reads, writes=writes)

    def setup(self):
        P = self.P
        TP, NT = self.TP, self.NT
        L = TP * 128
        self.x_p = self.inp("x_p", [L, D])
        self.x_s = self.inp("x_s", [128, D])
        self.y_p = self.outp("y_p", [L, D])
        self.y_s = self.outp("y_s", [128, D])
        self.ln1_g = self.inp("ln1_g", [4, D]); self.ln1_b = self.inp("ln1_b", [4, D])
        self.ln2_g = self.inp("ln2_g", [4, D]); self.ln2_b = self.inp("ln2_b", [4, D])
        if self.do_peer:
            self.d_wq = self.inp("peer_w_q", [4, D, 2048])
            self.d_keysT = self.inp("peer_keysT", [4, 128, 2048])
            self.d_u = self.inp("peer_u", [4 * NEXP, D])
            self.d_v = self.inp("peer_v", [4 * NEXP, D])
            self.uvb = self.nc.dram_tensor("uvb", [4 * NEXP, 2048], BF16, kind="Internal").ap()
        if self.do_mix:
            if 0 in self.layers:
                for n in ("s5_w_in", "s5_w_glu", "s5_w_out"):
                    self.inp(n, [D, D])
                self.inp("s5_aT", [128, 3, 32]); self.inp("s5_bT", [128, 2, 32, 16]); self.inp("s5_cT", [128, 2, 32, 16])
                self.inp("s5_dT", [128, 8]); self.inp("s5_b_gluT", [128, 8])
                self.inp("state_s5_re", [NSEQ, 4096]); self.inp("state_s5_im", [NSEQ, 4096])
                self.outp("s5_re_p", [32, 128]); self.outp("s5_im_p", [32, 128])
                self.outp("s5_re_s", [NSEQ, 4096]); self.outp("s5_im_s", [NSEQ, 4096])
            if 3 in self.layers:
                self.inp("ssd_w_in", [D, 5152]); self.inp("ssd_w_out", [2048, D]); self.inp("ssd_hd", [1, 96])
                self.inp("ssd_conv_wT", [128, 24, 4]); self.inp("ssd_conv_bT", [128, 24]); self.inp("ssd_norm_gT", [128, 16])
                self.inp("state_ssd_conv", [NSEQ * 3, 3072]); self.inp("state_ssd", [NSEQ, 2048, 128])
                self.outp("conv_p", [3, 3072]); self.outp("ssd_p", [2048, 128])
                self.outp("conv_s", [NSEQ * 3, 3072]); self.outp("ssd_s", [NSEQ * 2048, 128])
            if 1 in self.layers:
                self.inp("pool_w_in", [D, D]); self.inp("pool_w_out", [D, D]); self.inp("pool_w_grp", [4, 256, 256])
                self.inp("pool_scaleT", [128, 8]); self.inp("state_pool", [NSEQ * 15, D])
                self.outp("pool_p", [15, D]); self.outp("pool_s", [NSEQ * 15, D])
            if 2 in self.layers:
                self.inp("cmlp_w_in", [D, 2048]); self.inp("cmlp_w_out", [D, D]); self.inp("cmlp_b_in", [1, 2048])
                self.inp("cmlp_ln_g", [1, D]); self.inp("cmlp_ln_b", [1, D])
                self.inp("cmlp_w_sT", [4, 128, 128]); self.inp("cmlp_b_sT", [128, 4])
                self.outp("cmlp_v_s", [128, D])
        self.H = P.sb([128, NT, D], F32, "H")
        self.ident = P.sb([128, 128], F32, "ident")
        self.identb = P.sb([128, 128], BF16, "identb")
        self.iota16 = P.sb([128, 16], F32, "iota16")
        self.lng = P.sb([128, D], F32, "lng")
        self.lnb = P.sb([128, D], F32, "lnb")
        self.tmp = P.sb([128, D], F32, "tmp")
        self.small = P.sb([128, 32], F32, "small")
        self.psum = P.ps([128, 8, 512], F32, "psum")
        self.bank = [self.psum[:, i, :] for i in range(8)]
        iot = P.sb([128, 128], F32, "iot")
        P.I("pool", "iota", iot[:], pattern=[[1, 128]], base=0, channel_multiplier=-1,
                                      allow_small_or_imprecise_dtypes=True, writes=["iot"])
        P.I("dve", "tensor_single_scalar", self.ident[:], iot[:], 0.0, ALU.is_equal,
             reads=["iot"], writes=["ident"])
        P.I("dve", "tensor_copy", self.identb[:], self.ident[:], reads=["ident"], writes=["identb"])
        P.I("pool", "iota", self.iota16[:], pattern=[[1, 16]], base=0, channel_multiplier=0,
                                      allow_small_or_imprecise_dtypes=True, writes=["iota16"])
        self.iot = iot
        for i in range(TP):
            P.Dm("sp", "dma_start", out=self.H[:, i, :], in_=self.x_p[i * 128:(i + 1) * 128, :],
                  writes=[("H", i)])
        P.Dm("sp", "dma_start", out=self.H[:, TP, :], in_=self.x_s, writes=[("H", TP)])

    def finish(self):
        P = self.P
        TP = self.TP
        for i in range(TP):
            P.Dm("sp", "dma_start", out=self.y_p[i * 128:(i + 1) * 128, :], in_=self.H[:, i, :],
                  reads=[("H", i)])
        P.Dm("sp", "dma_start", out=self.y_s, in_=self.H[:, TP, :], reads=[("H", TP)])
        P.emit()

    def transpose_f32(self, src, nch, dst, src_reg, dst_reg, banks=(0, 1)):
        P = self.P
        for g in range(0, nch, 4):
            b = banks[(g // 4) % len(banks)]
            n = min(4, nch - g)
            for c in range(n):
                P.I("pe", "transpose", self.bank[b][:, c * 128:(c + 1) * 128],
                                                                src[:, (g + c) * 128:(g + c + 1) * 128], self.ident[:],
                     reads=[src_reg, "ident"], writes=[f"PS{b}"])
            self.ev(dst[:, g:g + n, :], self.bank[b][:, 0:n * 128].rearrange("p (c n) -> p c n", c=n),
                    reads=[f"PS{b}"], writes=[dst_reg])

    def load_ln(self, g_ap, b_ap, l):
        P = self.P
        P.Dm("sp", "dma_start", out=self.lng[:], in_=g_ap[l:l + 1, :].to_broadcast([128, D]), writes=["lng"])
        P.Dm("sp", "dma_start", out=self.lnb[:], in_=b_ap[l:l + 1, :].to_broadcast([128, D]), writes=["lnb"])

    def resid_ln(self, i, mix_ap, mix_regs):
        P = self.P
        Hi = self.H[:, i, :]
        tmp, sm = self.tmp, self.small
        P.I("dve", "scalar_tensor_tensor", tmp[:], Hi, ALPHA, mix_ap, ALU.mult, ALU.add,
             reads=[("H", i)] + list(mix_regs), writes=["tmp"])
        self.ln_inplace(tmp, "tmp", Hi, ("H", i), self.lng, self.lnb)

    def ln_inplace(self, src, src_reg, dst_ap, dst_reg, g_t, b_t, n=D):
        P = self.P
        sm = self.small
        nchk = n // 512
        for j in range(nchk):
            P.I("dve", "bn_stats", sm[:, j * 6:(j + 1) * 6], src[:, j * 512:(j + 1) * 512],
                 reads=[src_reg], writes=["small"])
        P.I("dve", "bn_aggr", sm[:, 12:14], sm[:, 0:6 * nchk], reads=["small"], writes=["small"])
        P.I("act", "activation", sm[:, 14:15], sm[:, 13:14], AF.Sqrt, bias=self.eps_t[:, 0:1],
             reads=["small", "eps"], writes=["small"])
        P.I("dve", "reciprocal", sm[:, 15:16], sm[:, 14:15], reads=["small"], writes=["small"])
        P.I("dve", "tensor_scalar", src[:, 0:n], src[:, 0:n], sm[:, 12:13], sm[:, 15:16], ALU.subtract, ALU.mult,
             reads=["small", src_reg], writes=[src_reg])
        P.I("dve", "tensor_tensor", src[:, 0:n], src[:, 0:n], g_t[:, 0:n], ALU.mult,
             reads=[src_reg, "lng"], writes=[src_reg])
        P.I("dve", "tensor_tensor", dst_ap, src[:, 0:n], b_t[:, 0:n], ALU.add,
             reads=[src_reg, "lnb"], writes=[dst_reg])

    def convert_tables(self, l, stg, cb):
        P = self.P
        n = len(stg)
        for ch in range(128):
            r0 = l * NEXP + ch * 128
            for hf, src in enumerate((self.d_u, self.d_v)):
                k = (ch * 2 + hf) % n
                P.Dm("sp", "dma_start", out=stg[k][:], in_=src[r0:r0 + 128, :], writes=[f"cv_s{k}"])
                P.I("act", "copy", cb[k][:], stg[k][:], reads=[f"cv_s{k}"], writes=[f"cv_b{k}"])
                P.Dm("sp", "dma_start", out=self.uvb[r0:r0 + 128, hf * 1024:(hf + 1) * 1024], in_=cb[k][:],
                     reads=[f"cv_b{k}"], writes=[("uvb", l)])

    def peer(self, l):
        P = self.P
        NT = self.NT
        P.flush(None, "cv0")
        P.push_scope()
        wq = P.sb([128, 8, 2048], BF16, "wq")
        keysT = P.sb([128, 16, 128], BF16, "keysT")
        hT = P.sb([128, 8, 128], BF16, "hT")
        qT = P.sb([128, 16, 128], BF16, "qT")
        sbig = P.sb([128, 16, 128], F32, "sbig")
        oh = sbig[:].rearrange("p c n -> p (c n)").rearrange("p (h a b) -> p h a b", h=8, a=16)
        QALL = [("sbig", g) for g in range(4)]
        tv = P.sb([128, 16, 16], F32, "tv")
        ti = P.sb([128, 16, 16], U32, "ti")
        tif = P.sb([128, 16, 16], F32, "tif")
        wk2 = [P.sb([128, 256], F32, f"wk{j}") for j in range(2)]
        bs = P.sb([128, 8, 16], F32, "bs")
        bj = P.sb([128, 8, 16], U32, "bj")
        ja = P.sb([128, 8, 16], U32, "ja")
        jaf = P.sb([128, 8, 16], F32, "jaf")
        jbf = P.sb([128, 8, 16], F32, "jbf")
        i0 = P.sb([128, 8, 16], F32, "i0")
        i1 = P.sb([128, 8, 16], F32, "i1")
        idx2 = [P.sb([128, 128], I32, f"idx{j}") for j in range(2)]
        gate2 = [P.sb([128, 8, 16], F32, f"gate{j}") for j in range(2)]
        gsum = P.sb([128, 8], F32, "gsum")
        act = P.sb([128, 128], F32, "actv")
        wgt = P.sb([128, 128], F32, "wgt")
        NG = self.NG
        gb = [P.sb([128, 2048], BF16, f"gb{j}") for j in range(NG)]
        junk2 = [P.sb([128, D], BF16, f"junk{j}") for j in range(2)]
        tmpv = [P.sb([128, D], BF16, f"tmpv{j}") for j in range(2)]
        cstg = [self.stg0[0]]
        ccb = [self.cb0[0]]
        bk = self.bank
        for k in range(8):
            P.Dm("pool", "dma_start", out=wq[:, k, :], in_=self.d_wq[l, k * 128:(k + 1) * 128, :], writes=["wq"])
        P.Dm("pool", "dma_start", out=keysT[:].rearrange("p c n -> p (c n)"), in_=self.d_keysT[l], writes=["keysT"])
        self.load_ln(self.ln2_g, self.ln2_b, l)
        tv4 = tv[:].rearrange("p (h i) k -> p h i k", i=2)
        tif4 = tif[:].rearrange("p (h i) k -> p h i k", i=2)
        cand = sbig[:].rearrange("p c n -> p (c n)").rearrange("p (h a b) -> p h a b", h=8, a=16)
        cand3 = sbig[:].rearrange("p c n -> p (c n)").rearrange("p (h x) -> p h x", h=8)
        sball = [("sbig", g) for g in range(4)]
        io4 = self.iota16[:].unsqueeze(1).unsqueeze(1).to_broadcast([128, 8, 16, 16])

        def front(i):
            sfx = i % 2
            idx, gate = idx2[sfx], gate2[sfx]
            ireg, greg = f"idx{sfx}", f"gate{sfx}"
            Hi = self.H[:, i, :]
            self.transpose_f32(Hi, 8, hT, ("H", i), "hT", banks=(6, 7))
            for c in range(16):
                b = 4 + (c // 4) % 2
                for k in range(8):
                    P.I("pe", "matmul", bk[b][:, (c % 4) * 128:(c % 4 + 1) * 128], lhsT=wq[:, k, c * 128:(c + 1) * 128],
                        rhs=hT[:, k, :], start=(k == 0), stop=(k == 7), reads=["wq", "hT"], writes=[f"PS{b}"])
                if c % 4 == 3:
                    self.ev(qT[:, c - 3:c + 1, :], bk[b][:, :].rearrange("p (c n) -> p c n", c=4),
                            reads=[f"PS{b}"], writes=[("qT", c // 4)], eng="act")
            for c in range(16):
                b = 6 + (c // 4) % 2
                P.I("pe", "matmul", bk[b][:, (c % 4) * 128:(c % 4 + 1) * 128], lhsT=qT[:, c, :], rhs=keysT[:, c, :],
                    start=True, stop=True, reads=[("qT", c // 4), "keysT"], writes=[f"PS{b}"])
                if c % 4 == 3:
                    self.ev(sbig[:, c - 3:c + 1, :], bk[b][:, :].rearrange("p (c n) -> p c n", c=4),
                            reads=[f"PS{b}"], writes=[("sbig", c // 4)], eng="act")
            TVA = [("tv", c) for c in range(16)]
            TIA = [("ti", c) for c in range(16)]
            for c in range(16):
                sr = ("sbig", c // 4)
                w_, wr_ = wk2[c % 2], f"wk{c % 2}"
                P.I("dve", "max", tv[:, c, 0:8], sbig[:, c, :], reads=[sr], writes=[("tv", c)])
                P.I("dve", "max_index", ti[:, c, 0:8], tv[:, c, 0:8], sbig[:, c, :], reads=[sr, ("tv", c)], writes=[("ti", c)])
                P.I("dve", "match_replace", w_[:, 0:128], tv[:, c, 0:8], sbig[:, c, :], -1e30, reads=[sr, ("tv", c)], writes=[wr_])
                P.I("dve", "max", tv[:, c, 8:16], w_[:, 0:128], reads=[wr_], writes=[("tv", c)])
                P.I("dve", "max_index", ti[:, c, 8:16], tv[:, c, 8:16], w_[:, 0:128], reads=[wr_, ("tv", c)], writes=[("ti", c)])
            P.I("dve", "tensor_copy", tif[:], ti[:], reads=TIA, writes=["tif"])
            P.I("dve", "tensor_tensor", cand, tv4[:, :, 0, :].unsqueeze(3).to_broadcast([128, 8, 16, 16]),
                tv4[:, :, 1, :].unsqueeze(2).to_broadcast([128, 8, 16, 16]), ALU.add, reads=TVA + sball, writes=sball)
            BSA = [("bs", hh) for hh in range(8)]
            BJA = [("bj", hh) for hh in range(8)]
            for hh in range(8):
                w_, wr_ = wk2[hh % 2], f"wk{hh % 2}"
                P.I("dve", "max", bs[:, hh, 0:8], cand3[:, hh, :], reads=sball, writes=[("bs", hh)])
                P.I("dve", "max_index", bj[:, hh, 0:8], bs[:, hh, 0:8], cand3[:, hh, :], reads=sball + [("bs", hh)], writes=[("bj", hh)])
                P.I("dve", "match_replace", w_[:], bs[:, hh, 0:8], cand3[:, hh, :], -1e30, reads=sball + [("bs", hh)], writes=[wr_])
                P.I("dve", "max", bs[:, hh, 8:16], w_[:], reads=[wr_], writes=[("bs", hh)])
                P.I("dve", "max_index", bj[:, hh, 8:16], bs[:, hh, 8:16], w_[:], reads=[wr_, ("bs", hh)], writes=[("bj", hh)])
            P.I("dve", "tensor_single_scalar", ja[:], bj[:], 4, ALU.logical_shift_right, reads=BJA, writes=["ja"])
            P.I("dve", "tensor_copy", jaf[:], ja[:], reads=["ja"], writes=["jaf"])
            P.I("dve", "tensor_single_scalar", ja[:], bj[:], 15, ALU.bitwise_and, reads=BJA + ["jaf"], writes=["ja"])
            P.I("dve", "tensor_copy", jbf[:], ja[:], reads=["ja"], writes=["jbf"])
            for (jf, half, dst, nm) in ((jaf, 0, i0, "i0"), (jbf, 1, i1, "i1")):
                P.I("dve", "tensor_tensor", oh, io4, jf[:].unsqueeze(3).to_broadcast([128, 8, 16, 16]), ALU.is_equal,
                    reads=["iota16", "jaf", "jbf"], writes=QALL)
                P.I("dve", "tensor_tensor", oh, oh, tif4[:, :, half, :].unsqueeze(2).to_broadcast([128, 8, 16, 16]), ALU.mult,
                    reads=QALL + ["tif"], writes=QALL)
                P.I("dve", "tensor_reduce", dst[:], oh, AX.X, ALU.add, reads=QALL, writes=[nm])
            P.I("dve", "scalar_tensor_tensor", i0[:], i0[:], 128.0, i1[:], ALU.mult, ALU.add, reads=["i0", "i1"], writes=["i0"])
            P.I("dve", "tensor_scalar", idx[:], i0[:].rearrange("p h k -> p (h k)"), float(l * NEXP), None, ALU.add,
                reads=["i0"], writes=[ireg])
            P.I("dve", "tensor_tensor", gate[:], bs[:], bs[:, :, 0:1].to_broadcast([128, 8, 16]), ALU.subtract, reads=BSA, writes=[greg])
            P.I("act", "activation", gate[:], gate[:], AF.Exp, reads=[greg], writes=[greg])
            P.I("dve", "tensor_reduce", gsum[:], gate[:], AX.X, ALU.add, reads=[greg], writes=["gsum"])
            P.I("dve", "reciprocal", gsum[:], gsum[:], reads=["gsum"], writes=["gsum"])
            P.I("dve", "tensor_tensor", gate[:], gate[:], gsum[:].unsqueeze(2).to_broadcast([128, 8, 16]), ALU.mult,
                reads=[greg, "gsum"], writes=[greg])

        def back(i, nflush, ncv):
            sfx = i % 2
            idx, gate = idx2[sfx], gate2[sfx]
            ireg, greg = f"idx{sfx}", f"gate{sfx}"
            Hi = self.H[:, i, :]
            gflat = gate[:].rearrange("p h k -> p (h k)")
            GS = 2
            for gi in range(128 // GS):
                js = list(range(gi * GS, (gi + 1) * GS))
                wr = ("wgt", gi % 8)
                ars = [("actv", j % 16) for j in js]
                for j in js:
                    g, gr = gb[j % NG], f"gb{j % NG}"
                    P.Dm("pool", "indirect_dma_start", out=g[:], out_offset=None, in_=self.uvb,
                         in_offset=bass.IndirectOffsetOnAxis(ap=idx[:, j:j + 1], axis=0), reads=[ireg, ("uvb", l)], writes=[gr])
                    P.I("dve", "scalar_tensor_tensor", junk2[j % 2][:], g[:, 0:1024], 1.0, Hi, ALU.mult, ALU.mult, accum_out=act[:, j:j + 1],
                        reads=[gr, ("H", i)], writes=[f"junk{j % 2}", ("actv", j % 16)])
                    P.flush(nflush)
                    if j % 3 == 2:
                        P.flush(ncv, "cv")
                P.I("act", "activation", wgt[:, js[0]:js[-1] + 1], act[:, js[0]:js[-1] + 1], AF.Gelu_apprx_tanh, reads=ars, writes=[wr])
                P.I("dve", "tensor_tensor", wgt[:, js[0]:js[-1] + 1], wgt[:, js[0]:js[-1] + 1], gflat[:, js[0]:js[-1] + 1], ALU.mult,
                    reads=[wr, greg], writes=[wr])
                for j in js:
                    g, gr = gb[j % NG], f"gb{j % NG}"
                    tb, tr = tmpv[j % 2], f"tmpv{j % 2}"
                    P.I("act", "activation", tb[:], g[:, 1024:2048], AF.Copy, scale=wgt[:, j:j + 1], reads=[gr, wr], writes=[tr])
                    for n in range(2):
                        P.I("pe", "matmul", bk[n], lhsT=self.identb[:], rhs=tb[:, n * 512:(n + 1) * 512], start=(j == 0), stop=(j == 127),
                            reads=[tr, "identb"], writes=[f"PS{n}"])
            P.flush(None)
            self.resid_ln(i, self.bank2(0), ["PS0", "PS1"])

        if l + 1 < 4 and (l + 1) in self.layers:
            P.defer_begin("cv")
            self.convert_tables(l + 1, cstg, ccb)
            ncv_total = P.defer_end()
        else:
            ncv_total = 0
        ncv = (ncv_total + (NT - 1) * 128 - 1) // max(1, (NT - 1) * 128) if ncv_total else 0
        front(0)
        for i in range(NT):
            if i + 1 < NT:
                P.defer_begin()
                front(i + 1)
                n = P.defer_end()
                back(i, (n + 99) // 100, ncv)
            else:
                back(i, 0, ncv)
        P.flush(None, "cv")
        P.pop_scope()

    def load_w(self, dst, src, reg):
        nk = src.shape[0] // 128
        for k in range(nk):
            self.P.Dm("pool", "dma_start", out=dst[:, k, :], in_=src[k * 128:(k + 1) * 128, :],
                       writes=[reg])

    def bank2(self, b):
        return self.psum[:, b:b + 2, :].rearrange("p a n -> p (a n)")

    def final_proj_ln(self, i, XT, xreg, w, wreg, nk, tok0=0):
        P = self.P
        for n in range(2):
            b = 6 + n
            for k in range(nk):
                P.I("pe", "matmul",
                    self.bank[b], lhsT=XT[:, k, tok0:tok0 + 128], rhs=w[:, k, n * 512:(n + 1) * 512],
                    start=(k == 0), stop=(k == nk - 1), reads=[xreg, wreg], writes=[f"PS{b}"])
        self.resid_ln(i, self.bank2(6), ["PS6", "PS7"])

    def make_hT(self, tiles, dst, reg):
        for n, i in enumerate(tiles):
            self.transpose_f32(self.H[:, i, :], 8, dst[:, :, n * 128:(n + 1) * 128], ("H", i), reg, banks=(0, 1))

    def cmlp(self, l):
        P = self.P
        TP, NT = self.TP, self.NT
        bk = self.bank
        d = self.din
        P.push_scope()
        w_in = P.sb([128, 8, 2048], BF16, "cw_in")
        w_out = P.sb([128, 8, 1024], BF16, "cw_out")
        b_in = P.sb([128, 2048], F32, "cb_in")
        cg = P.sb([128, D], F32, "c_lng")
        cb = P.sb([128, D], F32, "c_lnb")
        wsf = P.sb([128, 4, 128], F32, "wsf")
        wsp = P.sb([128, 4, 128], BF16, "wsp")
        wss = P.sb([128, 4, 128], BF16, "wss")
        bsp = P.sb([128, 4], F32, "bsp")
        bss = P.sb([128, 4], F32, "bss")
        hT = P.sb([128, 8, 128], BF16, "c_hT")
        zs = P.sb([128, 2048], F32, "zs")
        vb = P.sb([128, D], BF16, "vb")
        o = P.sb([128, D], F32, "c_o")
        oT = P.sb([128, 8, 128], BF16, "c_oT")
        self.load_w(w_in, d["cmlp_w_in"], "cw_in")
        self.load_w(w_out, d["cmlp_w_out"], "cw_out")
        P.Dm("sp", "dma_start", out=b_in[:], in_=d["cmlp_b_in"][0:1, :].to_broadcast([128, 2048]), writes=["cb_in"])
        P.Dm("sp", "dma_start", out=cg[:], in_=d["cmlp_ln_g"][0:1, :].to_broadcast([128, D]), writes=["c_lng"])
        P.Dm("sp", "dma_start", out=cb[:], in_=d["cmlp_ln_b"][0:1, :].to_broadcast([128, D]), writes=["c_lnb"])
        P.Dm("sp", "dma_start", out=bsp[:], in_=d["cmlp_b_sT"], writes=["bsp"])
        for j in range(NSEQ):
            P.Dm("sp", "dma_start", out=bss[j * 8:(j + 1) * 8, :], in_=d["cmlp_b_sT"][0:8, :], writes=["bss"])
        P.Dm("sp", "dma_start", out=wsf[:], in_=d["cmlp_w_sT"].rearrange("h s t -> s h t"), writes=["wsf"])
        for hh in range(4):
            P.I("pool", "affine_select", wsf[:, hh, :], wsf[:, hh, :], pattern=[[1, 128]],
                                                       compare_op=ALU.is_ge, fill=0.0, base=0, channel_multiplier=-1,
                 reads=["wsf"], writes=["wsf"])
        P.I("dve", "tensor_copy", wsp[:], wsf[:], reads=["wsf"], writes=["wsp"])
        wsf2 = P.sb([128, 4, 128], F32, "wsf2")
        P.I("dve", "memset", wsf2[:], 0.0, writes=["wsf2"])
        for j in range(NSEQ):
            P.Dm("sp", "dma_start", out=wsf2[j * 8:(j + 1) * 8, :, j * 8:(j + 1) * 8],
                                                  in_=d["cmlp_w_sT"][:, 0:8, 0:8].rearrange("h s t -> s h t"),
                  reads=[], writes=["wsf2"])
        for hh in range(4):
            P.I("pool", "affine_select", wsf2[:, hh, :], wsf2[:, hh, :], pattern=[[1, 128]],
                                                       compare_op=ALU.is_ge, fill=0.0, base=0, channel_multiplier=-1,
                 reads=["wsf2"], writes=["wsf2"])
        P.I("dve", "tensor_copy", wss[:], wsf2[:], reads=["wsf2"], writes=["wss"])
        self.load_ln(self.ln1_g, self.ln1_b, l)
        for i in range(NT):
            samp = (i == TP)
            ws_t, bs_t = (wss, bss) if samp else (wsp, bsp)
            wreg, breg = ("wss", "bss") if samp else ("wsp", "bsp")
            self.make_hT([i], hT, "c_hT")
            for n in range(4):
                b = 2 + n
                for k in range(8):
                    P.I("pe", "matmul", bk[b], lhsT=hT[:, k, :], rhs=w_in[:, k, n * 512:(n + 1) * 512],
                                                              start=(k == 0), stop=(k == 7),
                         reads=["c_hT", "cw_in"], writes=[f"PS{b}"])
                P.I("dve", "tensor_tensor", zs[:, n * 512:(n + 1) * 512], bk[b], b_in[:, n * 512:(n + 1) * 512], ALU.add,
                     reads=[f"PS{b}", "cb_in"], writes=[("zs", n)])
                P.I("act", "activation", zs[:, n * 512:(n + 1) * 512], zs[:, n * 512:(n + 1) * 512], AF.Gelu_apprx_tanh,
                     reads=[("zs", n)], writes=[("zs", n)])
            vv = zs[:, 1024:2048]
            sm = self.small
            for j in range(2):
                P.I("dve", "bn_stats", sm[:, j * 6:(j + 1) * 6], zs[:, 1024 + j * 512:1024 + (j + 1) * 512],
                     reads=[("zs", 2 + j)], writes=["small"])
            P.I("dve", "bn_aggr", sm[:, 12:14], sm[:, 0:12], reads=["small"], writes=["small"])
            P.I("act", "activation", sm[:, 14:15], sm[:, 13:14], AF.Sqrt, bias=self.eps_t[:, 0:1],
                 reads=["small", "eps"], writes=["small"])
            P.I("dve", "reciprocal", sm[:, 15:16], sm[:, 14:15], reads=["small"], writes=["small"])
            vregs = [("zs", 2), ("zs", 3)]
            P.I("dve", "tensor_scalar", vv, vv, sm[:, 12:13], sm[:, 15:16], ALU.subtract, ALU.mult,
                 reads=["small"] + vregs, writes=vregs)
            P.I("dve", "tensor_tensor", vv, vv, cg[:], ALU.mult, reads=vregs + ["c_lng"], writes=vregs)
            P.I("dve", "tensor_tensor", vv, vv, cb[:], ALU.add, reads=vregs + ["c_lnb"], writes=vregs)
            if samp:
                P.Dm("sp", "dma_start", out=self.dout["cmlp_v_s"], in_=vv, reads=vregs)
            P.I("act", "copy", vb[:], vv, reads=vregs, writes=["vb"])
            for hh in range(4):
                b = hh // 2
                P.I("pe", "matmul", bk[b][:, (hh % 2) * 256:(hh % 2 + 1) * 256], lhsT=ws_t[:, hh, :],
                                                          rhs=vb[:, hh * 256:(hh + 1) * 256], start=True, stop=True,
                     reads=[wreg, "vb"], writes=[f"PS{b}"])
            for hh in range(4):
                b = hh // 2
                P.I("dve", "scalar_tensor_tensor",
                    o[:, hh * 256:(hh + 1) * 256], bk[b][:, (hh % 2) * 256:(hh % 2 + 1) * 256], bs_t[:, hh:hh + 1],
                    zs[:, hh * 256:(hh + 1) * 256], ALU.add, ALU.mult,
                    reads=[f"PS{b}", breg, ("zs", hh // 2)], writes=["c_o"])
            self.transpose_f32(o, 8, oT, "c_o", "c_oT", banks=(2, 3))
            self.final_proj_ln(i, oT, "c_oT", w_out, "cw_out", 8)
        P.pop_scope()

    def pool(self, l):
        P = self.P
        TP, NT = self.TP, self.NT
        bk = self.bank
        d = self.din
        P.push_scope()
        w_in = P.sb([128, 8, 1024], BF16, "pw_in")
        w_out = P.sb([128, 8, 1024], BF16, "pw_out")
        w_grp = P.sb([128, 8, 256], BF16, "pw_grp")
        scl = P.sb([128, 8], F32, "p_scl")
        rc = P.sb([128, 4, 16], F32, "p_rc")
        BLK = min(4, TP)
        NB = BLK * 128
        HTb = P.sb([128, 8, NB], BF16, "p_HT")
        UT = P.sb([128, 8, 16 + NB], F32, "p_UT")
        SA = P.sb([128, max(16 + NB, 384)], F32, "p_SA")
        SB = P.sb([128, max(16 + NB, 384)], F32, "p_SB")
        ufm = P.sb([128, D], F32, "p_ufm")
        PT = P.sb([128, 8, NB], BF16, "p_PT")
        MT = P.sb([128, 8, NB], BF16, "p_MT")
        US = P.sb([128, 8, NSEQ, 24], F32, "p_US")
        stt = P.sb([120, 2, D], F32, "p_stt")
        utok = P.sb([128, D], F32, "p_utok")
        self.load_w(w_in, d["pool_w_in"], "pw_in")
        self.load_w(w_out, d["pool_w_out"], "pw_out")
        self.load_w(w_grp, d["pool_w_grp"].rearrange("g k n -> (g k) n"), "pw_grp")
        P.Dm("sp", "dma_start", out=scl[:], in_=d["pool_scaleT"], writes=["p_scl"])
        self.load_ln(self.ln1_g, self.ln1_b, l)
        for g in range(4):
            w = 2 ** (g + 1)
            P.I("dve", "tensor_scalar", rc[:, g, :], self.iota16[:], 1.0, float(w), ALU.add, ALU.min,
                 reads=["iota16"], writes=["p_rc"])
        P.I("dve", "reciprocal", rc[:].rearrange("p g t -> p (g t)"), rc[:].rearrange("p g t -> p (g t)"),
             reads=["p_rc"], writes=["p_rc"])
        P.I("dve", "memset", UT[:, :, 0:16], 0.0, writes=["p_UT"])

        def window(c, src3, lo_hi_views, first_block, is_sample):
            raise NotImplementedError

        def pooled_chunk(c, U, SAv, SBv, n0, PTout, first_block, ureg):
            g = c // 2
            lv = g + 1
            cur, curreg = U, ureg
            bufs = [(SAv, "p_SA"), (SBv, "p_SB")]
            for k in range(lv):
                sh = 2 ** k
                lo = 2 ** (k + 1)
                dst, dreg = bufs[k % 2]
                eng = "pool" if (k % 2 == 1 and cur is not U) else "dve"
                P.I(eng, "tensor_tensor",
                    dst(lo, None), cur(lo, None), cur(lo - sh, -sh), ALU.add,
                    reads=[curreg], writes=[dreg])
                cur, curreg = dst, dreg
            w = 2 ** lv
            P.I("dve", "scalar_tensor_tensor", PTout, cur(n0, None), 1.0 / w, U(n0, None), ALU.mult, ALU.subtract,
                 reads=[curreg, ureg], writes=["p_PT"])
            return cur, curreg

        nblk = (TP + BLK - 1) // BLK
        for blk in range(nblk):
            tiles = list(range(blk * BLK, min(TP, (blk + 1) * BLK)))
            nb = len(tiles) * 128
            self.make_hT(tiles, HTb, "p_HT")
            for c in range(8):
                b = 2 + c % 4
                for k in range(8):
                    P.I("pe", "matmul", bk[b][:, 0:nb], lhsT=w_in[:, k, c * 128:(c + 1) * 128], rhs=HTb[:, k, 0:nb],
                                                              start=(k == 0), stop=(k == 7),
                         reads=["pw_in", "p_HT"], writes=[f"PS{b}"])
                self.ev(UT[:, c, 16:16 + nb], bk[b][:, 0:nb], reads=[f"PS{b}"], writes=[("p_UT", c)])
            for c in range(8):
                g = c // 2
                U = lambda a, e, c=c: UT[:, c, a:(16 + nb + e) if e else 16 + nb]
                SAv = lambda a, e: SA[:, a:(16 + nb + e) if e else 16 + nb]
                SBv = lambda a, e: SB[:, a:(16 + nb + e) if e else 16 + nb]
                cur, curreg = pooled_chunk(c, U, SAv, SBv, 16, PT[:, c, 0:nb], blk == 0, ("p_UT", c))
                if blk == 0:
                    P.I("dve", "tensor_tensor", cur(16, None)[:, 0:16], cur(16, None)[:, 0:16], rc[:, g, :], ALU.mult,
                         reads=[curreg, "p_rc"], writes=[curreg])
                    P.I("dve", "tensor_tensor", PT[:, c, 0:16], cur(16, None)[:, 0:16], UT[:, c, 16:32], ALU.subtract,
                         reads=[curreg, ("p_UT", c)], writes=["p_PT"])
            if blk == nblk - 1:
                for c in range(8):
                    P.I("pe", "transpose", bk[0][0:16, c % 4 * 128:(c % 4 + 1) * 128], UT[:, c, nb:nb + 16], self.ident[:],
                         reads=[("p_UT", c), "ident"], writes=["PS0"])
                    if c % 4 == 3:
                        self.ev(utok[0:16, (c - 3) * 128:(c + 1) * 128], bk[0][0:16, :], reads=["PS0"], writes=["p_utok"])
                P.Dm("sp", "dma_start", out=self.dout["pool_p"], in_=utok[1:16, :], reads=["p_utok"])
            else:
                for c in range(8):
                    self.ev(UT[:, c, 1:16], UT[:, c, nb + 1:nb + 16], reads=[("p_UT", c)], writes=[("p_UT", c)], eng="pool")
            self.pool_tail(tiles, nb, PT, MT, w_grp, scl, w_out)
        for half in range(2):
            P.Dm("sp", "dma_start", out=stt[:, half, :], in_=d["state_pool"][half * 120:(half + 1) * 120, :],
                  writes=["p_stt"])
        P.Dm("sp", "dma_start", out=self.dout["pool_s"].rearrange("(s j) d -> s j d", j=15)[:, 0:7, :],
                                          in_=d["state_pool"].rearrange("(s j) d -> s j d", j=15)[:, 8:15, :], reads=[])
        for c in range(8):
            b = c % 2
            for half in range(2):
                P.I("pe", "transpose", bk[b][:, half * 120:(half + 1) * 120],
                                                                 stt[:, half, c * 128:(c + 1) * 128], self.ident[0:120, 0:120],
                     reads=["p_stt", "ident"], writes=[f"PS{b}"])
            self.ev(US[:, c, :, 1:16], bk[b][:, 0:240].rearrange("p (s j) -> p s j", j=15), reads=[f"PS{b}"], writes=[("p_US", c)])
        self.make_hT([TP], HTb, "p_HT")
        for c in range(8):
            b = 2 + c % 4
            for k in range(8):
                P.I("pe", "matmul", bk[b][:, 0:128], lhsT=w_in[:, k, c * 128:(c + 1) * 128], rhs=HTb[:, k, 0:128],
                                                          start=(k == 0), stop=(k == 7),
                     reads=["pw_in", "p_HT"], writes=[f"PS{b}"])
            self.ev(US[:, c, :, 16:24], bk[b][:, 0:128].rearrange("p (s t) -> p s t", t=8), reads=[f"PS{b}"], writes=[("p_US", c)])
        SA3 = SA[:, 0:NSEQ * 24].rearrange("p (s t) -> p s t", t=24)
        SB3 = SB[:, 0:NSEQ * 24].rearrange("p (s t) -> p s t", t=24)
        for c in range(8):
            U = lambda a, e, c=c: US[:, c, :, a:(24 + e) if e else 24]
            SAv = lambda a, e: SA3[:, :, a:(24 + e) if e else 24]
            SBv = lambda a, e: SB3[:, :, a:(24 + e) if e else 24]
            pooled_chunk(c, U, SAv, SBv, 16, PT[:, c, 0:128].rearrange("p (s t) -> p s t", t=8), False, ("p_US", c))
        for c in range(8):
            P.I("act", "copy", ufm[:, c * 128:(c + 1) * 128].rearrange("p (s t) -> p s t", t=8), US[:, c, :, 16:24],
                 reads=[("p_US", c)], writes=["p_ufm"])
        self.transpose_f32_fm(ufm, 8, utok, "p_ufm", "p_utok")
        for j in range(NSEQ):
            P.Dm("sp", "dma_start", out=self.dout["pool_s"][j * 15 + 7:j * 15 + 15, :], in_=utok[j * 8:(j + 1) * 8, :],
                  reads=["p_utok"])
        self.pool_tail([TP], 128, PT, MT, w_grp, scl, w_out)
        P.pop_scope()

    def transpose_f32_fm(self, src, nch, dst, src_reg, dst_reg, banks=(0, 1)):
        P = self.P
        for g in range(0, nch, 4):
            b = banks[(g // 4) % len(banks)]
            for c in range(4):
                P.I("pe", "transpose", self.bank[b][:, c * 128:(c + 1) * 128],
                                                                src[:, (g + c) * 128:(g + c + 1) * 128], self.ident[:],
                     reads=[src_reg, "ident"], writes=[f"PS{b}"])
            self.ev(dst[:, g * 128:(g + 4) * 128], self.bank[b], reads=[f"PS{b}"], writes=[dst_reg])

    def pool_tail(self, tiles, nb, PT, MT, w_grp, scl, w_out):
        P = self.P
        bk = self.bank
        for c in range(8):
            g, oc = c // 2, c % 2
            b = 2 + c % 4
            for kc in range(2):
                P.I("pe", "matmul", bk[b][:, 0:nb], lhsT=w_grp[:, g * 2 + kc, oc * 128:(oc + 1) * 128],
                                                                   rhs=PT[:, g * 2 + kc, 0:nb], start=(kc == 0), stop=(kc == 1),
                     reads=["pw_grp", "p_PT"], writes=[f"PS{b}"])
            P.I("dve", "tensor_scalar", MT[:, c, 0:nb], bk[b][:, 0:nb], scl[:, c:c + 1], None, ALU.mult,
                 reads=[f"PS{b}", "p_scl"], writes=["p_MT"])
        for n, i in enumerate(tiles):
            self.final_proj_ln(i, MT, "p_MT", w_out, "pw_out", 8, tok0=n * 128)


    def sin_turns(self, out, x, n, xreg, oreg, wi, wf, shift=0.0, eng="dve"):
        P = self.P
        e = eng
        if shift:
            P.I(e, "tensor_scalar", wf, x, shift, None, ALU.add, reads=[xreg], writes=["s_wf"])
            src, sreg = wf, "s_wf"
        else:
            src, sreg = x, xreg
        P.I(e, "tensor_copy", wi, src, reads=[sreg], writes=["s_wi"])
        P.I(e, "tensor_copy", out, wi, reads=["s_wi"], writes=[oreg])
        P.I(e, "tensor_tensor", out, src, out, ALU.subtract, reads=[sreg, oreg], writes=[oreg])
        P.I("dve", "scalar_tensor_tensor", out, out, 0.5, out, ALU.is_gt, ALU.subtract, reads=[oreg], writes=[oreg])
        P.I("dve", "scalar_tensor_tensor", out, out, 0.5, out, ALU.is_gt, ALU.subtract, reads=[oreg], writes=[oreg])
        P.I("act", "activation", out, out, AF.Sin, scale=2.0 * np.pi * (1.0 - 1e-6), reads=[oreg], writes=[oreg])

    def s5(self, l):
        P = self.P
        TP, NT = self.TP, self.NT
        bk = self.bank
        d = self.din
        BLK = min(4, TP)
        NB = BLK * 128
        P.push_scope()
        dcol = P.sb([128, 8], F32, "s_d")
        bglu = P.sb([128, 8], F32, "s_bglu")
        cst = P.sb([128, 12, 32], F32, "s_cst")
        RHO, RT, LBR, LBI, NLBI, FR, FI, T0, T1_, T2_, T3_, T4_ = [cst[:, j, :] for j in range(12)]
        wi32 = P.sb([128, 512], I32, "s_wi")
        wf32 = P.sb([128, 512], F32, "s_wf")
        BbT = P.sb([128, 32, 2, 128], BF16, "s_BbT")
        CT = P.sb([128, 32, 2, 128], BF16, "s_CT")
        Dg = P.sb([128, 8, 128], BF16, "s_Dg")
        HE = P.sb([128, 2, 32, 16], F32, "s_HE")
        H0 = P.sb([128, 2, 32, 16], F32, "s_H0")
        iotaS = P.sb([128, 512], F32, "s_iota")
        mask01 = P.sb([128, 128], F32, "s_m01")
        ones = P.sb([128, 512], F32, "s_ones")
        UTb = P.sb([128, 8, NB], BF16, "s_UT")
        GTb = P.sb([128, 8, NB], BF16, "s_GT")
        P.push_scope()
        aT = P.sb([128, 3, 32], F32, "s_aT")
        bT = P.sb([128, 2, 32, 16], F32, "s_bT")
        cT = P.sb([128, 2, 32, 16], F32, "s_cT")
        bb = P.sb([128, 2, 32, 16], F32, "s_bb")
        P.Dm("sp", "dma_start", out=aT[:], in_=d["s5_aT"], writes=["s_aT"])
        P.Dm("sp", "dma_start", out=bT[:], in_=d["s5_bT"], writes=["s_bT"])
        P.Dm("sp", "dma_start", out=cT[:], in_=d["s5_cT"], writes=["s_cT"])
        P.Dm("sp", "dma_start", out=dcol[:], in_=d["s5_dT"], writes=["s_d"])
        P.Dm("sp", "dma_start", out=bglu[:], in_=d["s5_b_gluT"], writes=["s_bglu"])
        self.load_ln(self.ln1_g, self.ln1_b, l)
        P.I("pool", "iota", iotaS[:], pattern=[[1, 512]], base=0, channel_multiplier=0,
                                      allow_small_or_imprecise_dtypes=True, writes=["s_iota"])
        P.I("dve", "memset", ones[:], 1.0, writes=["s_ones"])
        P.I("dve", "memset", mask01[:], 1.0, writes=["s_m01"])
        P.I("dve", "memset", mask01[:].rearrange("p (s t) -> p s t", t=8)[:, :, 0:1], 0.0, writes=["s_m01"])
        P.I("dve", "memset", HE[:], 0.0, writes=["s_HE"])
        A_RE, A_IM, LDT = aT[:, 0, :], aT[:, 1, :], aT[:, 2, :]
        C = "s_cst"

        def tt(out, a, b, op, eng="dve", extra=()):
            P.I(eng, "tensor_tensor", out, a, b, op, reads=[C, "s_aT"] + list(extra), writes=[C])
        P.I("act", "activation", T0, LDT, AF.Exp, reads=["s_aT"], writes=[C])
        tt(T1_, A_RE, T0, ALU.mult)
        P.I("act", "activation", RHO, T1_, AF.Exp, reads=[C], writes=[C])
        tt(T2_, A_IM, T0, ALU.mult)
        P.I("dve", "tensor_scalar", RT, T2_, 1.0 / (2.0 * np.pi), None, ALU.mult, reads=[C], writes=[C])
        self.sin_turns(T3_, RT, 32, C, C, wi32[:, 0:32], wf32[:, 0:32])
        self.sin_turns(T4_, RT, 32, C, C, wi32[:, 0:32], wf32[:, 0:32], shift=0.25)
        tt(LBR, RHO, T4_, ALU.mult)
        tt(LBI, RHO, T3_, ALU.mult)
        P.I("dve", "tensor_scalar", NLBI, LBI, -1.0, None, ALU.mult, reads=[C], writes=[C])
        tt(T0, A_RE, A_RE, ALU.mult)
        tt(T1_, A_IM, A_IM, ALU.mult)
        tt(T0, T0, T1_, ALU.add)
        P.I("dve", "reciprocal", T0, T0, reads=[C], writes=[C])
        P.I("dve", "tensor_scalar", T1_, LBR, -1.0, None, ALU.add, reads=[C], writes=[C])
        tt(T2_, T1_, A_RE, ALU.mult)
        tt(T3_, LBI, A_IM, ALU.mult)
        tt(T2_, T2_, T3_, ALU.add)
        tt(FR, T2_, T0, ALU.mult)
        tt(T2_, LBI, A_RE, ALU.mult)
        tt(T3_, T1_, A_IM, ALU.mult)
        tt(T2_, T2_, T3_, ALU.subtract)
        tt(FI, T2_, T0, ALU.mult)
        frb = FR.unsqueeze(2).to_broadcast([128, 32, 16])
        fib = FI.unsqueeze(2).to_broadcast([128, 32, 16])
        tmpb = P.sb([128, 32, 16], F32, "s_tmpb")
        P.I("dve", "tensor_tensor", bb[:, 0], frb, bT[:, 0], ALU.mult, reads=[C, "s_bT"], writes=["s_bb"])
        P.I("dve", "tensor_tensor", tmpb[:], fib, bT[:, 1], ALU.mult, reads=[C, "s_bT"], writes=["s_tmpb"])
        P.I("dve", "tensor_tensor", bb[:, 0], bb[:, 0], tmpb[:], ALU.subtract, reads=["s_bb", "s_tmpb"], writes=["s_bb"])
        P.I("dve", "tensor_tensor", bb[:, 1], frb, bT[:, 1], ALU.mult, reads=[C, "s_bT", "s_bb"], writes=["s_bb"])
        P.I("dve", "tensor_tensor", tmpb[:], fib, bT[:, 0], ALU.mult, reads=[C, "s_bT", "s_bb"], writes=["s_tmpb"])
        P.I("dve", "tensor_tensor", bb[:, 1], bb[:, 1], tmpb[:], ALU.add, reads=["s_bb", "s_tmpb"], writes=["s_bb"])
        pads = P.sb([128, 2, 4, 128], F32, "s_pads")
        P.I("dve", "memset", pads[:], 0.0, writes=["s_pads"])
        for gp in range(32):
            q = gp % 4
            for ri in range(2):
                for g2 in range(2):
                    P.I("pool", "tensor_copy",
                        pads[g2 * 64:(g2 + 1) * 64, ri, q, q * 32 + g2 * 16:q * 32 + g2 * 16 + 16],
                        bb[g2 * 64:(g2 + 1) * 64, ri, gp, :], reads=["s_bb", "s_pads"], writes=["s_pads"])
            b = gp % 2
            for ri in range(2):
                P.I("pe", "transpose", bk[b][:, ri * 128:(ri + 1) * 128], pads[:, ri, q, :], self.ident[:],
                     reads=["s_pads", "ident"], writes=[f"PS{b}"])
            self.ev(BbT[:, gp, :, :], bk[b][:, 0:256].rearrange("p (r n) -> p r n", r=2), reads=[f"PS{b}"], writes=["s_BbT"])
        P.I("dve", "memset", CT[:], 0.0, writes=["s_CT"])
        P.I("dve", "tensor_scalar", cT[:, 1], cT[:, 1], -1.0, None, ALU.mult, reads=["s_cT"], writes=["s_cT"])
        for q in range(4):
            for ri in range(2):
                for g2 in range(2):
                    dst = CT[g2 * 64:(g2 + 1) * 64, :, ri, :].rearrange("p (a q) r -> p a q r", q=4)[:, :, q, q * 32 + g2 * 16:q * 32 + g2 * 16 + 16]
                    src = cT[g2 * 64:(g2 + 1) * 64, ri].rearrange("p (a q) i -> p a q i", q=4)[:, :, q, :]
                    P.I("dve", "tensor_copy", dst, src, reads=["s_cT"], writes=["s_CT"])
        for c in range(8):
            P.I("dve", "tensor_scalar", Dg[:, c, :], self.ident[:], dcol[:, c:c + 1], None, ALU.mult,
                 reads=["ident", "s_d"], writes=["s_Dg"])
        st = P.sb([16, 2, 4096], F32, "s_st")
        P.Dm("sp", "dma_start", out=st[:, 0, :], in_=d["state_s5_re"], writes=["s_st"])
        P.Dm("sp", "dma_start", out=st[:, 1, :], in_=d["state_s5_im"], writes=["s_st"])
        for ri in range(2):
            for g8 in range(4):
                b = g8 % 2
                for j in range(8):
                    gp = g8 * 8 + j
                    P.I("pe", "transpose", bk[b][:, j * 16:(j + 1) * 16], st[:, ri, gp * 128:(gp + 1) * 128],
                                                                      self.ident[0:16, 0:16],
                         reads=["s_st", "ident"], writes=[f"PS{b}"])
                self.ev(H0[:, ri, g8 * 8:(g8 + 1) * 8, :], bk[b][:, 0:128].rearrange("p (g s) -> p g s", s=16),
                        reads=[f"PS{b}"], writes=["s_H0"])

        P.pop_scope()
        nblk = (TP + BLK - 1) // BLK
        blocks = [(list(range(bi * BLK, min(TP, (bi + 1) * BLK))), False) for bi in range(nblk)] + [([TP], True)]
        for bidx, (tiles, samp) in enumerate(blocks):
            N = len(tiles) * 128
            P.push_scope()
            w_in = P.sb([128, 8, 1024], BF16, "sw_in")
            HTb = P.sb([128, 8, NB], BF16, "s_HT")
            self.load_w(w_in, d["s5_w_in"], "sw_in")
            self.make_hT(tiles, HTb, "s_HT")
            for c in range(8):
                b = 2 + c % 4
                for k in range(8):
                    P.I("pe", "matmul", bk[b][:, 0:N], lhsT=w_in[:, k, c * 128:(c + 1) * 128], rhs=HTb[:, k, 0:N],
                                                              start=(k == 0), stop=(k == 7), reads=["sw_in", "s_HT"], writes=[f"PS{b}"])
                self.ev(UTb[:, c, 0:N], bk[b][:, 0:N], reads=[f"PS{b}"], writes=[("s_UT", c)])
            P.pop_scope()
            P.push_scope()
            tab = [P.sb([128, 3, 512], F32, f"s_tab{j}") for j in range(2)]
            wk4_2 = [[P.sb([128, 512], F32, f"s_w{j}_{z}") for j in range(4)] for z in range(2)]
            BR_2 = [P.sb([128, 512], F32, f"s_BR{z}") for z in range(2)]; BI_2 = [P.sb([128, 512], F32, f"s_BI{z}") for z in range(2)]
            GR_2 = [P.sb([128, 512], F32, f"s_GR{z}") for z in range(2)]; GI_2 = [P.sb([128, 512], F32, f"s_GI{z}") for z in range(2)]
            HRb = [P.sb([128, 512], BF16, f"s_HR{j}") for j in range(2)]
            HIb = [P.sb([128, 512], BF16, f"s_HI{j}") for j in range(2)]
            inj = P.sb([128, 4, 16], F32, "s_inj")
            ns = NSEQ if samp else 1

            def V(t, n=N):
                return t[:, 0:n].rearrange("p (s t) -> p s t", t=8) if samp else t[:, 0:n]

            def TV(t):
                return t[:, 0:8].unsqueeze(1).to_broadcast([128, NSEQ, 8]) if samp else t[:, 0:N]

            def starts(t):
                return t[:, 0:N].rearrange("p (s t) -> p s t", t=8)[:, :, 0] if samp else t[:, 0:1]

            def ends(t):
                return t[:, 0:N].rearrange("p (s t) -> p s t", t=8)[:, :, 7] if samp else t[:, N - 1:N]
            nloc = 8 if samp else N

            def X(eng, meth, reads, writes, *args, **kw):
                if eng == "sp_dma":
                    P.dma("sp", lambda h: getattr(h, meth)(*args, **kw), reads=reads, writes=writes)
                else:
                    P.op(eng, lambda h: getattr(h, meth)(*args, **kw), reads=reads, writes=writes)
            for c in range(8):
                by = c % 2
                X("pe", "matmul", ["s_Dg", ("s_UT", c)], [f"PS{by}"], bk[by][:, 0:N], lhsT=Dg[:, c, :], rhs=UTb[:, c, 0:N], start=True, stop=False)
                for q in range(4):
                    gp = 4 * c + q
                    sl = gp % 2
                    P.flush(6, "cv0")
                    wk4, BR, BI, GR, GI = wk4_2[sl], BR_2[sl], BI_2[sl], GR_2[sl], GI_2[sl]
                    Z = f"_{sl}"
                    tb, treg = tab[sl], f"s_tab{sl}"
                    cosT, sinT, rhoT = tb[:, 0, :], tb[:, 1, :], tb[:, 2, :]
                    X("dve", "tensor_scalar", ["s_iota", C], [treg], rhoT[:, 0:nloc], iotaS[:, 0:nloc], RT[:, gp:gp + 1], None, ALU.mult)
                    self.sin_turns(sinT[:, 0:nloc], rhoT[:, 0:nloc], nloc, treg, treg, wi32[:, 0:nloc], wf32[:, 0:nloc], eng="dve")
                    self.sin_turns(cosT[:, 0:nloc], rhoT[:, 0:nloc], nloc, treg, treg, wi32[:, 0:nloc], wf32[:, 0:nloc], shift=0.25, eng="dve")
                    if samp:
                        X("dve", "tensor_scalar", ["s_m01", C, treg], [treg], rhoT[:, 0:N], mask01[:, 0:N], RHO[:, gp:gp + 1], None, ALU.mult)
                    else:
                        X("dve", "tensor_scalar", ["s_ones", C, treg], [treg], rhoT[:, 0:N], ones[:, 0:N], RHO[:, gp:gp + 1], None, ALU.mult)
                    for ri in range(2):
                        X("pe", "matmul", ["s_BbT", ("s_UT", c)], [f"PS{2 + ri}"], bk[2 + ri][:, 0:N], lhsT=BbT[:, gp, ri, :], rhs=UTb[:, c, 0:N],
                          start=True, stop=True)
                    pr, pi = bk[2][:, 0:N], bk[3][:, 0:N]
                    if samp:
                        pr = pr.rearrange("p (s t) -> p s t", t=8); pi = pi.rearrange("p (s t) -> p s t", t=8)
                    X("dve", "tensor_tensor", ["PS2", treg], ["s_w0" + Z], V(wk4[0]), pr, TV(cosT), ALU.mult)
                    X("dve", "tensor_tensor", ["PS3", treg], ["s_w1" + Z], V(wk4[1]), pi, TV(sinT), ALU.mult)
                    X("dve", "tensor_tensor", ["PS3", treg], ["s_w2" + Z], V(wk4[2]), pi, TV(cosT), ALU.mult)
                    X("dve", "tensor_tensor", ["PS2", treg], ["s_w3" + Z], V(wk4[3]), pr, TV(sinT), ALU.mult)
                    X("dve", "tensor_tensor", ["s_w0" + Z, "s_w1" + Z], ["s_BR" + Z], BR[:, 0:N], wk4[0][:, 0:N], wk4[1][:, 0:N], ALU.add)
                    X("dve", "tensor_tensor", ["s_w2" + Z, "s_w3" + Z], ["s_BI" + Z], BI[:, 0:N], wk4[2][:, 0:N], wk4[3][:, 0:N], ALU.subtract)
                    if samp or bidx > 0:
                        hp = H0 if samp else HE
                        hreg = "s_H0" if samp else "s_HE"
                        hr, hi = hp[:, 0, gp, 0:ns], hp[:, 1, gp, 0:ns]
                        X("dve", "tensor_scalar", [hreg, C], ["s_inj"], inj[:, 0, 0:ns], hr, LBR[:, gp:gp + 1], None, ALU.mult)
                        X("dve", "scalar_tensor_tensor", [hreg, C, "s_inj"], ["s_inj"], inj[:, 0, 0:ns], hi, NLBI[:, gp:gp + 1], inj[:, 0, 0:ns], ALU.mult, ALU.add)
                        X("dve", "tensor_scalar", [hreg, C, "s_inj"], ["s_inj"], inj[:, 1, 0:ns], hi, LBR[:, gp:gp + 1], None, ALU.mult)
                        X("dve", "scalar_tensor_tensor", [hreg, C, "s_inj"], ["s_inj"], inj[:, 1, 0:ns], hr, LBI[:, gp:gp + 1], inj[:, 1, 0:ns], ALU.mult, ALU.add)
                        X("dve", "tensor_tensor", ["s_BR" + Z, "s_inj"], ["s_BR" + Z], starts(BR), starts(BR), inj[:, 0, 0:ns], ALU.add)
                        X("dve", "tensor_tensor", ["s_BI" + Z, "s_inj"], ["s_BI" + Z], starts(BI), starts(BI), inj[:, 1, 0:ns], ALU.add)
                    X("dve", "tensor_tensor_scan", [treg, "s_BR" + Z], ["s_GR" + Z], GR[:, 0:N], rhoT[:, 0:N], BR[:, 0:N], 0.0, ALU.mult, ALU.add)
                    X("dve", "tensor_tensor_scan", [treg, "s_BI" + Z], ["s_GI" + Z], GI[:, 0:N], rhoT[:, 0:N], BI[:, 0:N], 0.0, ALU.mult, ALU.add)
                    hs = gp % 2
                    X("dve", "tensor_tensor", ["s_GR" + Z, treg], ["s_w0" + Z], V(wk4[0]), V(GR), TV(cosT), ALU.mult)
                    X("dve", "tensor_tensor", ["s_GI" + Z, treg], ["s_w1" + Z], V(wk4[1]), V(GI), TV(sinT), ALU.mult)
                    X("dve", "tensor_tensor", ["s_GR" + Z, treg], ["s_w2" + Z], V(wk4[2]), V(GR), TV(sinT), ALU.mult)
                    X("dve", "tensor_tensor", ["s_GI" + Z, treg], ["s_w3" + Z], V(wk4[3]), V(GI), TV(cosT), ALU.mult)
                    X("dve", "tensor_tensor", ["s_w0" + Z, "s_w1" + Z], [f"s_HR{hs}"], HRb[hs][:, 0:N], wk4[0][:, 0:N], wk4[1][:, 0:N], ALU.subtract)
                    X("dve", "tensor_tensor", ["s_w2" + Z, "s_w3" + Z], [f"s_HI{hs}"], HIb[hs][:, 0:N], wk4[2][:, 0:N], wk4[3][:, 0:N], ALU.add)
                    X("dve", "tensor_tensor", ["s_w0" + Z, "s_w1" + Z], ["s_HE"], HE[:, 0, gp, 0:ns], ends(wk4[0]), ends(wk4[1]), ALU.subtract)
                    X("dve", "tensor_tensor", ["s_w2" + Z, "s_w3" + Z], ["s_HE"], HE[:, 1, gp, 0:ns], ends(wk4[2]), ends(wk4[3]), ALU.add)
                    X("pe", "matmul", ["s_CT", f"s_HR{hs}"], [f"PS{by}"], bk[by][:, 0:N], lhsT=CT[:, gp, 0, :], rhs=HRb[hs][:, 0:N], start=False, stop=False)
                    X("pe", "matmul", ["s_CT", f"s_HI{hs}"], [f"PS{by}"], bk[by][:, 0:N], lhsT=CT[:, gp, 1, :], rhs=HIb[hs][:, 0:N], start=False, stop=(q == 3))
                X("act", "activation", [f"PS{by}"], [("s_GT", c)], GTb[:, c, 0:N], bk[by][:, 0:N], AF.Gelu_apprx_tanh)
            P.pop_scope()
            if samp or bidx == nblk - 1:
                nm = ("s5_re_s", "s5_im_s") if samp else ("s5_re_p", "s5_im_p")
                P.push_scope()
                so = P.sb([16, 2, 4096], F32, "s_so")
                for ri in range(2):
                    if samp:
                        for g8 in range(8):
                            b = g8 % 2
                            for j in range(4):
                                gp = g8 * 4 + j
                                X("pe", "transpose", ["s_HE", "ident"], [f"PS{b}"], bk[b][0:16, j * 128:(j + 1) * 128], HE[:, ri, gp, :], self.ident[:])
                            self.ev(so[:, ri, g8 * 512:(g8 + 1) * 512], bk[b][0:16, :], reads=[f"PS{b}"], writes=["s_so"])
                        X("sp_dma", "dma_start", ["s_so"], [], out=self.dout[nm[ri]], in_=so[:, ri, :])
                    else:
                        X("pe", "transpose", ["s_HE", "ident"], [f"PS{ri}"], bk[ri][0:32, 0:128], HE[:, ri, :, 0], self.ident[:])
                        X("dve", "tensor_copy", [f"PS{ri}"], ["tmp"], self.tmp[0:32, ri * 128:(ri + 1) * 128], bk[ri][0:32, 0:128])
                        X("sp_dma", "dma_start", ["tmp"], [], out=self.dout[nm[ri]], in_=self.tmp[0:32, ri * 128:(ri + 1) * 128])
                P.pop_scope()
            P.push_scope()
            w_glu = P.sb([128, 8, 1024], BF16, "sw_glu")
            w_out = P.sb([128, 8, 1024], BF16, "sw_out")
            OT = P.sb([128, 8, NB], BF16, "s_OT")
            SG = [P.sb([128, 512], F32, f"s_SG{j}") for j in range(2)]
            self.load_w(w_glu, d["s5_w_glu"], "sw_glu")
            self.load_w(w_out, d["s5_w_out"], "sw_out")
            for oc in range(8):
                b = 2 + oc % 4
                for k in range(8):
                    P.I("pe", "matmul", bk[b][:, 0:N], lhsT=w_glu[:, k, oc * 128:(oc + 1) * 128], rhs=GTb[:, k, 0:N],
                                                               start=(k == 0), stop=(k == 7), reads=["sw_glu", ("s_GT", k)], writes=[f"PS{b}"])
                sg = SG[oc % 2]
                P.I("act", "activation", sg[:, 0:N], bk[b][:, 0:N], AF.Sigmoid, bias=bglu[:, oc:oc + 1],
                     reads=[f"PS{b}", "s_bglu"], writes=[f"s_SG{oc % 2}"])
                P.I("dve", "tensor_tensor", OT[:, oc, 0:N], GTb[:, oc, 0:N], sg[:, 0:N], ALU.mult,
                     reads=[("s_GT", oc), f"s_SG{oc % 2}"], writes=["s_OT"])
            for n, i in enumerate(tiles):
                self.final_proj_ln(i, OT, "s_OT", w_out, "sw_out", 8, tok0=n * 128)
            P.pop_scope()
        P.pop_scope()


    def ssd(self, l):
        P = self.P
        nc = self.nc
        TP, NT = self.TP, self.NT
        bk = self.bank
        d = self.din
        LP = TP * 128
        BLK = min(4, TP)
        NB = BLK * 128
        scr_z = nc.dram_tensor("scr_z", [NT * 128, 2048], F32, kind="Internal").ap()
        scr_x = nc.dram_tensor("scr_x", [24, 128, 3 + LP], F32, kind="Internal").ap()
        scr_xs = nc.dram_tensor("scr_xs", [24, 128, NSEQ * 11], F32, kind="Internal").ap()
        P.push_scope()
        DT = P.sb([128, NT, 32], F32, "d_DT")
        hd = P.sb([128, 96], F32, "d_hd")
        cwT = P.sb([128, 24, 4], F32, "d_cw")
        cbT = P.sb([128, 24], F32, "d_cb")
        ngT = P.sb([128, 16], F32, "d_ng")
        P.Dm("sp", "dma_start", out=hd[:], in_=d["ssd_hd"][0:1, :].to_broadcast([128, 96]), writes=["d_hd"])
        P.Dm("sp", "dma_start", out=cwT[:], in_=d["ssd_conv_wT"], writes=["d_cw"])
        P.Dm("sp", "dma_start", out=cbT[:], in_=d["ssd_conv_bT"], writes=["d_cb"])
        P.Dm("sp", "dma_start", out=ngT[:], in_=d["ssd_norm_gT"], writes=["d_ng"])
        P.I("act", "activation", hd[:, 32:64], hd[:, 32:64], AF.Exp, reads=["d_hd"], writes=["d_hd"])
        P.I("dve", "tensor_scalar", hd[:, 32:64], hd[:, 32:64], -1.0, None, ALU.mult, reads=["d_hd"], writes=["d_hd"])
        self.load_ln(self.ln1_g, self.ln1_b, l)
        P.push_scope()
        w_in = P.sb([128, 8, 5152], BF16, "dw_in")
        HTb = P.sb([128, 8, NB], BF16, "d_HT")
        stage = [P.sb([128, 512], F32, f"d_stg{j}") for j in range(2)]
        zst = P.sb([128, 2048], F32, "d_zst")
        cstt = P.sb([48, 3072], F32, "d_cst")
        self.load_w(w_in, d["ssd_w_in"], "dw_in")
        P.I("dve", "memset", zst[:, 0:72], 0.0, writes=["d_zst"])
        P.Dm("sp", "dma_start", out=scr_x[:, :, 0:3].rearrange("c p t -> p c t"), in_=zst[:, 0:72].rearrange("p (c t) -> p c t", t=3),
             reads=["d_zst"], writes=["scr_x"])
        P.Dm("sp", "dma_start", out=cstt[:], in_=d["state_ssd_conv"], writes=["d_cst"])
        for c in range(24):
            b = c % 2
            P.I("pe", "transpose", bk[b][:, 0:48], cstt[:, c * 128:(c + 1) * 128], self.ident[0:48, 0:48],
                reads=["d_cst", "ident"], writes=[f"PS{b}"])
            sg = stage[c % 2]
            self.ev(sg[:, 0:48], bk[b][:, 0:48], reads=[f"PS{b}"], writes=[f"d_stg{c % 2}"])
            P.Dm("sp", "dma_start", out=scr_xs[c].rearrange("p (s t) -> p s t", t=11)[:, :, 0:3],
                 in_=sg[:, 0:48].rearrange("p (s t) -> p s t", t=3), reads=[f"d_stg{c % 2}"], writes=["scr_xs"])
        nblk = (TP + BLK - 1) // BLK
        blocks = [(list(range(bi * BLK, min(TP, (bi + 1) * BLK))), False) for bi in range(nblk)] + [([TP], True)]
        for bidx, (tiles, samp) in enumerate(blocks):
            N = len(tiles) * 128
            tok0 = tiles[0] * 128
            self.make_hT(tiles, HTb, "d_HT")
            for c in range(24):
                b = 2 + c % 4
                for k in range(8):
                    P.I("pe", "matmul", bk[b][:, 0:N], lhsT=w_in[:, k, 2048 + c * 128:2048 + (c + 1) * 128], rhs=HTb[:, k, 0:N],
                        start=(k == 0), stop=(k == 7), reads=["dw_in", "d_HT"], writes=[f"PS{b}"])
                sg = stage[c % 2]
                self.ev(sg[:, 0:N], bk[b][:, 0:N], reads=[f"PS{b}"], writes=[f"d_stg{c % 2}"])
                if samp:
                    P.Dm("sp", "dma_start", out=scr_xs[c].rearrange("p (s t) -> p s t", t=11)[:, :, 3:11],
                         in_=sg[:, 0:128].rearrange("p (s t) -> p s t", t=8), reads=[f"d_stg{c % 2}"], writes=["scr_xs"])
                else:
                    P.Dm("sp", "dma_start", out=scr_x[c, :, 3 + tok0:3 + tok0 + N], in_=sg[:, 0:N],
                         reads=[f"d_stg{c % 2}"], writes=["scr_x"])
            for n, i in enumerate(tiles):
                hTi = HTb[:, :, n * 128:(n + 1) * 128]
                for nn in range(4):
                    b = 2 + nn
                    for k in range(8):
                        P.I("pe", "matmul", bk[b], lhsT=hTi[:, k, :], rhs=w_in[:, k, nn * 512:(nn + 1) * 512],
                            start=(k == 0), stop=(k == 7), reads=["dw_in", "d_HT"], writes=[f"PS{b}"])
                    self.ev(zst[:, nn * 512:(nn + 1) * 512], bk[b], reads=[f"PS{b}"], writes=["d_zst"])
                P.Dm("sp", "dma_start", out=scr_z[i * 128:(i + 1) * 128, :], in_=zst[:], reads=["d_zst"], writes=["scr_z"])
                for k in range(8):
                    P.I("pe", "matmul", bk[6][:, 0:32], lhsT=hTi[:, k, :], rhs=w_in[:, k, 5120:5152],
                        start=(k == 0), stop=(k == 7), reads=["dw_in", "d_HT"], writes=["PS6"])
                P.I("dve", "tensor_tensor", DT[:, i, :], bk[6][:, 0:32], hd[:, 0:32], ALU.add, reads=["PS6", "d_hd"], writes=["d_DT"])
                P.I("act", "activation", DT[:, i, :], DT[:, i, :], AF.Exp, reads=["d_DT"], writes=["d_DT"])
                P.I("act", "activation", DT[:, i, :], DT[:, i, :], AF.Ln, bias=self.one_t[:, 0:1], reads=["d_DT", "one"], writes=["d_DT"])
                if samp or i == TP - 1:
                    for hf in range(2):
                        for nn in range(3):
                            b = 2 + nn
                            c0 = 2048 + hf * 1536 + nn * 512
                            for k in range(8):
                                P.I("pe", "matmul", bk[b], lhsT=hTi[:, k, :], rhs=w_in[:, k, c0:c0 + 512],
                                    start=(k == 0), stop=(k == 7), reads=["dw_in", "d_HT"], writes=[f"PS{b}"])
                            self.ev(zst[:, nn * 512:(nn + 1) * 512], bk[b], reads=[f"PS{b}"], writes=["d_zst"])
                        if samp:
                            for j in range(NSEQ):
                                P.Dm("sp", "dma_start", out=self.dout["conv_s"][j * 3:j * 3 + 3, hf * 1536:(hf + 1) * 1536],
                                     in_=zst[j * 8 + 5:j * 8 + 8, 0:1536], reads=["d_zst"])
                        else:
                            P.Dm("sp", "dma_start", out=self.dout["conv_p"][:, hf * 1536:(hf + 1) * 1536], in_=zst[125:128, 0:1536],
                                 reads=["d_zst"])
        P.pop_scope()
        P.push_scope()
        w_out = P.sb([128, 16, 1024], BF16, "dw_out")
        self.load_w(w_out, d["ssd_w_out"], "dw_out")
        for k in range(16):
            P.I("dve", "tensor_scalar", w_out[:, k, :], w_out[:, k, :], ngT[:, k:k + 1], None, ALU.mult,
                reads=["dw_out", "d_ng"], writes=["dw_out"])
        mk = P.sb([128, 8, 128], F32, "d_mk")
        ONES, TRI, STRICT, BONES, TRIS = [mk[:, j, :] for j in range(5)]
        mki = P.sb([128, 2, 128], I32, "d_mki")
        mcol = P.sb([128, 24], F32, "d_mcol")
        M = "d_mk"
        P.I("dve", "memset", ONES, 1.0, writes=[M])
        P.I("pool", "affine_select", TRI, ONES, pattern=[[1, 128]], compare_op=ALU.is_ge, fill=0.0, base=0, channel_multiplier=-1,
            reads=[M], writes=[M])
        P.I("pool", "affine_select", STRICT, ONES, pattern=[[-1, 128]], compare_op=ALU.is_gt, fill=0.0, base=0, channel_multiplier=1,
            reads=[M], writes=[M])
        P.I("pool", "iota", mki[:, 0, :], pattern=[[1, 128]], base=0, channel_multiplier=0, reads=[], writes=["d_mki"])
        P.I("pool", "iota", mki[:, 1, :], pattern=[[0, 128]], base=0, channel_multiplier=1, reads=["d_mki"], writes=["d_mki"])
        P.I("dve", "tensor_single_scalar", mki[:].rearrange("p a n -> p (a n)"), mki[:].rearrange("p a n -> p (a n)"), 3, ALU.arith_shift_right,
            reads=["d_mki"], writes=["d_mki"])
        P.I("dve", "tensor_copy", mk[:, 5:7, :], mki[:], reads=["d_mki"], writes=[M])
        P.I("dve", "tensor_tensor", BONES, mk[:, 5, :], mk[:, 6, :], ALU.is_equal, reads=[M], writes=[M])
        P.I("dve", "tensor_tensor", TRIS, BONES, TRI, ALU.mult, reads=[M], writes=[M])
        P.I("dve", "tensor_tensor", mcol[:, 0:16], self.iota16[:], mk[:, 6, 0:16], ALU.is_equal, reads=[M, "iota16"], writes=["d_mcol"])
        P.I("dve", "tensor_single_scalar", mcol[:, 16:17], mk[:, 6, 0:1], 8.0, ALU.is_lt, reads=[M, "d_mcol"], writes=["d_mcol"])
        P.I("dve", "tensor_single_scalar", mcol[:, 17:18], mk[:, 6, 0:1], 8.0, ALU.is_ge, reads=[M, "d_mcol"], writes=["d_mcol"])
        XB = P.sb([128, 8, 176], F32, "d_XB")
        CA2 = [P.sb([128, 128], F32, f"d_CA{j}") for j in range(2)]
        XCf = P.sb([128, 8, 128], F32, "d_XCf")
        BTb = P.sb([128, 4, 128], BF16, "d_BTb")
        CTb = P.sb([128, 4, 128], BF16, "d_CTb")
        Xb = P.sb([128, 2048], BF16, "d_Xb")
        Btb = P.sb([128, 512], BF16, "d_Btb")
        Xw = P.sb([128, 2048], BF16, "d_Xw")
        Y = P.sb([128, 2048], F32, "d_Y")
        zt = P.sb([128, 2048], F32, "d_zt")
        CBm = P.sb([128, 4, 128], F32, "d_CBm")
        R16 = P.sb([128, 16, 128], F32, "d_R16")
        DEC = [P.sb([128, 128], F32, f"d_DEC{j}") for j in range(4)]
        MTb = [P.sb([128, 128], BF16, f"d_MT{j}") for j in range(4)]
        YT = P.sb([128, 16, 128], BF16, "d_YT")
        sv = P.sb([128, 8, 32], F32, "d_sv")
        DTA, ACS, ALAST, EA, DE, WSC, CDP, STMP = [sv[:, j, :] for j in range(8)]
        ss = P.sb([128, 8], F32, "d_ss")
        S = "d_sv"
        A32, D32 = hd[:, 32:64], hd[:, 64:96]

        def bc(a):
            return a.unsqueeze(2).to_broadcast([128, 32, 64])

        def v3(t):
            return t[:, :].rearrange("p (h e) -> p h e", e=64)
        ps4 = lambda b0: self.psum[:, b0:b0 + 4, :].rearrange("p a n -> p (a n)")

        def tile_front(i, samp):
            tri = TRIS if samp else TRI
            tok0 = i * 128
            for cg in range(3):
                if samp:
                    P.Dm("sp", "dma_start", out=XB[:, :, 0:176], in_=scr_xs[cg * 8:(cg + 1) * 8].rearrange("c p t -> p c t"),
                         reads=["scr_xs"], writes=["d_XB"])
                else:
                    P.Dm("sp", "dma_start", out=XB[:, :, 0:131], in_=scr_x[cg * 8:(cg + 1) * 8, :, tok0:tok0 + 131].rearrange("c p t -> p c t"),
                         reads=["scr_x"], writes=["d_XB"])
                for cc in range(8):
                    c = cg * 8 + cc
                    CA, CAr = CA2[cc % 2], f"d_CA{cc % 2}"
                    if samp:
                        xv = lambda k: XB[:, cc, :].rearrange("p (s t) -> p s t", t=11)[:, :, k:k + 8]
                        cav = CA[:, :].rearrange("p (s t) -> p s t", t=8)
                    else:
                        xv = lambda k: XB[:, cc, k:k + 128]
                        cav = CA[:, :]
                    P.I("dve", "tensor_scalar", cav, xv(0), cwT[:, c, 0:1], None, ALU.mult, reads=["d_XB", "d_cw"], writes=[CAr])
                    for k in range(1, 4):
                        P.I("dve", "scalar_tensor_tensor", cav, xv(k), cwT[:, c, k:k + 1], cav, ALU.mult, ALU.add,
                            reads=["d_XB", "d_cw", CAr], writes=[CAr])
                    if cg < 2:
                        P.I("act", "activation", XCf[:, cc, :], CA[:, :], AF.Silu, bias=cbT[:, c:c + 1], reads=[CAr, "d_cb"], writes=["d_XCf"])
                    else:
                        if cc < 4:
                            P.I("act", "activation", XCf[:, cc, :], CA[:, :], AF.Silu, bias=cbT[:, c:c + 1], reads=[CAr, "d_cb"], writes=["d_XCf"])
                            P.I("dve", "tensor_copy", BTb[:, cc, :], XCf[:, cc, :], reads=["d_XCf"], writes=["d_BTb"])
                        else:
                            P.I("act", "activation", CTb[:, cc - 4, :], CA[:, :], AF.Silu, bias=cbT[:, c:c + 1], reads=[CAr, "d_cb"], writes=["d_CTb"])
                if cg < 2:
                    for g4 in range(2):
                        b = g4
                        for c4 in range(4):
                            P.I("pe", "transpose", bk[b][:, c4 * 128:(c4 + 1) * 128], XCf[:, g4 * 4 + c4, :], self.ident[:],
                                reads=["d_XCf", "ident"], writes=[f"PS{b}"])
                        self.ev(Xb[:, cg * 1024 + g4 * 512:cg * 1024 + (g4 + 1) * 512], bk[b], reads=[f"PS{b}"], writes=["d_Xb"])
                else:
                    for c4 in range(4):
                        P.I("pe", "transpose", bk[0][:, c4 * 128:(c4 + 1) * 128], XCf[:, c4, :], self.ident[:],
                            reads=["d_XCf", "ident"], writes=["PS0"])
                    self.ev(Btb[:, :], bk[0], reads=["PS0"], writes=["d_Btb"])
            dtv = DT[:, i, :]
            P.I("dve", "tensor_tensor", DTA, dtv, A32, ALU.mult, reads=["d_DT", "d_hd"], writes=[S])
            P.I("pe", "matmul", bk[4][:, 0:32], lhsT=tri, rhs=DTA, start=True, stop=True, reads=[M, S], writes=["PS4"])
            P.I("pe", "matmul", bk[4][:, 32:64], lhsT=(BONES if samp else ONES), rhs=DTA, start=True, stop=True, reads=[M, S], writes=["PS4"])
            P.I("dve", "tensor_copy", sv[:, 1:3, :], bk[4][:, 0:64].rearrange("p (a n) -> p a n", a=2), reads=["PS4"], writes=[S])
            P.I("act", "activation", EA, ACS, AF.Exp, reads=[S], writes=[S])
            P.I("dve", "tensor_tensor", STMP, ALAST, ACS, ALU.subtract, reads=[S], writes=[S])
            P.I("act", "activation", DE, STMP, AF.Exp, reads=[S], writes=[S])
            P.I("dve", "tensor_tensor", WSC, dtv, DE, ALU.mult, reads=[S, "d_DT"], writes=[S])
            P.I("act", "activation", CDP, ALAST, AF.Exp, reads=[S], writes=[S])
            P.I("dve", "tensor_tensor", v3(Xw), v3(Xb), bc(WSC), ALU.mult, reads=["d_Xb", S], writes=["d_Xw"])
            for g in range(4):
                P.I("pe", "matmul", bk[5][:, g * 128:(g + 1) * 128], lhsT=BTb[:, g, :], rhs=CTb[:, g, :], start=True, stop=True,
                    reads=["d_BTb", "d_CTb"], writes=["PS5"])
            P.I("dve", "tensor_tensor", CBm[:], bk[5].rearrange("p (g n) -> p g n", g=4), tri.unsqueeze(1).to_broadcast([128, 4, 128]), ALU.mult,
                reads=["PS5", M], writes=["d_CBm"])

        def tile_back(i, samp, yoff_ap, yoff_regs):
            tri = TRIS if samp else TRI
            dtv = DT[:, i, :]
            P.I("dve", "tensor_tensor", v3(Y), v3(Xb), bc(D32), ALU.mult, reads=["d_Xb", "d_hd"], writes=["d_Y"])
            P.I("dve", "tensor_tensor", v3(zt), yoff_ap.rearrange("p (h e) -> p h e", e=64), bc(EA), ALU.mult,
                reads=list(yoff_regs) + [S], writes=["d_zt"])
            P.I("pool", "tensor_tensor", Y[:], Y[:], zt[:], ALU.add, reads=["d_Y", "d_zt"], writes=["d_Y"])
            for hh in range(32):
                g = hh // 8
                j = hh % 4
                if hh % 16 == 0:
                    P.I("dve", "tensor_tensor", R16[:], DTA[:, hh:hh + 16].unsqueeze(2).to_broadcast([128, 16, 128]),
                        tri.unsqueeze(1).to_broadcast([128, 16, 128]), ALU.mult, reads=[M, S], writes=["d_R16"])
                P.I("pe", "matmul", bk[4 + j][:, 0:128], lhsT=STRICT, rhs=R16[:, hh % 16, :], start=True, stop=True, reads=[M, "d_R16"], writes=[f"PS{4 + j}"])
                P.I("act", "activation", DEC[j][:], bk[4 + j][:, 0:128], AF.Exp, reads=[f"PS{4 + j}"], writes=[f"d_DEC{j}"])
                P.I("dve", "scalar_tensor_tensor", MTb[j][:], DEC[j][:], dtv[:, hh:hh + 1], CBm[:, g, :], ALU.mult, ALU.mult,
                    reads=[f"d_DEC{j}", "d_DT", "d_CBm"], writes=[f"d_MT{j}"])
                P.I("pe", "matmul", bk[g][:, (hh % 8) * 64:(hh % 8 + 1) * 64], lhsT=MTb[j][:], rhs=Xb[:, hh * 64:(hh + 1) * 64], start=True, stop=True,
                    reads=[f"d_MT{j}", "d_Xb"], writes=[f"PS{g}"])
            P.I("dve", "tensor_tensor", Y[:], Y[:], ps4(0), ALU.add, reads=["d_Y", "PS0", "PS1", "PS2", "PS3"], writes=["d_Y"])
            P.Dm("sp", "dma_start", out=zt[:], in_=scr_z[i * 128:(i + 1) * 128, :], reads=["scr_z"], writes=["d_zt"])
            P.I("act", "activation", zt[:], zt[:], AF.Silu, reads=["d_zt"], writes=["d_zt"])
            P.I("dve", "tensor_tensor", Y[:], Y[:], zt[:], ALU.mult, reads=["d_Y", "d_zt"], writes=["d_Y"])
            for g in range(4):
                P.I("act", "activation", zt[:, g * 512:(g + 1) * 512], Y[:, g * 512:(g + 1) * 512], AF.Square, accum_out=ss[:, g:g + 1],
                    reads=["d_Y", "d_zt", "d_ss"], writes=["d_zt", "d_ss"])
            P.I("act", "activation", ss[:, 4:8], ss[:, 0:4], AF.Sqrt, bias=self.eps_t[:, 0:1], scale=1.0 / 512.0, reads=["d_ss", "eps"], writes=["d_ss"])
            P.I("dve", "reciprocal", ss[:, 4:8], ss[:, 4:8], reads=["d_ss"], writes=["d_ss"])
            for g in range(4):
                P.I("dve", "tensor_scalar", Y[:, g * 512:(g + 1) * 512], Y[:, g * 512:(g + 1) * 512], ss[:, 4 + g:5 + g], None, ALU.mult,
                    reads=["d_Y", "d_ss"], writes=["d_Y"])
            self.transpose_f32(Y, 16, YT, "d_Y", "d_YT", banks=(0, 1))
            self.final_proj_ln(i, YT, "d_YT", w_out, "dw_out", 16)

        P.push_scope()
        h0 = P.sb([128, 16, 128], F32, "d_h0")
        h0T = [P.sb([128, 2048], BF16, f"d_h0T{j}") for j in range(2)]
        Bm = P.sb([128, 512], BF16, "d_Bm")
        Ej = P.sb([128, 128], F32, "d_Ej")
        cdb = P.sb([128, NSEQ, 32], F32, "d_cdb")
        cdc = P.sb([128, NSEQ, 16], F32, "d_cdc")
        tile_front(TP, True)
        for j in range(NSEQ):
            P.I("dve", "tensor_scalar", Ej[:], ONES, mcol[:, j:j + 1], None, ALU.mult, reads=[M, "d_mcol"], writes=["d_Ej"])
            P.I("pe", "matmul", bk[4][:, j * 32:(j + 1) * 32], lhsT=Ej[:], rhs=DTA, start=True, stop=True, reads=["d_Ej", S], writes=["PS4"])
        P.I("act", "activation", cdb[:].rearrange("p j h -> p (j h)"), bk[4], AF.Exp, reads=["PS4"], writes=["d_cdb"])
        cdb4 = cdb[:].rearrange("p j (r two) -> p j r two", two=2)
        P.I("dve", "tensor_scalar", cdc[:], cdb4[:, :, :, 0], mcol[:, 16:17], None, ALU.mult, reads=["d_cdb", "d_mcol"], writes=["d_cdc"])
        P.I("dve", "scalar_tensor_tensor", cdc[:], cdb4[:, :, :, 1], mcol[:, 17:18], cdc[:], ALU.mult, ALU.add,
            reads=["d_cdb", "d_mcol", "d_cdc"], writes=["d_cdc"])
        P.I("dve", "memset", zt[:], 0.0, writes=["d_zt"])
        for j in range(NSEQ):
            hT_j = h0T[j % 2]
            hreg = f"d_h0T{j % 2}"
            P.Dm("sp", "dma_start", out=h0[:], in_=d["state_ssd"][j].rearrange("(r p) n -> p r n", p=128), writes=["d_h0"])
            self.transpose_f32(h0[:].rearrange("p r n -> p (r n)"), 16, hT_j[:, :].rearrange("p (r m) -> p r m", m=128), "d_h0", hreg, banks=(0, 1))
            for g in range(4):
                b = 2 + g % 2
                P.I("pe", "matmul", bk[b], lhsT=CTb[:, g, :], rhs=hT_j[:, g * 512:(g + 1) * 512], start=True, stop=True,
                    reads=["d_CTb", hreg], writes=[f"PS{b}"])
                P.I("dve", "scalar_tensor_tensor", zt[:, g * 512:(g + 1) * 512], bk[b], mcol[:, j:j + 1], zt[:, g * 512:(g + 1) * 512], ALU.mult, ALU.add,
                    reads=[f"PS{b}", "d_mcol", "d_zt"], writes=["d_zt"])
            P.I("pool", "tensor_scalar", Bm[:], Btb[:], mcol[:, j:j + 1], None, ALU.mult, reads=["d_Btb", "d_mcol"], writes=["d_Bm"])
            for r in range(16):
                b = 4 + r // 4
                P.I("pe", "matmul", bk[b][:, (r % 4) * 128:(r % 4 + 1) * 128], lhsT=Xw[:, r * 128:(r + 1) * 128], rhs=Bm[:, (r // 4) * 128:(r // 4 + 1) * 128],
                    start=True, stop=True, reads=["d_Xw", "d_Bm"], writes=[f"PS{b}"])
            P.I("dve", "tensor_tensor", h0[:], h0[:], cdc[:, j, :].unsqueeze(2).to_broadcast([128, 16, 128]), ALU.mult,
                reads=["d_h0", "d_cdc"], writes=["d_h0"])
            P.I("dve", "tensor_tensor", h0[:].rearrange("p r n -> p (r n)"), h0[:].rearrange("p r n -> p (r n)"), ps4(4), ALU.add,
                reads=["d_h0", "PS4", "PS5", "PS6", "PS7"], writes=["d_h0"])
            P.Dm("sp", "dma_start", out=self.dout["ssd_s"][j * 2048:(j + 1) * 2048, :].rearrange("(r p) n -> p r n", p=128), in_=h0[:], reads=["d_h0"])
        P.I("dve", "tensor_copy", Y[:], zt[:], reads=["d_zt"], writes=["d_Y"])
        P.I("pool", "tensor_copy", h0[:].rearrange("p r n -> p (r n)"), Y[:], reads=["d_Y"], writes=["d_h0"])
        tile_back(TP, True, h0[:].rearrange("p r n -> p (r n)"), ["d_h0"])
        P.pop_scope()
        P.push_scope()
        hT = P.sb([128, 2048], F32, "d_hT")
        hTb = P.sb([128, 2048], BF16, "d_hTb")
        P.I("dve", "memset", hT[:], 0.0, writes=["d_hT"])
        P.I("dve", "memset", hTb[:], 0.0, writes=["d_hTb"])
        for i in range(TP):
            tile_front(i, False)
            for g in range(4):
                P.I("pe", "matmul", bk[g], lhsT=CTb[:, g, :], rhs=hTb[:, g * 512:(g + 1) * 512], start=True, stop=True,
                    reads=["d_CTb", "d_hTb"], writes=[f"PS{g}"])
            for g in range(4):
                P.I("pe", "matmul", bk[4 + g], lhsT=Btb[:, g * 128:(g + 1) * 128], rhs=Xw[:, g * 512:(g + 1) * 512], start=True, stop=True,
                    reads=["d_Btb", "d_Xw"], writes=[f"PS{4 + g}"])
            P.I("dve", "tensor_tensor", v3(hT), v3(hT), bc(CDP), ALU.mult, reads=["d_hT", S], writes=["d_hT"])
            P.I("dve", "tensor_tensor", hT[:], hT[:], ps4(4), ALU.add, reads=["d_hT", "PS4", "PS5", "PS6", "PS7"], writes=["d_hT"])
            tile_back(i, False, ps4(0), ["PS0", "PS1", "PS2", "PS3"])
            P.I("act", "copy", hTb[:], hT[:], reads=["d_hT"], writes=["d_hTb"])
        self.transpose_f32(hT, 16, Y[:, :].rearrange("p (r n) -> p r n", n=128), "d_hT", "d_Y", banks=(0, 1))
        P.Dm("sp", "dma_start", out=self.dout["ssd_p"].rearrange("(r p) n -> p r n", p=128), in_=Y[:, :].rearrange("p (r n) -> p r n", n=128),
             reads=["d_Y"])
        P.pop_scope()
        P.pop_scope()
        P.pop_scope()

    def build(self):
        P = self.P
        self.setup()
        self.eps_t = P.sb([128, 1], F32, "eps")
        P.I("dve", "memset", self.eps_t[:], LN_EPS, writes=["eps"])
        self.one_t = P.sb([128, 1], F32, "one")
        P.I("dve", "memset", self.one_t[:], 1.0, writes=["one"])
        if "ffn" in self.dbg:
            self.outp("dbg_ffn", [128, D])
            self.outp("dbg_idx", [128, 128], I32)
            self.outp("dbg_act", [128, 128]); self.outp("dbg_wgt", [128, 128]); self.outp("dbg_gate", [128, 128])
        self.load_ln_dummy = None
        if self.do_peer:
            stg0 = self.stg0 = [P.sb([128, D], F32, "cv0_s")]
            cb0 = self.cb0 = [P.sb([128, D], BF16, "cv0_b")]
            P.defer_begin("cv0")
            self.convert_tables(self.layers[0], stg0, cb0)
            P.defer_end()
        for l in self.layers:
            if self.do_mix:
                getattr(self, ("s5", "pool", "cmlp", "ssd")[l])(l)
            if self.do_peer:
                self.peer(l)
        self.finish()
        return self.nc


def make_in_map(inp, c, names, TP=16):
    f = np.ascontiguousarray
    sl = slice(c * NSEQ, (c + 1) * NSEQ)
    m = {}
    for n in names:
        if n == "x_p":
            m[n] = f(inp["x_prompt"][c, :TP * 128])
        elif n == "x_s":
            m[n] = f(inp["x_sample"][sl].reshape(128, D))
        elif n == "peer_w_q":
            m[n] = f(inp["peer_w_q"])
        elif n == "peer_keysT":
            m[n] = f(inp["peer_keys"].transpose(0, 4, 1, 2, 3).reshape(4, 128, 2048))
        elif n in ("peer_u", "peer_v"):
            m[n] = f(inp[n].reshape(4 * NEXP, D))
        elif n == "s5_aT":
            def lay(a):
                return a.reshape(32, 2, 64).transpose(1, 2, 0).reshape(128, 32)
            ld = np.broadcast_to(inp["s5_log_dt"][:, None], (64, 64))
            m[n] = f(np.stack([lay(inp["s5_a_re"]), lay(inp["s5_a_im"]), lay(ld)], axis=1))
        elif n == "s5_bT":
            def layb(b):
                return b.reshape(32, 2, 64, 16).transpose(1, 2, 0, 3).reshape(128, 32, 16)
            m[n] = f(np.stack([layb(inp["s5_b_re"]), layb(inp["s5_b_im"])], axis=1))
        elif n == "s5_cT":
            def layc(cc):
                return cc.reshape(32, 2, 16, 64).transpose(1, 3, 0, 2).reshape(128, 32, 16)
            m[n] = f(np.stack([layc(inp["s5_c_re"]), layc(inp["s5_c_im"])], axis=1))
        elif n == "s5_dT":
            m[n] = f(inp["s5_d"].reshape(8, 128).T)
        elif n == "s5_b_gluT":
            m[n] = f(inp["s5_b_glu"].reshape(8, 128).T)
        elif n in ("state_s5_re", "state_s5_im"):
            m[n] = f(inp[n][sl].reshape(NSEQ, 4096))
        elif n == "ssd_hd":
            m[n] = f(np.concatenate([inp["ssd_dt_bias"], inp["ssd_a_log"], inp["ssd_d"]]).reshape(1, 96))
        elif n == "ssd_conv_wT":
            m[n] = f(inp["ssd_conv_w"].reshape(4, 24, 128).transpose(2, 1, 0))
        elif n == "ssd_conv_bT":
            m[n] = f(inp["ssd_conv_b"].reshape(24, 128).T)
        elif n == "ssd_norm_gT":
            m[n] = f(inp["ssd_norm_g"].reshape(16, 128).T)
        elif n == "state_ssd_conv":
            m[n] = f(inp[n][sl].reshape(NSEQ * 3, 3072))
        elif n == "state_ssd":
            m[n] = f(inp[n][sl].reshape(NSEQ, 2048, 128))
        elif n == "pool_scaleT":
            m[n] = f(inp["pool_scale"].reshape(8, 128).T)
        elif n == "state_pool":
            m[n] = f(inp["state_pool"][sl].reshape(NSEQ * 15, D))
        elif n in ("cmlp_b_in", "cmlp_ln_g", "cmlp_ln_b"):
            m[n] = f(inp[n].reshape(1, -1))
        elif n == "cmlp_w_sT":
            m[n] = f(inp["cmlp_w_s"].transpose(0, 2, 1))
        elif n == "cmlp_b_sT":
            m[n] = f(inp["cmlp_b_s"].T)
        else:
            m[n] = f(inp[n])
    return m


_CACHE = {}


def kernel(**inputs):
    inp = {k: np.asarray(v) for k, v in inputs.items()}
    ncores = 8
    if "k" not in _CACHE:
        k = K(TP=16, layers=(0, 1, 2, 3))
        k.build()
        _CACHE["k"] = k
    k = _CACHE["k"]
    names = list(k.din)
    in_maps = [make_in_map(inp, c, names, TP=16) for c in range(ncores)]
    res = run_bass_kernel_spmd(k.nc, in_maps, core_ids=list(range(ncores)))
    R = res.results

    def cat(name, shp):
        return np.ascontiguousarray(np.stack([np.asarray(R[c][name]).reshape(shp) for c in range(ncores)]))
    y_prompt = cat("y_p", (2048, D))
    y_sample = cat("y_s", (NSEQ, DSEQ, D)).reshape(128, DSEQ, D)
    s5_re_p = cat("s5_re_p", (64, 64))
    s5_im_p = cat("s5_im_p", (64, 64))
    pool_p = cat("pool_p", (15, D))
    conv_p = cat("conv_p", (3, 3072))
    ssd_p = cat("ssd_p", (32, 64, 128))
    s5_re_s = cat("s5_re_s", (NSEQ, 64, 64)).reshape(128, 64, 64)
    s5_im_s = cat("s5_im_s", (NSEQ, 64, 64)).reshape(128, 64, 64)
    pool_s = cat("pool_s", (NSEQ, 15, D)).reshape(128, 15, D)
    cmlp_v_s = cat("cmlp_v_s", (NSEQ, DSEQ, D)).reshape(128, DSEQ, D)
    conv_s = cat("conv_s", (NSEQ, 3, 3072)).reshape(128, 3, 3072)
    ssd_s = cat("ssd_s", (NSEQ, 32, 64, 128)).reshape(128, 32, 64, 128)
    outs = (y_prompt, y_sample, s5_re_p, s5_im_p, pool_p, conv_p, ssd_p,
            s5_re_s, s5_im_s, pool_s, cmlp_v_s, conv_s, ssd_s)
    return tuple(np.asarray(o, dtype=np.float32) for o in outs)
```
